# Optimizing a Trainium2 kernel written in Bass

```python
import jax
import jax.numpy as jnp
from jax import lax
import numpy as np

D_MODEL = 1024
BATCH = 8
SEQ = 2048
DEPTH = 4

GRID_W = 64
CTX_LEN = 256
NORM_EPS = 1e-6
N_MOD = 6

POOL_WINDOWS = (2, 4, 8, 16)
POOL_GROUP = 64
POOL_WIDTH = POOL_GROUP * len(POOL_WINDOWS)

N_HEADS = 8
N_KV_HEADS = 2
HEAD_DIM = 64
Q_PER_KV = N_HEADS // N_KV_HEADS
ATTN_WIDTH = N_HEADS * HEAD_DIM
KV_WIDTH = N_KV_HEADS * HEAD_DIM
Q_BLOCK = 128
ROPE_THETA = 10000.0

HG_HEADS = 4
HG_DK = 64
HG_DV = 64
HG_WIDTH = HG_HEADS * HG_DK
HG_CHUNK = 16

D_FF = ((8 * D_MODEL + 3 * 256 - 1) // (3 * 256)) * 256

IN_SIZES = (POOL_WIDTH,
            ATTN_WIDTH, KV_WIDTH, KV_WIDTH,
            HG_WIDTH, HG_WIDTH,
            HG_WIDTH, HG_WIDTH,
            HG_WIDTH,
            D_MODEL, D_MODEL, D_MODEL)
IN_WIDTH = sum(IN_SIZES)
IN_SPLITS = tuple(int(s) for s in np.cumsum(IN_SIZES)[:-1])

kernel_name = "hybrid_pool_gqa_hgrn2_dit_trunk"


def rmsnorm(x, w):
    xf = x.astype(jnp.float32)
    y = xf * lax.rsqrt(jnp.mean(xf * xf, axis=-1, keepdims=True) + NORM_EPS)
    return (y * w.astype(jnp.float32)).astype(x.dtype)


def _rope_axis(x, pos):
    n = x.shape[-1]
    half = n // 2
    freqs = ROPE_THETA ** (-jnp.arange(half, dtype=jnp.float32) * (2.0 / n))
    ang = pos.astype(jnp.float32)[:, None] * freqs[None, :]
    cos = jnp.cos(ang)[None, :, None, :]
    sin = jnp.sin(ang)[None, :, None, :]
    xf = x.astype(jnp.float32)
    x1, x2 = xf[..., :half], xf[..., half:]
    return jnp.concatenate([x1 * cos - x2 * sin, x2 * cos + x1 * sin], axis=-1).astype(x.dtype)


def rope_2d(x, row, col):
    h = x.shape[-1] // 2
    return jnp.concatenate([_rope_axis(x[..., :h], row), _rope_axis(x[..., h:], col)], axis=-1)


def pool_mix(u, w_pool, s_pool):
    B, L, _ = u.shape
    uf = u.astype(jnp.float32)
    cs = jnp.concatenate([jnp.zeros((B, 1, POOL_WIDTH), jnp.float32), jnp.cumsum(uf, axis=1)], axis=1)
    t = jnp.arange(L)
    outs = []
    for gi, w in enumerate(POOL_WINDOWS):
        lo = jnp.clip(t - w // 2, 0, L)
        hi = jnp.clip(t + w - w // 2, 0, L)
        csg = cs[..., gi * POOL_GROUP:(gi + 1) * POOL_GROUP]
        cnt = (hi - lo).astype(jnp.float32)[None, :, None]
        mean = (csg[:, hi] - csg[:, lo]) / cnt
        outs.append(mean - uf[..., gi * POOL_GROUP:(gi + 1) * POOL_GROUP])
    y = jnp.stack(outs, axis=2).astype(u.dtype)
    y = jnp.einsum('blgc,gcd->blgd', y, w_pool).reshape(B, L, POOL_WIDTH)
    return y * s_pool


def _attend(q, k, v):
    s = jnp.einsum('bqhgd,bshd->bhgqs', q, k).astype(jnp.float32) * (HEAD_DIM ** -0.5)
    p = jax.nn.softmax(s, axis=-1).astype(v.dtype)
    return jnp.einsum('bhgqs,bshd->bqhgd', p, v)


def latent_attention(q, k, v, k_ctx, v_ctx):
    B, L = q.shape[:2]
    k_all = jnp.concatenate([k, k_ctx], axis=1)
    v_all = jnp.concatenate([v, v_ctx], axis=1)
    nb = L // Q_BLOCK
    qb = jnp.moveaxis(q.reshape(B, nb, Q_BLOCK, N_KV_HEADS, Q_PER_KV, HEAD_DIM), 1, 0)
    ob = lax.map(lambda blk: _attend(blk, k_all, v_all), qb)
    return jnp.moveaxis(ob, 0, 1).reshape(B, L, ATTN_WIDTH)


def _hgrn_gates(z, lb):
    z = z.astype(jnp.float32)
    log_f = jnp.logaddexp(jnp.log(lb), jnp.log1p(-lb) + jax.nn.log_sigmoid(z))
    one_minus_f = (1.0 - lb) * jax.nn.sigmoid(-z)
    return log_f, one_minus_f


def hgrn_scan(q, k, v, log_f, s0):
    B, L, H, K = q.shape
    V = v.shape[-1]
    C = HG_CHUNK
    N = L // C
    q = q.reshape(B, N, C, H, K)
    k = k.reshape(B, N, C, H, K)
    v = v.reshape(B, N, C, H, V)
    G = jnp.cumsum(log_f.reshape(B, N, C, H, K), axis=2)
    G_last = G[:, :, -1]
    mask = jnp.tril(jnp.ones((C, C), dtype=bool))[None, None, :, :, None, None]
    diff = G[:, :, :, None] - G[:, :, None, :]
    decay = jnp.exp(jnp.where(mask, diff, -jnp.inf))
    A = jnp.einsum('bnthk,bnshk,bntshk->bnhts', q, k, decay)
    o_intra = jnp.einsum('bnhts,bnshv->bnthv', A, v)
    k_dec = k * jnp.exp(G_last[:, :, None] - G)
    kv = jnp.einsum('bnchk,bnchv->bnhkv', k_dec, v)
    a_chunk = jnp.exp(G_last)

    def step(S, inp):
        a, u = inp
        return a[..., None] * S + u, S

    s_fin, s_start = lax.scan(step, s0, (jnp.moveaxis(a_chunk, 1, 0), jnp.moveaxis(kv, 1, 0)))
    s_start = jnp.moveaxis(s_start, 0, 1)
    o_inter = jnp.einsum('bnchk,bnhkv->bnchv', q * jnp.exp(G), s_start)
    return (o_intra + o_inter).reshape(B, L, H, V), s_fin


def hgrn_bidir(P, s0_f, s0_b):
    o_f, s_f = hgrn_scan(P['hq'], P['k_f'], P['hv'], P['logf_f'], s0_f)
    rev = lambda a: a[:, ::-1]
    o_b, s_b = hgrn_scan(rev(P['hq']), rev(P['k_b']), rev(P['hv']), rev(P['logf_b']), s0_b)
    return o_f + rev(o_b), s_f, s_b


def hgrn_readout(o, gate, norm_w):
    B, L = o.shape[:2]
    return rmsnorm(o, norm_w).reshape(B, L, HG_WIDTH) * jax.nn.silu(gate)


def in_projection(h, w_in_l, q_norm_l, k_norm_l, lb_l, row=None, col=None):
    B, L, _ = h.shape
    (u_pool, q, k, v, hq, hi, zf_f, zf_b, hgate, g_pool, g_attn, g_hg) = jnp.split(h @ w_in_l, IN_SPLITS, axis=-1)
    q = rmsnorm(q.reshape(B, L, N_HEADS, HEAD_DIM), q_norm_l)
    k = rmsnorm(k.reshape(B, L, N_KV_HEADS, HEAD_DIM), k_norm_l)
    if row is not None:
        q = rope_2d(q, row, col)
        k = rope_2d(k, row, col)
    hs = lambda a: a.reshape(B, L, HG_HEADS, HG_DK)
    logf_f, k_f = _hgrn_gates(hs(zf_f), lb_l[0].reshape(HG_HEADS, HG_DK))
    logf_b, k_b = _hgrn_gates(hs(zf_b), lb_l[1].reshape(HG_HEADS, HG_DK))
    return {
        'pool': u_pool,
        'q': q.reshape(B, L, N_KV_HEADS, Q_PER_KV, HEAD_DIM),
        'k': k,
        'v': v.reshape(B, L, N_KV_HEADS, HEAD_DIM),
        'hq': hs(hq).astype(jnp.float32),
        'hv': hi.reshape(B, L, HG_HEADS, HG_DV),
        'logf_f': logf_f, 'k_f': k_f, 'logf_b': logf_b, 'k_b': k_b,
        'hgate': hgate, 'g_pool': g_pool, 'g_attn': g_attn, 'g_hg': g_hg,
    }


def merge_branches(P, pool_out, attn_out, hg_out, wbp, wba, wbh, wo):
    y = (jax.nn.sigmoid(P['g_pool']) * (pool_out @ wbp)
         + jax.nn.sigmoid(P['g_attn']) * (attn_out @ wba)
         + jax.nn.sigmoid(P['g_hg']) * (hg_out @ wbh))
    return y @ wo


def swiglu(h, w1, w2):
    a, b = jnp.split(h @ w1, 2, axis=-1)
    return (jax.nn.silu(a) * b) @ w2


def setup_inputs(seed: int = 0) -> dict:
    key = jax.random.key(seed)
    ks = jax.random.split(key, 24)
    f32 = jnp.float32
    nrm = lambda k, shape, s: jax.random.normal(k, shape, f32) * s
    return {
        'x': nrm(ks[0], (BATCH, SEQ, D_MODEL), 1.0),
        'c': nrm(ks[1], (BATCH, D_MODEL), 1.0),
        'ctx': nrm(ks[2], (BATCH, CTX_LEN, D_MODEL), 1.0),
        'c_ctx': nrm(ks[3], (D_MODEL,), 1.0),
        'w_ada': nrm(ks[4], (DEPTH, D_MODEL, N_MOD * D_MODEL), 0.5 * D_MODEL ** -0.5),
        'b_ada': nrm(ks[5], (DEPTH, N_MOD * D_MODEL), 0.01),
        'norm1_w': 1.0 + nrm(ks[6], (DEPTH, D_MODEL), 0.02),
        'w_in': nrm(ks[7], (DEPTH, D_MODEL, IN_WIDTH), D_MODEL ** -0.5),
        'pool_w': nrm(ks[8], (DEPTH, len(POOL_WINDOWS), POOL_GROUP, POOL_GROUP), POOL_GROUP ** -0.5),
        'pool_scale': 1.0 + nrm(ks[9], (DEPTH, POOL_WIDTH), 0.1),
        'q_norm_w': 1.0 + nrm(ks[10], (DEPTH, HEAD_DIM), 0.02),
        'k_norm_w': 1.0 + nrm(ks[11], (DEPTH, HEAD_DIM), 0.02),
        'hg_lb_logits': nrm(ks[12], (DEPTH, 2, HG_WIDTH), 0.5),
        'hg_norm_w': 1.0 + nrm(ks[13], (DEPTH, HG_DV), 0.02),
        'w_branch_pool': nrm(ks[14], (DEPTH, POOL_WIDTH, D_MODEL), POOL_WIDTH ** -0.5),
        'w_branch_attn': nrm(ks[15], (DEPTH, ATTN_WIDTH, D_MODEL), ATTN_WIDTH ** -0.5),
        'w_branch_hg': nrm(ks[16], (DEPTH, HG_WIDTH, D_MODEL), HG_WIDTH ** -0.5),
        'w_out': nrm(ks[17], (DEPTH, D_MODEL, D_MODEL), D_MODEL ** -0.5),
        'norm2_w': 1.0 + nrm(ks[18], (DEPTH, D_MODEL), 0.02),
        'w_ffn_in': nrm(ks[19], (DEPTH, D_MODEL, 2 * D_FF), D_MODEL ** -0.5),
        'w_ffn_out': nrm(ks[20], (DEPTH, D_FF, D_MODEL), D_FF ** -0.5),
    }


def reference(x, c, ctx, c_ctx, w_ada, b_ada, norm1_w, w_in, pool_w, pool_scale,
              q_norm_w, k_norm_w, hg_lb_logits, hg_norm_w, w_branch_pool, w_branch_attn,
              w_branch_hg, w_out, norm2_w, w_ffn_in, w_ffn_out):
    B, L, _ = x.shape
    ROWS = L // GRID_W
    row = jnp.repeat(jnp.arange(ROWS, dtype=jnp.int32), GRID_W)
    col = jnp.tile(jnp.arange(GRID_W, dtype=jnp.int32), ROWS)
    lb_cum = jnp.cumsum(jax.nn.softmax(hg_lb_logits.astype(jnp.float32), axis=0), axis=0)
    lb_all = lb_cum - lb_cum[0]
    c_act = jax.nn.silu(c)
    cc_act = jax.nn.silu(c_ctx)
    xc = ctx
    s_zero = jnp.zeros((B, HG_HEADS, HG_DK, HG_DV), jnp.float32)
    for l in range(DEPTH):
        sh1, sc1, g1, sh2, sc2, g2 = jnp.split((c_act @ w_ada[l] + b_ada[l])[:, None, :], N_MOD, axis=-1)
        sh1c, sc1c, g1c, sh2c, sc2c, g2c = jnp.split(cc_act @ w_ada[l] + b_ada[l], N_MOD, axis=-1)

        h = rmsnorm(x, norm1_w[l]) * (1.0 + sc1) + sh1
        hc = rmsnorm(xc, norm1_w[l]) * (1.0 + sc1c) + sh1c
        P = in_projection(h, w_in[l], q_norm_w[l], k_norm_w[l], lb_all[l], row, col)
        Pc = in_projection(hc, w_in[l], q_norm_w[l], k_norm_w[l], lb_all[l])

        o_hc, s_f, s_b = hgrn_bidir(Pc, s_zero, s_zero)
        o_hl, _, _ = hgrn_bidir(P, s_f, s_b)

        y_lat = merge_branches(
            P,
            pool_mix(P['pool'], pool_w[l], pool_scale[l]),
            latent_attention(P['q'], P['k'], P['v'], Pc['k'], Pc['v']),
            hgrn_readout(o_hl, P['hgate'], hg_norm_w[l]),
            w_branch_pool[l], w_branch_attn[l], w_branch_hg[l], w_out[l])
        x_new = x + g1 * y_lat

        if l < DEPTH - 1:
            Lc = xc.shape[1]
            y_ctx = merge_branches(
                Pc,
                pool_mix(Pc['pool'], pool_w[l], pool_scale[l]),
                _attend(Pc['q'], Pc['k'], Pc['v']).reshape(B, Lc, ATTN_WIDTH),
                hgrn_readout(o_hc, Pc['hgate'], hg_norm_w[l]),
                w_branch_pool[l], w_branch_attn[l], w_branch_hg[l], w_out[l])
            xc = xc + g1c * y_ctx
        x = x_new

        h2 = rmsnorm(x, norm2_w[l]) * (1.0 + sc2) + sh2
        x = x + g2 * swiglu(h2, w_ffn_in[l], w_ffn_out[l])
        if l < DEPTH - 1:
            h2c = rmsnorm(xc, norm2_w[l]) * (1.0 + sc2c) + sh2c
            xc = xc + g2c * swiglu(h2c, w_ffn_in[l], w_ffn_out[l])
    return x
```

```python
import math
import numpy as np
import concourse.bass as bass
import concourse.mybir as mybir
from concourse.bass_utils import run_bass_kernel_spmd

F32 = mybir.dt.float32; BF16 = mybir.dt.bfloat16; I32 = mybir.dt.int32
AF = mybir.ActivationFunctionType; ALU = mybir.AluOpType
AX = mybir.AxisListType

D = 1024; KC = 8; L = 2048; CT = 256; T = L + CT; DEPTH = 4
NT = T // 128
BLOCKS = [(0, 512), (512, 512), (1024, 512), (1536, 512), (2048, 256)]
IN_W = 5376; DFF = 2816; FC = DFF // 128
EPS = 1e-6
UW = 2336


def ucol(t):
    return t + 8 if t < L else t + 24


class Sched:
    EPOCH = 30000

    def __init__(self, nc, same_sync=True, ndma=12):
        self.nc = nc
        self.E = {'pe': nc.tensor, 'act': nc.scalar, 'dve': nc.vector, 'pool': nc.gpsimd, 'sp': nc.sync}
        self.cnt = {e: 0 for e in self.E}
        self.esem = {e: [] for e in self.E}
        self.waited = {e: {} for e in self.E}
        self.res = {}
        self.dsems = []; self.dval = []
        self.dpool = {}; self.drr = {}
        for q in ('sp', 'pool'):
            self.dpool[q] = []
            for i in range(ndma):
                self.dpool[q].append(len(self.dsems))
                self.dsems.append(nc.alloc_semaphore(f"d_{q}_{i}")); self.dval.append(0)
            self.drr[q] = 0
        self.same_sync = same_sync

    def _wait(self, e, tok, force=False):
        if tok is None: return
        if tok[0] == 'e':
            _, x, ep, v = tok
            if x == e and (e == 'pe' or not (self.same_sync or force)): return
            cur = self.waited[e].get(('e', x), (-1, 0))
            if (ep, v) <= cur: return
            self.waited[e][('e', x)] = (ep, v)
            self.E[e].wait_ge(self.esem[x][ep], v)
        else:
            _, i, v = tok
            if v == 0 or self.waited[e].get(('d', i), 0) >= v: return
            self.waited[e][('d', i)] = v
            self.E[e].wait_ge(self.dsems[i], v)

    @staticmethod
    def _split(key):
        return key if isinstance(key, tuple) else (key, None)

    def _ents(self, key):
        name, sub = self._split(key)
        d = self.res.get(name, {})
        if sub is None: return list(d.values())
        return [d[k] for k in (sub, None) if k in d]

    def _deps(self, r, w):
        deps = []
        for k in r:
            for ent in self._ents(k):
                if ent['w'] is not None: deps.append(ent['w'])
        for k in w:
            for ent in self._ents(k):
                if ent['w'] is not None: deps.append(ent['w'])
                deps.extend(ent['r'].values())
        return deps

    def _record(self, tok, r, w):
        rk = (tok[0], tok[1])
        for k in r:
            name, sub = self._split(k)
            ent = self.res.setdefault(name, {}).setdefault(sub, {'w': None, 'r': {}})
            ent['r'][rk] = tok
        for k in w:
            name, sub = self._split(k)
            if sub is None: self.res[name] = {None: {'w': tok, 'r': {}}}
            else: self.res.setdefault(name, {})[sub] = {'w': tok, 'r': {}}

    def op(self, e, fn, r=(), w=()):
        for tok in self._deps(r, w): self._wait(e, tok)
        inst = fn(self.E[e])
        k = self.cnt[e]; self.cnt[e] += 1
        ep, v = divmod(k, self.EPOCH); v += 1
        while len(self.esem[e]) <= ep:
            self.esem[e].append(self.nc.alloc_semaphore(f"s_{e}_{len(self.esem[e])}"))
        inst.then_inc(self.esem[e][ep], 1)
        tok = ('e', e, ep, v)
        self._record(tok, r, w)
        return tok

    def dma(self, q, out, in_, r=(), w=(), **kw):
        pool = self.dpool[q]; i = pool[self.drr[q] % len(pool)]; self.drr[q] += 1
        self._wait(q, ('d', i, self.dval[i]))
        for tok in self._deps(r, w): self._wait(q, tok)
        inst = self.E[q].dma_start(out=out, in_=in_, **kw)
        self.dval[i] += 16
        inst.then_inc(self.dsems[i], 16)
        tok = ('d', i, self.dval[i])
        self._record(tok, r, w)
        return tok

    def last_tok(self, e):
        k = self.cnt[e] - 1
        if k < 0: return None
        ep, v = divmod(k, self.EPOCH)
        return ('e', e, ep, v + 1)

    def barrier(self, pool=False):
        toks = [self.last_tok(e) for e in ('pe', 'act', 'dve', 'pool')]
        for i in self.dpool['sp']: toks.append(('d', i, self.dval[i]))
        for e in ('pe', 'act', 'dve', 'sp') + (('pool',) if pool else ()):
            for tok in toks: self._wait(e, tok, force=True)

    def final(self, e='sp'):
        toks = [self.last_tok(x) for x in ('pe', 'act', 'dve', 'pool')]
        toks += [('d', i, self.dval[i]) for i in range(len(self.dsems))]
        for tok in toks: self._wait(e, tok, force=True)


class Scope:
    uid = 0

    def __init__(self, nc):
        from contextlib import ExitStack
        self.nc = nc; self.es = ExitStack()

    def sb(self, name, shape, dtp):
        Scope.uid += 1
        return self.es.enter_context(self.nc.sbuf_tensor(f"{name}_{Scope.uid}", list(shape), dtp))

    def close(self):
        self.es.close()


class Rot:
    def __init__(self, items):
        self.items = items; self.i = 0

    def get(self):
        it = self.items[self.i % len(self.items)]; self.i += 1
        return it


def build(depth=DEPTH, taps=(), stop_after=None):
    nc = bass.Bass("TRN2", target_bir_lowering=False)
    S = Sched(nc)
    dram = lambda name, shape, dtp, kind="Internal": nc.dram_tensor(name, list(shape), dtp, kind=kind).ap()
    x_d = dram("x", [L, D], F32, "ExternalInput")
    ctx_d = dram("ctx", [CT, D], F32, "ExternalInput")
    c_d = dram("c", [8, 128], F32, "ExternalInput")
    cc_d = dram("c_ctx", [8, 128], F32, "ExternalInput")
    w_ada = dram("w_ada", [DEPTH, D, 6 * D], F32, "ExternalInput")
    b_ada = dram("b_ada", [DEPTH, 6 * D], F32, "ExternalInput")
    norm1_w = dram("norm1_w", [DEPTH, D], F32, "ExternalInput")
    w_in = dram("w_in", [DEPTH, D, IN_W], F32, "ExternalInput")
    pool_w = dram("pool_w", [DEPTH, 4, 64, 64], F32, "ExternalInput")
    pool_scale = dram("pool_scale", [DEPTH, 256], F32, "ExternalInput")
    q_norm_w = dram("q_norm_w", [DEPTH, 64], F32, "ExternalInput")
    k_norm_w = dram("k_norm_w", [DEPTH, 64], F32, "ExternalInput")
    hg_lb = dram("hg_lb_logits", [DEPTH, 2, 256], F32, "ExternalInput")
    hg_norm_w = dram("hg_norm_w", [DEPTH, 64], F32, "ExternalInput")
    w_bp = dram("w_branch_pool", [DEPTH, 256, D], F32, "ExternalInput")
    w_ba = dram("w_branch_attn", [DEPTH, 512, D], F32, "ExternalInput")
    w_bh = dram("w_branch_hg", [DEPTH, 256, D], F32, "ExternalInput")
    w_out = dram("w_out", [DEPTH, D, D], F32, "ExternalInput")
    norm2_w = dram("norm2_w", [DEPTH, D], F32, "ExternalInput")
    w_f1 = dram("w_ffn_in", [DEPTH, D, 2 * DFF], F32, "ExternalInput")
    w_f2 = dram("w_ffn_out", [DEPTH, DFF, D], F32, "ExternalInput")
    out_d = dram("out", [L, D], F32, "ExternalOutput")
    tap_d = {}
    XT = dram("XT", [8, 128, T], F32)
    U = dram("U", [2, 128, UW], F32)
    HQ = dram("HQ", [2, 128, T], F32)
    Z = dram("Z", [4, 128, T], F32)
    HGATE = dram("HGATE", [2, 128, T], BF16)
    VH = dram("VH", [T, 256], BF16)
    GT = dram("GT", [24, 128, T], BF16)
    BR = dram("BR", [8, 128, T], BF16)
    ACTS = dram("ACTS", [FC, 128, T], BF16)

    def sb(name, shape, dtp):
        return nc.alloc_sbuf_tensor(name, list(shape), dtp)

    Wb = [sb(f"W{i}", [128, 4096], BF16) for i in range(3)]
    Wrot = Rot([0, 1, 2])
    hT = sb("hT", [128, KC, T], BF16)
    ident_f = sb("ident_f", [128, 128], F32); ident_b = sb("ident_b", [128, 128], BF16)
    ones_b = sb("ones_b", [128, 128], BF16); blk_b = sb("blk_b", [128, 128], BF16)
    onesf = sb("onesf", [128, 128], F32); Rt = sb("Rt", [128, 128], BF16)
    cosT = sb("cosT", [128, L], F32); sinT = sb("sinT", [128, L], F32)
    maskF = sb("maskF", [128, 128], BF16); maskB = sb("maskB", [128, 128], BF16)
    Emask = sb("Emask", [128, 8], BF16); Erev = sb("Erev", [128, 8], BF16)
    m01 = sb("m01", [128, T], BF16)
    cst = sb("cst", [128, 128], F32); bada = sb("bada", [128, 192], F32)
    lbT = sb("lbT", [128, 16], F32); omlb = sb("omlb", [128, 16], F32); nomlb = sb("nomlb", [128, 16], F32)
    cT = sb("cT", [128, 16], BF16)
    modTs = [sb(f"modT{i}", [128, 48, 2], F32) for i in range(2)]
    a1s = [sb(f"a1_{i}", [128, 8, 2], F32) for i in range(2)]; a2s = [sb(f"a2_{i}", [128, 8, 2], F32) for i in range(2)]
    CUR = {'l': 0}
    rcE = sb("rcE", [128, 2, 16], F32)
    pwbd = sb("pwbd", [128, 2, 128], BF16)
    psb = [nc.alloc_psum_tensor(f"ps{i}", [128, 512], F32) for i in range(8)]
    stf = [sb(f"stf{i}", [128, 512], F32) for i in range(2)]; STF = Rot([0, 1])
    stb = [sb(f"stb{i}", [128, 512], BF16) for i in range(3)]; STB = Rot([0, 1, 2])

    V = lambda e: e

    def act(out, in_, func, r, w, **kw):
        return S.op('act', lambda e: e.activation(out=out, in_=in_, func=func, **kw), r, w)

    def tt(out, in0, in1, op, r, w, eng='dve'):
        return S.op(eng, lambda e: e.tensor_tensor(out=out, in0=in0, in1=in1, op=op), r, w)

    def ts(out, in0, s1, s2, op0, op1, r, w, eng='dve'):
        if op1 is None:
            return S.op(eng, lambda e: e.tensor_scalar(out=out, in0=in0, scalar1=s1, scalar2=None, op0=op0), r, w)
        return S.op(eng, lambda e: e.tensor_scalar(out=out, in0=in0, scalar1=s1, scalar2=s2, op0=op0, op1=op1), r, w)

    def stt(out, in0, scalar, in1, op0, op1, r, w):
        return S.op('dve', lambda e: e.scalar_tensor_tensor(out=out, in0=in0, scalar=scalar, in1=in1, op0=op0, op1=op1), r, w)

    def cp(out, in_, r, w, eng='dve'):
        return S.op(eng, lambda e: e.tensor_copy(out=out, in_=in_), r, w)

    def mm(out, lhsT, rhs, start, stop, r, w):
        return S.op('pe', lambda e: e.matmul(out, lhsT=lhsT, rhs=rhs, start=start, stop=stop), r, w)

    def tr(out, in_, ident, r, w):
        return S.op('pe', lambda e: e.transpose(out, in_, ident), r, w)

    def memset(ap, val, w, eng='dve'):
        return S.op(eng, lambda e: e.memset(ap, val), (), w)

    def load_w(src, shape_free, ncols_total):
        i = Wrot.get()
        a, b = shape_free
        dst = Wb[i][:, 0:a * b].rearrange("p (a b) -> p a b", a=a)
        S.dma('pool', dst, src, w=[f"W{i}"])
        return dst, f"W{i}"

    sc0 = Scope(nc); sb0 = sb; sb = sc0.sb
    xs = [sb(f"xs{i}", [128, D], F32) for i in range(2)]
    xs2 = [sb(f"xt2_{i}", [128, KC, 128], F32) for i in range(2)]
    iot = sb("iot", [128, 128], I32); pI = sb("pI", [128, 1], I32); cI = sb("cI", [128, 128], I32)
    pc = sb("pc", [128, 1], I32); ccI = sb("ccI", [128, 128], I32)
    tmpA = sb("tmpA", [128, 128], F32); tmpB = sb("tmpB", [128, 128], F32)
    S.op('pool', lambda e: e.iota(iot[:], pattern=[[1, 128]], base=0, channel_multiplier=-1), w=['iot'])
    S.op('pool', lambda e: e.iota(pI[:], pattern=[[0, 1]], base=0, channel_multiplier=1), w=['pI'])
    S.op('pool', lambda e: e.iota(cI[:], pattern=[[1, 128]], base=0, channel_multiplier=0), w=['cI'])
    ts(ident_f[:], iot[:], 0.0, None, ALU.is_equal, None, ['iot'], ['ident_f'])
    cp(ident_b[:], ident_f[:], ['ident_f'], ['ident_b'])
    memset(ones_b[:], 1.0, ['ones_b']); memset(onesf[:], 1.0, ['onesf'])
    memset(blk_b[:], 0.0, ['blk_b'])
    memset(blk_b[0:64, 0:64], 1.0, ['blk_b']); memset(blk_b[64:128, 64:128], 1.0, ['blk_b'])
    ts(tmpA[:], iot[:], 16.0, None, ALU.is_equal, None, ['iot'], ['tmpA'])
    ts(tmpB[:], iot[:], -16.0, None, ALU.is_equal, None, ['iot'], ['tmpB'])
    for b in range(4):
        memset(tmpA[:, 32 * b:32 * b + 16], 0.0, ['tmpA'])
        memset(tmpB[:, 32 * b + 16:32 * b + 32], 0.0, ['tmpB'])
    tt(Rt[:], tmpA[:], tmpB[:], ALU.subtract, ['tmpA', 'tmpB'], ['Rt'])
    ts(pc[:], pI[:], 4, None, ALU.arith_shift_right, None, ['pI'], ['pc'])
    ts(ccI[:], cI[:], 4, None, ALU.arith_shift_right, None, ['cI'], ['ccI'])
    tt(tmpA[:], ccI[:], pc[:, 0:1].broadcast_to([128, 128]), ALU.is_equal, ['ccI', 'pc'], ['tmpA'])
    ts(tmpB[:], iot[:], 0.0, None, ALU.is_ge, None, ['iot'], ['tmpB'])
    tt(maskF[:], tmpA[:], tmpB[:], ALU.mult, ['tmpA', 'tmpB'], ['maskF'])
    ts(tmpB[:], iot[:], 0.0, None, ALU.is_le, None, ['iot', 'maskF'], ['tmpB'])
    tt(maskB[:], tmpA[:], tmpB[:], ALU.mult, ['tmpA', 'tmpB'], ['maskB'])
    tt(Emask[:], cI[:, 0:8], pc[:, 0:1].broadcast_to([128, 8]), ALU.is_equal, ['cI', 'pc'], ['Emask'])
    ts(tmpB[:, 0:8], cI[:, 0:8], -1.0, 7.0, ALU.mult, ALU.add, ['cI', 'maskB'], ['tmpB'])
    tt(Erev[:], tmpB[:, 0:8], pc[:, 0:1].broadcast_to([128, 8]), ALU.is_equal, ['tmpB', 'pc'], ['Erev'])
    big_i = sb("big_i", [128, T], I32); big_f = sb("big_f", [128, T], F32)
    big_g = sb("big_g", [128, T], F32); big_h = sb("big_h", [128, T], F32)
    S.op('pool', lambda e: e.iota(big_i[:], pattern=[[1, T]], base=0, channel_multiplier=0), w=['big_i'])
    ts(big_i[:], big_i[:], 15, None, ALU.bitwise_and, None, ['big_i'], ['big_i'])
    ts(m01[:], big_i[:], 0.0, None, ALU.is_gt, None, ['big_i'], ['m01'])
    freq = sb("freq", [128, 1], F32); jI = sb("jI", [128, 1], I32)
    ts(jI[:], pI[:], 15, None, ALU.bitwise_and, None, ['pI'], ['jI'])
    cp(freq[:], jI[:], ['jI'], ['freq'])
    act(freq[:], freq[:], AF.Exp, ['freq'], ['freq'], scale=-math.log(10000.0) / 16.0)
    for q in range(4):
        pat = [[1, 32], [0, 64]] if q % 2 == 0 else [[0, 32], [1, 64]]
        S.op('pool', lambda e, q=q, pat=pat: e.iota(big_i[32 * q:32 * q + 32, 0:L], pattern=pat, base=0, channel_multiplier=0),
             r=['m01'], w=['big_i'])
    ts(big_f[:, 0:L], big_i[:, 0:L], freq[:, 0:1], None, ALU.mult, None, ['big_i', 'freq'], ['big_f'])
    TWO_PI = 2.0 * math.pi

    def sin_of(dst, shift, dkey):
        ts(big_g[:, 0:L], big_f[:, 0:L], shift, None, ALU.add, None, ['big_f'], ['big_g'])
        ki = big_i[:, 0:L]
        ts(ki, big_g[:, 0:L], 1.0 / TWO_PI, None, ALU.mult, None, ['big_g'], ['big_i'])
        cp(big_h[:, 0:L], ki, ['big_i'], ['big_h'])
        stt(big_g[:, 0:L], big_h[:, 0:L], -TWO_PI, big_g[:, 0:L], ALU.mult, ALU.add, ['big_h', 'big_g'], ['big_g'])
        ts(big_h[:, 0:L], big_g[:, 0:L], math.pi, -TWO_PI, ALU.is_gt, ALU.mult, ['big_g'], ['big_h'])
        tt(big_g[:, 0:L], big_g[:, 0:L], big_h[:, 0:L], ALU.add, ['big_g', 'big_h'], ['big_g'])
        ts(big_h[:, 0:L], big_g[:, 0:L], -math.pi, TWO_PI, ALU.is_lt, ALU.mult, ['big_g'], ['big_h'])
        tt(big_g[:, 0:L], big_g[:, 0:L], big_h[:, 0:L], ALU.add, ['big_g', 'big_h'], ['big_g'])
        ts(big_g[:, 0:L], big_g[:, 0:L], -3.141592, 3.141592, ALU.max, ALU.min, ['big_g'], ['big_g'])
        act(dst, big_g[:, 0:L], AF.Sin, ['big_g'], [dkey])

    sin_of(sinT[:], 0.0, 'sinT')
    sin_of(cosT[:], math.pi / 2.0, 'cosT')
    for ch in range(2):
        for half in range(2):
            w = [2, 4, 8, 16][2 * ch + half]
            rows = slice(64 * half, 64 * half + 64)
            memset(rcE[rows, ch, :], 1.0 / w, ['rcE'], eng='pool')
            for t in range(w // 2):
                memset(rcE[rows, ch, t:t + 1], 1.0 / (t + w // 2), ['rcE'], eng='pool')
            for i in range(8):
                if (8 - i) < w // 2:
                    memset(rcE[rows, ch, 8 + i:9 + i], 1.0 / ((8 - i) + w // 2), ['rcE'], eng='pool')
    memset(pwbd[:], 0.0, ['pwbd'])
    memset(big_h[:], 0.0, ['big_h'])
    if True:
        for ch in range(2):
            S.dma('sp', U[ch, :, 0:T], big_h[:, 0:T], r=['big_h'], w=[('U', ch)])
            S.dma('sp', U[ch, :, T:UW], big_h[:, 0:UW - T], r=['big_h'], w=[('U', ch)])

    stg = sb("stg", [128, 128], F32)
    for half in range(2):
        memset(stg[:], 0.0, ['stg'])
        S.dma('sp', stg[0:96, :], b_ada.rearrange("l (j p) -> (l j) p", p=128)[96 * half:96 * half + 96, :], w=['stg'])
        tr(psb[0][:, 0:128], stg[:], ident_f[:], ['stg', 'ident_f'], ['ps0'])
        cp(bada[:, 96 * half:96 * half + 96], psb[0][:, 0:96], ['ps0'], ['bada'])
    memset(stg[:], 0.0, ['stg'])
    S.dma('sp', stg[0:32, :], norm1_w.rearrange("l (k p) -> (l k) p", p=128), w=['stg'])
    S.dma('sp', stg[32:64, :], norm2_w.rearrange("l (k p) -> (l k) p", p=128), w=['stg'])
    S.dma('sp', stg[64:72, :], pool_scale.rearrange("l (k p) -> (l k) p", p=128), w=['stg'])
    S.dma('sp', stg[72:88, :], hg_lb.rearrange("l d (k p) -> (l d k) p", p=128), w=['stg'])
    for (r0, src) in ((88, q_norm_w), (92, k_norm_w), (96, hg_norm_w)):
        S.dma('sp', stg[r0:r0 + 4, 0:64], src[:, :], w=['stg'])
        S.dma('sp', stg[r0:r0 + 4, 64:128], src[:, :], w=['stg'])
    S.dma('sp', stg[100:108, :], c_d[:, :], w=['stg'])
    S.dma('sp', stg[108:116, :], cc_d[:, :], w=['stg'])
    tr(psb[0][:, 0:128], stg[:], ident_f[:], ['stg', 'ident_f'], ['ps0'])
    cp(cst[:], psb[0][:, 0:128], ['ps0'], ['cst'])
    C_N1, C_N2, C_PS, C_LB, C_QN, C_KN, C_HN, C_C = 0, 32, 64, 72, 88, 92, 96, 100
    act(cT[:], cst[:, C_C:C_C + 16], AF.Silu, ['cst'], ['cT'])
    ex = sb("ex", [128, 16], F32); ssum = sb("ssum", [128, 4], F32)
    act(ex[:], cst[:, C_LB:C_LB + 16], AF.Exp, ['cst'], ['ex'])
    exv = ex[:].rearrange("p (l m) -> p l m", l=4)
    tt(ssum[:], exv[:, 0, :], exv[:, 1, :], ALU.add, ['ex'], ['ssum'])
    tt(ssum[:], ssum[:], exv[:, 2, :], ALU.add, ['ex', 'ssum'], ['ssum'])
    tt(ssum[:], ssum[:], exv[:, 3, :], ALU.add, ['ex', 'ssum'], ['ssum'])
    S.op('dve', lambda e: e.reciprocal(out=ssum[:], in_=ssum[:]), ['ssum'], ['ssum'])
    lbv = lbT[:].rearrange("p (l m) -> p l m", l=4)
    memset(lbT[:], 0.0, ['lbT'])
    tt(lbv[:, 1, :], exv[:, 1, :], ssum[:], ALU.mult, ['ex', 'ssum'], ['lbT'])
    tt(ex[:, 8:12], ex[:, 4:8], ex[:, 8:12], ALU.add, ['ex', 'lbT'], ['ex'])
    tt(lbv[:, 2, :], exv[:, 2, :], ssum[:], ALU.mult, ['ex', 'ssum'], ['lbT'])
    tt(ex[:, 12:16], ex[:, 8:12], ex[:, 12:16], ALU.add, ['ex', 'lbT'], ['ex'])
    tt(lbv[:, 3, :], exv[:, 3, :], ssum[:], ALU.mult, ['ex', 'ssum'], ['lbT'])
    ts(omlb[:], lbT[:], -1.0, 1.0, ALU.mult, ALU.add, ['lbT'], ['omlb'])
    ts(nomlb[:], omlb[:], -1.0, None, ALU.mult, None, ['omlb'], ['nomlb'])

    def adaln_steps(l):
        pk = 7; p = l % 2
        modT = modTs[p]
        wsrc = w_ada[l].rearrange("(kc p) n -> p kc n", p=128)
        nxt = load_w(wsrc[:, :, 0:512], (8, 512), 512)
        yield
        for wt in range(12):
            Wv, wk = nxt
            if wt + 1 < 12:
                nxt = load_w(wsrc[:, :, (wt + 1) * 512:(wt + 2) * 512], (8, 512), 512)
            for jj in range(4):
                j = wt * 4 + jj
                for kc in range(KC):
                    mm(psb[pk][:, 2 * j:2 * j + 2], Wv[:, kc, jj * 128:(jj + 1) * 128],
                       cT[:].rearrange("p (w k) -> p k w", w=2)[:, kc, :], kc == 0, kc == KC - 1, [wk, 'cT'], [f"ps{pk}"])
            yield
        tt(modT[:], psb[pk][:, 0:96].rearrange("p (j w) -> p j w", w=2),
           bada[:, l * 48:(l + 1) * 48].unsqueeze(2).broadcast_to([128, 48, 2]), ALU.add, [f"ps{pk}", 'bada'], [f"modT{p}"])
        for (dst, dkey, scc, ncol) in ((a1s[p], f"a1_{p}", 8, C_N1), (a2s[p], f"a2_{p}", 32, C_N2)):
            for wch in range(2):
                stt(dst[:, :, wch], modT[:, scc:scc + 8, wch], 1.0, cst[:, ncol + l * 8:ncol + l * 8 + 8], ALU.add, ALU.mult,
                    [f"modT{p}", 'cst'], [dkey])
        yield

    def modcol(idx, kc, wch):
        return modTs[CUR['l'] % 2][:, idx * 8 + kc, wch:wch + 1]

    def modkey():
        return f"modT{CUR['l'] % 2}"

    def xload(i):
        src = x_d[i * 128:(i + 1) * 128, :] if i < 16 else ctx_d[(i - 16) * 128:(i - 15) * 128, :]
        S.dma('sp', xs[i % 2][:], src, w=[f"xs{i % 2}"])
    ada0 = adaln_steps(0)
    xload(0)
    for i in range(NT):
        b = i % 2
        next(ada0, None)
        for hh in range(2):
            pk = 2 * b + hh
            for q in range(4):
                kc = hh * 4 + q
                tr(psb[pk][:, q * 128:(q + 1) * 128], xs[b][:, kc * 128:(kc + 1) * 128], ident_f[:], [f"xs{b}", 'ident_f'], [f"ps{pk}"])
            cp(xs2[b][:, hh * 4:hh * 4 + 4, :], psb[pk][:].rearrange("p (q t) -> p q t", q=4), [f"ps{pk}"], [f"xt2_{b}"],
               eng='dve' if hh == 0 else 'dve')
        if i + 1 < NT: xload(i + 1)
        S.dma('sp', XT.rearrange("c p t -> p c t")[:, :, i * 128:(i + 1) * 128], xs2[b][:], r=[f"xt2_{b}"], w=[('XT', ('ld', i))])
    for _ in ada0: pass
    S.barrier()
    sc0.close(); sb = sb0

    XTv = XT.rearrange("c p t -> p c t")
    PSA = Rot([0, 1, 2, 3])

    def norm_phase(a_t, akey, sh_idx):
        sc = Scope(nc); sb = sc.sb
        xblk = [sb(f"xblk{i}", [128, KC, 512], F32) for i in range(2)]
        sqb = [sb(f"sqb{i}", [128, KC, 512], BF16) for i in range(2)]
        rstd_b = [sb(f"rstd{i}", [128, 512], F32) for i in range(2)]
        lnv_b = [sb(f"lnv{i}", [128, 512], F32) for i in range(2)]
        ntmp = [sb(f"ntmp{i}", [128, 512], F32) for i in range(8)]

        def stats(bi):
            t0, n = BLOCKS[bi]; b = bi % 2
            S.dma('sp', xblk[b][:, :, 0:n], XTv[:, :, t0:t0 + n], r=['XT'], w=[f"xblk{b}"])
            act(sqb[b][:, :, 0:n], xblk[b][:, :, 0:n], AF.Square, [f"xblk{b}"], [f"sqb{b}"])
            pk = PSA.get()
            for kc in range(KC):
                mm(psb[pk][:, 0:n], ones_b[:], sqb[b][:, kc, 0:n], kc == 0, kc == KC - 1, ['ones_b', f"sqb{b}"], [f"ps{pk}"])
            act(lnv_b[b][:, 0:n], psb[pk][:, 0:n], AF.Ln, [f"ps{pk}"], [f"lnv{b}"], scale=1.0 / D, bias=EPS)
            act(rstd_b[b][:, 0:n], lnv_b[b][:, 0:n], AF.Exp, [f"lnv{b}"], [f"rstd{b}"], scale=-0.5)

        def apply(bi):
            t0, n = BLOCKS[bi]; b = bi % 2
            wch = 0 if t0 < L else 1
            for kc in range(KC):
                stt(ntmp[kc][:, 0:n], xblk[b][:, kc, 0:n], a_t[:, kc, wch:wch + 1], rstd_b[b][:, 0:n], ALU.mult, ALU.mult,
                    [f"xblk{b}", f"rstd{b}", akey], [f"ntmp{kc}"])
            for kc in range(KC):
                if kc % 2 == 0:
                    act(hT[:, kc, t0:t0 + n], ntmp[kc][:, 0:n], AF.Identity, [f"ntmp{kc}", modkey()], [('hT', (kc, bi))],
                        bias=modcol(sh_idx, kc, wch), scale=1.0)
                else:
                    ts(hT[:, kc, t0:t0 + n], ntmp[kc][:, 0:n], modcol(sh_idx, kc, wch), None, ALU.add, None,
                       [f"ntmp{kc}", modkey()], [('hT', (kc, bi))])
        stats(0)
        for bi in range(len(BLOCKS)):
            if bi + 1 < len(BLOCKS): stats(bi + 1)
            apply(bi)
        S.barrier(); sc.close()

    def tap(name, src_ap, shape, dtp, r):
        if name not in taps: return
        t_ = dram("tap_" + name, shape, dtp, "ExternalOutput")
        tap_d[name] = t_
        S.dma('sp', t_, src_ap, r=r)

    qraw = [sb(f"qraw{i}", [128, 512], F32) for i in range(2)]
    qsq = [sb(f"qsq{i}", [128, 512], BF16) for i in range(2)]
    qn_b = [sb(f"qn{i}", [128, 512], BF16) for i in range(2)]
    qt1 = [sb(f"qt1_{i}", [128, 512], F32) for i in range(2)]
    qt2 = [sb(f"qt2_{i}", [128, 512], F32) for i in range(2)]
    QR = Rot([0, 1])
    A = {}

    def open_attn_scope():
        sc = Scope(nc); sb = sc.sb
        A['qT'] = sb("qT", [128, 4, T], BF16)
        A['kTz'] = sb("kTz", [128, 2, 2, T], BF16)
        A['aT'] = sb("aT", [128, 4, T], BF16)
        A['VE'] = sb("VE", [128, NT, 2, 192], BF16)
        A['pT'] = [sb(f"pT{i}", [128, 512], BF16) for i in range(4)]
        A['rden'] = sb("rden", [128, 512], F32); A['rbc'] = sb("rbc", [128, 512], F32)
        A['tmst'] = [sb(f"tmst{i}", [128, 4, 128], BF16) for i in range(2)]
        memset(A['VE'][:], 1.0, ['VE'])
        memset(A['kTz'][0:64, :, 1, :], 0.0, ['kTz'])
        return sc
    PT = Rot([0, 1, 2, 3]); TMST = Rot([0, 1])

    def qk_epilogue(pk, n, bi, t0, dest, dkey, wcol):
        latent = t0 < L
        i = QR.get()
        act(qn_b[i][:, 0:n], psb[pk][:, 0:n], AF.Copy, [f"ps{pk}", 'cst'], [f"qn{i}"], scale=wcol)
        act(qsq[i][:, 0:n], psb[pk][:, 0:n], AF.Square, [f"ps{pk}"], [f"qsq{i}"])

        def part2():
            p2 = PSQ.get()
            mm(psb[p2][:, 0:n], blk_b[:], qsq[i][:, 0:n], True, True, ['blk_b', f"qsq{i}"], [f"ps{p2}"])
            if latent:
                p3 = PSQ.get()
                mm(psb[p3][:, 0:n], Rt[:], qn_b[i][:, 0:n], True, True, ['Rt', f"qn{i}"], [f"ps{p3}"])
            act(qt1[i][:, 0:n], psb[p2][:, 0:n], AF.Ln, [f"ps{p2}"], [f"qt1_{i}"], scale=1.0 / 64, bias=EPS)
            act(qt1[i][:, 0:n], qt1[i][:, 0:n], AF.Exp, [f"qt1_{i}"], [f"qt1_{i}"], scale=-0.5)
            if latent:
                tt(qraw[i][:, 0:n], qn_b[i][:, 0:n], cosT[:, t0:t0 + n], ALU.mult, [f"qn{i}", 'cosT'], [f"qraw{i}"])
                tt(qt2[i][:, 0:n], psb[p3][:, 0:n], sinT[:, t0:t0 + n], ALU.mult, [f"ps{p3}", 'sinT'], [f"qt2_{i}"])
                tt(qraw[i][:, 0:n], qraw[i][:, 0:n], qt2[i][:, 0:n], ALU.add, [f"qraw{i}", f"qt2_{i}"], [f"qraw{i}"])
                tt(dest, qraw[i][:, 0:n], qt1[i][:, 0:n], ALU.mult, [f"qraw{i}", f"qt1_{i}"], [dkey])
            else:
                tt(dest, qn_b[i][:, 0:n], qt1[i][:, 0:n], ALU.mult, [f"qn{i}", f"qt1_{i}"], [dkey])
        return part2

    PSQ = Rot([4, 5, 6])

    def evac_dram(pk, n, dst_ap, dkey, func, dtp):
        if dtp == F32:
            i = STF.get(); buf = stf[i]; bk = f"stf{i}"
        else:
            i = STB.get(); buf = stb[i]; bk = f"stb{i}"
        if func is None:
            cp(buf[:, 0:n], psb[pk][:, 0:n], [f"ps{pk}"], [bk])
        else:
            act(buf[:, 0:n], psb[pk][:, 0:n], func, [f"ps{pk}"], [bk])
        S.dma('sp', dst_ap, buf[:, 0:n], r=[bk], w=[dkey])

    DEF = {'fn': None}

    def flush_deferred():
        if DEF['fn'] is not None:
            DEF['fn'](); DEF['fn'] = None

    def fm_chunk(Wv, wk, jj, epi):
        for bi, (t0, n) in enumerate(BLOCKS):
            pk = PSA.get()
            for kc in range(KC):
                mm(psb[pk][:, 0:n], Wv[:, kc, jj * 128:(jj + 1) * 128], hT[:, kc, t0:t0 + n], kc == 0, kc == KC - 1,
                   [wk, ('hT', (kc, bi))], [f"ps{pk}"])
            flush_deferred()
            r_ = epi(pk, n, bi, t0)
            DEF['fn'] = r_ if callable(r_) else None

    def tm_chunk(Wv, wk, jj, epi):
        for g4 in range(0, NT, 4):
            pk = PSA.get()
            nt4 = min(4, NT - g4)
            for q in range(nt4):
                i = g4 + q
                bi = min(i // 4, 4)
                for kc in range(KC):
                    mm(psb[pk][:, q * 128:(q + 1) * 128], hT[:, kc, i * 128:(i + 1) * 128], Wv[:, kc, jj * 128:(jj + 1) * 128],
                       kc == 0, kc == KC - 1, [wk, ('hT', (kc, bi))], [f"ps{pk}"])
            epi(pk, g4, nt4)

    def in_proj(l):
        wsrc = w_in[l].rearrange("(kc p) n -> p kc n", p=128)
        qcol = cst[:, C_QN + l:C_QN + l + 1]; kcol = cst[:, C_KN + l:C_KN + l + 1]

        def epi_for(j):
            if j in (0, 1):
                return lambda pk, n, bi, t0: evac_dram(pk, n, U[j, :, ucol(t0):ucol(t0) + n], ('U', (j, bi)), None, F32)
            if 2 <= j <= 5:
                return lambda pk, n, bi, t0: qk_epilogue(pk, n, bi, t0, A['qT'][:, j - 2, t0:t0 + n], ('qT', (j - 2, bi)), qcol)
            if j in (8, 9):
                return lambda pk, n, bi, t0: evac_dram(pk, n, HQ[j - 8, :, t0:t0 + n], ('HQ', (j - 8, bi)), AF.Copy, F32)
            if 12 <= j <= 15:
                return lambda pk, n, bi, t0: evac_dram(pk, n, Z[j - 12, :, t0:t0 + n], ('Z', (j - 12, bi)), None, F32)
            if j in (16, 17):
                return lambda pk, n, bi, t0: evac_dram(pk, n, HGATE[j - 16, :, t0:t0 + n], ('HGATE', (j - 16, bi)), AF.Silu, BF16)
            if j >= 18:
                return lambda pk, n, bi, t0: evac_dram(pk, n, GT[j - 18, :, t0:t0 + n], ('GT', (j - 18, bi)), AF.Sigmoid, BF16)
            return None

        def epi_v(pk, g4, nt4):
            for g in range(2):
                cp(A['VE'][:, g4:g4 + nt4, g, 64:128], psb[pk][:, 0:nt4 * 128].rearrange("p (q c) -> p q c", q=nt4)[:, :, g * 64:(g + 1) * 64],
                   [f"ps{pk}"], ['VE'])

        def epi_hi(hc):
            def f(pk, g4, nt4):
                i = TMST.get()
                act(A['tmst'][i][:, 0:nt4, :], psb[pk][:, 0:nt4 * 128].rearrange("p (q c) -> p q c", q=nt4), AF.Copy, [f"ps{pk}"], [f"tmst{i}"])
                S.dma('sp', VH.rearrange("(i p) c -> p i c", p=128)[:, g4:g4 + nt4, hc * 128:(hc + 1) * 128], A['tmst'][i][:, 0:nt4, :],
                      r=[f"tmst{i}"], w=[('VH', (hc, g4))])
            return f

        i = Wrot.get()
        Wk = Wb[i][:, 0:8 * 256].rearrange("p (a b) -> p a b", a=8)
        for g in range(2):
            for dup in range(2):
                S.dma('pool', Wk[:, :, g * 128 + dup * 64:g * 128 + dup * 64 + 64], wsrc[:, :, 768 + g * 64:768 + g * 64 + 64], w=[f"W{i}"])
        for g in range(2):
            def kepi(pk, n, bi, t0, g=g):
                p2 = qk_epilogue(pk, n, bi, t0, A['kTz'][:, g, 0, t0:t0 + n], ('kTz', (g, bi)), kcol)

                def fin():
                    p2()
                    cp(A['kTz'][64:128, g, 1, t0:t0 + n], A['kTz'][64:128, g, 0, t0:t0 + n], [('kTz', (g, bi))], [('kTz', (g, bi))])
                    memset(A['kTz'][64:128, g, 0, t0:t0 + n], 0.0, [('kTz', (g, bi))])
                return fin
            fm_chunk(Wk, f"W{i}", g, kepi)
        ntile = (IN_W + 511) // 512
        for wt in range(ntile):
            c0 = wt * 512; ncl = min(512, IN_W - c0)
            Wv, wk = load_w(wsrc[:, :, c0:c0 + ncl], (8, ncl), ncl)
            for jj in range(ncl // 128):
                j = wt * 4 + jj
                if j == 6: continue
                if j == 7: tm_chunk(Wv, wk, jj, epi_v)
                elif j in (10, 11): tm_chunk(Wv, wk, jj, epi_hi(j - 10))
                else: fm_chunk(Wv, wk, jj, epi_for(j))
        flush_deferred()

    def attention(l, ada_steps=None):
        LA = 2
        for bi, (t0, n) in enumerate(BLOCKS):
            latent = t0 < L
            if not latent and l == depth - 1 and depth == DEPTH:
                continue
            ktiles = list(range(NT)) if latent else [16, 17]
            items = [(h, ii, i) for h in range(8) for ii, i in enumerate(ktiles)]
            sc_bank = {}

            def emit_score(idx):
                h, ii, i = items[idx]
                g = h // 4; ch = h // 2; pb = (h % 2) * 64
                pk = PSA.get()
                mm(psb[pk][:, 0:n], A['kTz'][:, g, h % 2, i * 128:(i + 1) * 128], A['qT'][:, ch, t0:t0 + n], True, True,
                   [('kTz', (g, min(i // 4, 4))), ('qT', (ch, bi))], [f"ps{pk}"])
                sc_bank[idx] = pk

            def epi2(h):
                ch = h // 2; pb = (h % 2) * 64; po = 4 + (h % 2); r0 = 64 if h % 2 == 0 else 0
                mm(psb[6][:, 0:n], onesf[r0:r0 + 1, :], A['rden'][r0:r0 + 1, 0:n], True, True, ['onesf', ('rden', h % 2)], ['ps6'])
                cp(A['rbc'][pb:pb + 64, 0:n], psb[6][pb:pb + 64, 0:n], ['ps6'], [('rbc', h % 2)])
                tt(A['aT'][pb:pb + 64, ch, t0:t0 + n], psb[po][pb:pb + 64, 0:n], A['rbc'][pb:pb + 64, 0:n], ALU.mult,
                   [f"ps{po}", ('rbc', h % 2)], [('aT', (ch, bi, h % 2))])

            pending = []
            for idx in range(min(LA, len(items))): emit_score(idx)
            for idx, (h, ii, i) in enumerate(items):
                g = h // 4; po = 4 + (h % 2)
                voff = 64 if h % 2 == 0 else 0
                pk = sc_bank.pop(idx)
                pi = PT.get()
                act(A['pT'][pi][:, 0:n], psb[pk][:, 0:n], AF.Exp, [f"ps{pk}"], [f"pT{pi}"], scale=0.125)
                if idx + LA < len(items): emit_score(idx + LA)
                mm(psb[po][:, 0:n], A['VE'][:, i, g, voff:voff + 128], A['pT'][pi][:, 0:n], ii == 0, ii == len(ktiles) - 1,
                   ['VE', f"pT{pi}"], [f"ps{po}"])
                if ii == len(ktiles) - 1:
                    r0 = 64 if h % 2 == 0 else 0
                    S.op('dve', lambda e, r0=r0, po=po: e.reciprocal(out=A['rden'][r0:r0 + 1, 0:n], in_=psb[po][r0:r0 + 1, 0:n]),
                         [f"ps{po}"], [('rden', h % 2)])
                    pending.append((idx + min(8, 2 * len(ktiles) - 2), h))
                while pending and pending[0][0] <= idx:
                    epi2(pending.pop(0)[1])
                if ii == 0 and ada_steps is not None:
                    next(ada_steps, None)
            while pending:
                epi2(pending.pop(0)[1])
        for ch in range(4):
            S.dma('sp', BR[2 + ch, :, :], A['aT'][:, ch, :], r=['aT'], w=[('BR', 2 + ch)])

    def pool_phase(l):
        N = UW
        sc = Scope(nc); sb = sc.sb
        upad = sb("upad", [128, 2, UW], F32)
        s2 = sb("pl_s2", [128, UW], F32); s4 = sb("pl_s4", [128, UW], F32); s8 = sb("pl_s8", [128, UW], F32)
        ypool = sb("ypool", [128, 2, T], BF16)
        yedge = sb("yedge", [128, 16], F32)
        for ch in range(2):
            S.dma('sp', upad[:, ch, :], U[ch, :, :], r=['U'], w=[('upad', ch)])
            u = upad[:, ch, :]
            tt(s2[:, 1:N], u[:, 0:N - 1], u[:, 1:N], ALU.add, [('upad', ch)], ['pl_s2'])
            tt(s4[:, 2:N - 1], s2[:, 1:N - 2], s2[:, 3:N], ALU.add, ['pl_s2'], ['pl_s4'])
            if ch == 0:
                lv = ((s2, 'pl_s2'), (s4, 'pl_s4'))
            else:
                tt(s8[:, 4:N - 3], s4[:, 2:N - 5], s4[:, 6:N - 1], ALU.add, ['pl_s4'], ['pl_s8'])
                tt(s2[:, 8:N - 7], s8[:, 4:N - 11], s8[:, 12:N - 3], ALU.add, ['pl_s8', 'pl_s2'], ['pl_s2'])
                lv = ((s8, 'pl_s8'), (s2, 'pl_s2'))
            for half in range(2):
                w = [2, 4, 8, 16][2 * ch + half]
                rows = slice(64 * half, 64 * half + 64)
                src, skey = lv[half]
                for (ts0, tn, uc) in ((0, L, 8), (L, CT, L + 24)):
                    stt(ypool[rows, ch, ts0:ts0 + tn], src[rows, uc:uc + tn], 1.0 / w, u[rows, uc:uc + tn], ALU.mult, ALU.subtract,
                        [skey, ('upad', ch)], [('ypool', ch)])
                    for (e0, tb) in ((0, 0), (tn - 8, 8)):
                        tt(yedge[rows, 0:8], src[rows, uc + e0:uc + e0 + 8], rcE[rows, ch, tb:tb + 8], ALU.mult, [skey, 'rcE'], ['yedge'])
                        tt(ypool[rows, ch, ts0 + e0:ts0 + e0 + 8], yedge[rows, 0:8], u[rows, uc + e0:uc + e0 + 8], ALU.subtract,
                           ['yedge', ('upad', ch)], [('ypool', ch)])
        for g in range(4):
            ch, half = g // 2, g % 2
            S.dma('pool', pwbd[64 * half:64 * half + 64, ch, 64 * half:64 * half + 64], pool_w[l, g, :, :], w=['pwbd'])
        for ch in range(2):
            for bi, (t0, n) in enumerate(BLOCKS):
                pk = PSA.get()
                mm(psb[pk][:, 0:n], pwbd[:, ch, :], ypool[:, ch, t0:t0 + n], True, True, ['pwbd', ('ypool', ch)], [f"ps{pk}"])
                i = STB.get()
                ts(stb[i][:, 0:n], psb[pk][:, 0:n], cst[:, C_PS + l * 2 + ch:C_PS + l * 2 + ch + 1], None, ALU.mult, None, [f"ps{pk}", 'cst'], [f"stb{i}"])
                S.dma('sp', BR[ch, :, t0:t0 + n], stb[i][:, 0:n], r=[f"stb{i}"], w=[('BR', (ch, bi))])
        S.barrier(); sc.close()

    NCH = T // 16
    VBLK = Rot([0, 1]); ATM = Rot([0, 1, 2])
    Sbf = [hT[:, 4 * d:4 * d + 4, :].rearrange("p a b -> p (a b)").rearrange("p (v n) -> p v n", v=64) for d in range(2)]

    def nat(i, n, d=0):
        if d == 0:
            return (i - 16) * 8 + n if i >= 16 else 16 + i * 8 + n
        return 128 + (i - 16) * 8 + n if i >= 16 else i * 8 + n

    def hgrn_phase(l):
        sc = Scope(nc); sb = sc.sb
        qdec = [sb(f"qdec{d}", [128, T], BF16) for d in range(2)]
        ktil = [sb(f"ktil{d}", [128, T], BF16) for d in range(2)]
        kdecTM = [sb(f"kdecTM{d}", [128, NT, 128], BF16) for d in range(2)]
        abuf = [sb(f"abuf{d}", [128, NCH], F32) for d in range(2)]
        Vh = sb("Vh", [128, NT, 256], BF16)
        S.dma('sp', Vh[:], VH.rearrange("(i p) c -> p i c", p=128), r=['VH'], w=['Vh'])
        for hp in range(2):
            sc1 = Scope(nc); sb = sc1.sb
            TH = T // 3
            NS = 4
            hzs = [sb(f"hz{u}", [128, TH], F32) for u in range(NS)]; hls = [sb(f"hlogf{u}", [128, TH], F32) for u in range(NS)]
            hGs = [sb(f"hG{u}", [128, TH], F32) for u in range(NS)]; hDs = [sb(f"hD{u}", [128, TH], F32) for u in range(NS)]
            kds = [sb(f"kdecT{u}", [128, TH], BF16) for u in range(NS)]
            hqs = [sb(f"hq_sb{u}", [128, TH], F32) for u in range(2)]
            for th in range(3):
                sl = slice(th * TH, (th + 1) * TH)
                hq_sb = hqs[th % 2]; hqk = f"hq_sb{th % 2}"
                S.dma('sp', hq_sb[:], HQ[hp, :, sl], r=['HQ'], w=[hqk])
                U2 = []
                for d in range(2):
                    u = (2 * th + d) % NS
                    lcol = l * 4 + d * 2 + hp
                    U2.append(dict(d=d, u=u, lcol=lcol, ocol=omlb[:, lcol:lcol + 1], hz=hzs[u], hl=hls[u], hG=hGs[u], hD=hDs[u], kd=kds[u]))
                for q in U2:
                    S.dma('sp', q['hz'][:], Z[q['d'] * 2 + hp, :, sl], r=['Z'], w=[f"hz{q['u']}"])
                for q in U2:
                    act(q['hz'][:], q['hz'][:], AF.Sigmoid, [f"hz{q['u']}"], [f"hz{q['u']}"])
                for q in U2:
                    ts(q['hl'][:], q['hz'][:], q['ocol'], lbT[:, q['lcol']:q['lcol'] + 1], ALU.mult, ALU.add,
                       [f"hz{q['u']}", 'omlb', 'lbT'], [f"hlogf{q['u']}"])
                for q in U2:
                    act(q['hl'][:], q['hl'][:], AF.Ln, [f"hlogf{q['u']}"], [f"hlogf{q['u']}"])
                for q in U2:
                    ts(q['hz'][:], q['hz'][:], -1.0, 1.0, ALU.mult, ALU.add, [f"hz{q['u']}"], [f"hz{q['u']}"])
                for q in U2:
                    S.op('dve', lambda e, q=q: e.tensor_tensor_scan(out=q['hG'][:], data0=m01[:, 0:TH], data1=q['hl'][:], initial=0.0,
                                                                    op0=ALU.mult, op1=ALU.add), ['m01', f"hlogf{q['u']}"], [f"hG{q['u']}"])
                for q in U2:
                    d = q['d']; loff = 16 if d == 0 else 0; coff = 0 if d == 0 else 128
                    if th < 2:
                        act(abuf[d][:, loff + 48 * th:loff + 48 * th + 48], q['hG'][:, 15:TH:16], AF.Exp, [f"hG{q['u']}"], [f"abuf{d}"])
                    else:
                        act(abuf[d][:, loff + 96:loff + 128], q['hG'][:, 15:512:16], AF.Exp, [f"hG{q['u']}"], [f"abuf{d}"])
                        act(abuf[d][:, coff:coff + 16], q['hG'][:, 512 + 15:TH:16], AF.Exp, [f"hG{q['u']}"], [f"abuf{d}"])
                for q in U2:
                    G3 = q['hG'][:].rearrange("p (c s) -> p c s", s=16)
                    tt(q['hD'][:].rearrange("p (c s) -> p c s", s=16), G3[:, :, 15:16].broadcast_to([128, TH // 16, 16]), G3, ALU.subtract,
                       [f"hG{q['u']}"], [f"hD{q['u']}"], eng='pool')
                q1 = U2[1]
                tt(q1['hD'][:], q1['hD'][:], q1['hl'][:], ALU.add, [f"hD{q1['u']}", f"hlogf{q1['u']}"], [f"hD{q1['u']}"])
                tt(q1['hG'][:], q1['hG'][:], q1['hl'][:], ALU.subtract, [f"hG{q1['u']}", f"hlogf{q1['u']}"], [f"hG{q1['u']}"])
                for q in U2:
                    q['Gd'], q['gk'] = (q['hG'], f"hG{q['u']}") if q['d'] == 0 else (q['hD'], f"hD{q['u']}")
                    q['GL'], q['glk'] = (q['hD'], f"hD{q['u']}") if q['d'] == 0 else (q['hG'], f"hG{q['u']}")
                for q in U2:
                    act(q['hl'][:], q['Gd'][:], AF.Exp, [q['gk']], [f"hlogf{q['u']}"])
                for q in U2:
                    tt(qdec[q['d']][:, sl], hq_sb[:], q['hl'][:], ALU.mult, [hqk, f"hlogf{q['u']}"], [(f"qdec{q['d']}", th)], eng='pool')
                for q in U2:
                    act(q['hl'][:], q['Gd'][:], AF.Exp, [q['gk']], [f"hlogf{q['u']}"], scale=-1.0)
                for q in U2:
                    stt(ktil[q['d']][:, sl], q['hz'][:], q['ocol'], q['hl'][:], ALU.mult, ALU.mult,
                        [f"hz{q['u']}", f"hlogf{q['u']}", 'omlb'], [(f"ktil{q['d']}", th)])
                for q in U2:
                    act(q['hl'][:], q['GL'][:], AF.Exp, [q['glk']], [f"hlogf{q['u']}"])
                for q in U2:
                    stt(q['kd'][:], q['hz'][:], q['ocol'], q['hl'][:], ALU.mult, ALU.mult,
                        [f"hz{q['u']}", f"hlogf{q['u']}", 'omlb'], [f"kdecT{q['u']}"])
                for q in U2:
                    for g3 in range(2):
                        pk = PSA.get()
                        pv = psb[pk][:].bitcast(BF16)
                        for t3 in range(3):
                            c0 = (g3 * 3 + t3) * 128
                            tr(pv[:, t3 * 128:(t3 + 1) * 128], q['kd'][:, c0:c0 + 128], ident_b[:], [f"kdecT{q['u']}", 'ident_b'], [f"ps{pk}"])
                        i0_ = th * 6 + g3 * 3
                        act(kdecTM[q['d']][:, i0_:i0_ + 3, :], pv[:, 0:3 * 128].rearrange("p (q c) -> p q c", q=3), AF.Copy,
                            [f"ps{pk}"], [f"kdecTM{q['d']}"])
            S.barrier(); sc1.close()
            sc2 = Scope(nc); sb = sc2.sb
            hgate_sb = sb("hgate_sb", [128, T], BF16)
            osum = sb("osum", [128, T], F32)
            Vblk = [sb(f"Vblk{i}", [128, 2, 8, 64], BF16) for i in range(2)]
            ATm = [sb(f"ATm{i}", [128, 128], BF16) for i in range(3)]
            kvbufs = [sb(f"kvbuf{d}", [128, 64, NCH], BF16) for d in range(2)]
            VR = 8
            a_rep = sb("a_rep", [128, VR, NCH], F32)
            S.dma('sp', hgate_sb[:], HGATE[hp, :, :], r=['HGATE'], w=['hgate_sb'])
            for i in range(NT):
                vi = VBLK.get()
                tt(Vblk[vi][:], Vh[:, i, hp * 128:(hp + 1) * 128].rearrange("p (h v) -> p h v", h=2).unsqueeze(2).broadcast_to([128, 2, 8, 64]),
                   Emask[:].unsqueeze(1).unsqueeze(3).broadcast_to([128, 2, 8, 64]), ALU.mult, ['Vh', 'Emask'], [f"Vblk{vi}"])
                for d in range(2):
                    j0 = nat(i, 0, d)
                    pk = PSA.get()
                    for h2 in range(2):
                        mm(psb[pk][h2 * 64:(h2 + 1) * 64, :], kdecTM[d][:, i, h2 * 64:(h2 + 1) * 64],
                           Vblk[vi][:, h2, :, :].rearrange("p n v -> p (n v)"), True, True, [f"kdecTM{d}", f"Vblk{vi}"], [f"ps{pk}"])
                    act(kvbufs[d][:, :, j0:j0 + 8], psb[pk][:].rearrange("p (n v) -> p v n", n=8), AF.Copy, [f"ps{pk}"], [f"kvbuf{d}"])
            items = [(i, h2, d) for i in range(NT) for h2 in range(2) for d in range(2)]
            abank = {}

            def emit_A(idx):
                i, h2, d = items[idx]
                rows = slice(h2 * 64, h2 * 64 + 64)
                pk = PSA.get()
                mm(psb[pk][:, 0:128], ktil[d][rows, i * 128:(i + 1) * 128], qdec[d][rows, i * 128:(i + 1) * 128], True, True,
                   [f"ktil{d}", f"qdec{d}"], [f"ps{pk}"])
                abank[idx] = pk
            for idx in range(2): emit_A(idx)
            for idx, (i, h2, d) in enumerate(items):
                rows = slice(h2 * 64, h2 * 64 + 64)
                po = 4 + (i % 2)
                pk = abank.pop(idx)
                ai = ATM.get()
                tt(ATm[ai][:], psb[pk][:, 0:128], (maskF if d == 0 else maskB)[:], ALU.mult, [f"ps{pk}", 'maskF', 'maskB'], [f"ATm{ai}"])
                if idx + 2 < len(items): emit_A(idx + 2)
                mm(psb[po][rows, 0:128], Vh[:, i, hp * 128 + h2 * 64:hp * 128 + h2 * 64 + 64], ATm[ai][:], d == 0, d == 1,
                   ['Vh', f"ATm{ai}"], [(f"ps{po}", h2)])
                if h2 == 1 and d == 1:
                    act(osum[:, i * 128:(i + 1) * 128], psb[po][:, 0:128], AF.Copy, [f"ps{po}"], [('osum', i)])
            for d in range(2):
                kvbuf = kvbufs[d]; kvk = f"kvbuf{d}"
                cp(a_rep[:], abuf[d][:].unsqueeze(1).broadcast_to([128, VR, NCH]), [f"abuf{d}"], ['a_rep'])
                rc = 0 if d == 0 else NCH - 1
                memset(a_rep[:, :, rc:rc + 1], 0.0, ['a_rep'])
                af = a_rep[:].rearrange("p v n -> p (v n)")
                for g4 in range(64 // VR):
                    kf = kvbuf[:, VR * g4:VR * g4 + VR, :].rearrange("p v n -> p (v n)")
                    of = Sbf[d][:, VR * g4:VR * g4 + VR, :].rearrange("p v n -> p (v n)")
                    if d == 0:
                        S.op('dve', lambda e: e.tensor_tensor_scan(out=of, data0=af, data1=kf, initial=0.0, op0=ALU.mult, op1=ALU.add),
                             [kvk, 'a_rep'], [f"Sbf{d}"])
                    else:
                        NF = VR * NCH
                        S.op('dve', lambda e: e.tensor_tensor_scan(out=of[:, NF - 1::-1], data0=af[:, NF - 1::-1], data1=kf[:, NF - 1::-1],
                                                                  initial=0.0, op0=ALU.mult, op1=ALU.add), [kvk, 'a_rep'], [f"Sbf{d}"])
            for i in range(NT):
                po = 4 + (i % 2)
                for h2 in range(2):
                    rows = slice(h2 * 64, h2 * 64 + 64)
                    items = []
                    for d in range(2):
                        for n in range(8):
                            m = nat(i, n, d)
                            if d == 0:
                                if m == 0: continue
                                js = m - 1
                            else:
                                if m == NCH - 1: continue
                                js = m + 1
                            items.append((d, n, js))
                    seen = set()
                    for k, (d, n, js) in enumerate(items):
                        mm(psb[po][rows, n * 16:(n + 1) * 16], Sbf[d][rows, :, js], qdec[d][rows, i * 128 + n * 16:i * 128 + (n + 1) * 16],
                           k == 0, k == len(items) - 1, [f"Sbf{d}", f"qdec{d}"], [(f"ps{po}", h2)])
                tt(osum[:, i * 128:(i + 1) * 128], psb[po][:, 0:128], osum[:, i * 128:(i + 1) * 128], ALU.add, [f"ps{po}", ('osum', i)], [('osum', i)])
            hcol = cst[:, C_HN + l:C_HN + l + 1]
            for bi, (t0, n) in enumerate(BLOCKS):
                i = QR.get()
                act(qsq[i][:, 0:n], osum[:, t0:t0 + n], AF.Square, ['osum'], [f"qsq{i}"])
                p2 = PSQ.get()
                mm(psb[p2][:, 0:n], blk_b[:], qsq[i][:, 0:n], True, True, ['blk_b', f"qsq{i}"], [f"ps{p2}"])
                act(qt1[i][:, 0:n], psb[p2][:, 0:n], AF.Ln, [f"ps{p2}"], [f"qt1_{i}"], scale=1.0 / 64, bias=EPS)
                act(qt1[i][:, 0:n], qt1[i][:, 0:n], AF.Exp, [f"qt1_{i}"], [f"qt1_{i}"], scale=-0.5)
                stt(qt2[i][:, 0:n], osum[:, t0:t0 + n], hcol, qt1[i][:, 0:n], ALU.mult, ALU.mult, ['osum', f"qt1_{i}", 'cst'], [f"qt2_{i}"])
                si = STB.get()
                tt(stb[si][:, 0:n], qt2[i][:, 0:n], hgate_sb[:, t0:t0 + n], ALU.mult, [f"qt2_{i}", 'hgate_sb'], [f"stb{si}"])
                S.dma('sp', BR[6 + hp, :, t0:t0 + n], stb[si][:, 0:n], r=[f"stb{si}"], w=[('BR', (6 + hp, bi))])
            if 'osum' in taps and l == 0 and hp == 0: tap('osum', osum[:], [128, T], F32, ['osum'])
            S.barrier(); sc2.close()
        S.barrier(); sc.close()

    GTB = Rot([0, 1]); MT = Rot([0, 1, 2])

    def merge_phase(l):
        S.barrier(pool=True)
        sc = Scope(nc); sb = sc.sb
        wbr = sb("wbr", [128, 8, D], BF16)
        wo_sb = sb("wo_sb", [128, 8, D], BF16)
        brb = [sb(f"brb{i}", [128, 8, 512], BF16) for i in range(2)]
        gtb = [sb(f"gtb{i}", [128, 3, 512], BF16) for i in range(2)]
        yT = [sb(f"yT{i}", [128, 8, 512], BF16) for i in range(2)]
        mt = [sb(f"mt{i}", [128, 512], F32) for i in range(3)]
        xj = [sb(f"xj{i}", [128, 512], F32) for i in range(2)]; XJ = Rot([0, 1])
        c2 = [sb(f"c2_{i}", [128, 512], F32) for i in range(2)]; C2 = Rot([0, 1])
        for cbh in range(2):
            cs = slice(cbh * 512, cbh * 512 + 512)
            S.dma('pool', wbr[:, 0:2, cs], w_bp[l].rearrange("(kc p) n -> p kc n", p=128)[:, :, cs], w=[('wbr', (0, cbh))])
            S.dma('pool', wbr[:, 2:6, cs], w_ba[l].rearrange("(kc p) n -> p kc n", p=128)[:, :, cs], w=[('wbr', (1, cbh))])
            S.dma('pool', wbr[:, 6:8, cs], w_bh[l].rearrange("(kc p) n -> p kc n", p=128)[:, :, cs], w=[('wbr', (2, cbh))])
        for cbh in range(2):
            cs = slice(cbh * 512, cbh * 512 + 512)
            for h in range(2):
                S.dma('pool', wo_sb[:, h * 4:h * 4 + 4, cs], w_out[l].rearrange("(kc p) n -> p kc n", p=128)[:, h * 4:h * 4 + 4, cs],
                      w=[('wo_sb', (h, cbh))])
        groups = ((0, 2), (2, 6), (6, 8))
        mblocks = [(bi, t0, n) for bi, (t0, n) in enumerate(BLOCKS) if not (t0 >= L and l == depth - 1 and depth == DEPTH)]

        def load_brb(k):
            bi_, t0_, n_ = mblocks[k]
            S.dma('sp', brb[bi_ % 2][:, :, 0:n_], BR.rearrange("c p t -> p c t")[:, :, t0_:t0_ + n_], r=['BR'], w=[f"brb{bi_ % 2}"])
        def branch_load(k, j):
            bi, t0, n = mblocks[k]
            gi = GTB.get()
            S.dma('sp', gtb[gi][:, :, 0:n], GT.rearrange("(g j) p t -> j p g t", g=3)[j, :, :, t0:t0 + n], r=['GT'], w=[f"gtb{gi}"])
            return gi

        def wout_load(k, j):
            bi, t0, n = mblocks[k]
            xi = XJ.get()
            S.dma('sp', xj[xi][:, 0:n], XT[j, :, t0:t0 + n], r=[('XT', (j, bi))], w=[f"xj{xi}"])
            return xi

        def branch_j(k, j, gi):
            bi, t0, n = mblocks[k]; b = bi % 2
            pks = []
            for gidx, (k0, k1) in enumerate(groups):
                pk = PSA.get() if gidx < 2 else PSQ.get()
                for kc in range(k0, k1):
                    mm(psb[pk][:, 0:n], wbr[:, kc, j * 128:(j + 1) * 128], brb[b][:, kc, 0:n], kc == k0, kc == k1 - 1,
                       [('wbr', (gidx, j // 4)), f"brb{b}"], [f"ps{pk}"])
                pks.append(pk)
            m0 = MT.get(); m1 = MT.get(); ci = C2.get()
            tt(mt[m0][:, 0:n], psb[pks[0]][:, 0:n], gtb[gi][:, 0, 0:n], ALU.mult, [f"ps{pks[0]}", f"gtb{gi}"], [f"mt{m0}"])
            tt(mt[m1][:, 0:n], psb[pks[1]][:, 0:n], gtb[gi][:, 1, 0:n], ALU.mult, [f"ps{pks[1]}", f"gtb{gi}"], [f"mt{m1}"])
            act(c2[ci][:, 0:n], psb[pks[2]][:, 0:n], AF.Copy, [f"ps{pks[2]}"], [f"c2_{ci}"])
            tt(c2[ci][:, 0:n], c2[ci][:, 0:n], gtb[gi][:, 2, 0:n], ALU.mult, [f"c2_{ci}", f"gtb{gi}"], [f"c2_{ci}"], eng='pool')
            tt(mt[m0][:, 0:n], mt[m0][:, 0:n], mt[m1][:, 0:n], ALU.add, [f"mt{m0}", f"mt{m1}"], [f"mt{m0}"])
            tt(yT[b][:, j, 0:n], mt[m0][:, 0:n], c2[ci][:, 0:n], ALU.add, [f"mt{m0}", f"c2_{ci}"], [(f"yT{b}", j)], eng='pool')

        def wout_j(k, j, xi):
            bi, t0, n = mblocks[k]; b = bi % 2
            wch = 0 if t0 < L else 1
            pk = PSQ.get()
            for kc in range(KC):
                mm(psb[pk][:, 0:n], wo_sb[:, kc, j * 128:(j + 1) * 128], yT[b][:, kc, 0:n], kc == 0, kc == KC - 1,
                   [('wo_sb', (kc // 4, j // 4)), (f"yT{b}", kc)], [f"ps{pk}"])
            stt(xj[xi][:, 0:n], psb[pk][:, 0:n], modcol(2, j, wch), xj[xi][:, 0:n], ALU.mult, ALU.add,
                [f"ps{pk}", modkey(), f"xj{xi}"], [f"xj{xi}"])
            S.dma('sp', XT[j, :, t0:t0 + n], xj[xi][:, 0:n], r=[f"xj{xi}"], w=[('XT', (j, bi))])

        load_brb(0)
        if len(mblocks) > 1: load_brb(1)
        steps = [('b', 0, j) for j in range(8)]
        for k in range(len(mblocks)):
            for j in range(8):
                if k + 1 < len(mblocks): steps.append(('b', k + 1, j))
                steps.append(('w', k, j))
        loaded = {}

        def issue_load(idx):
            kind, k, j = steps[idx]
            loaded[idx] = branch_load(k, j) if kind == 'b' else wout_load(k, j)
        nb = {'b': 0, 'w': 0}
        pend = []
        for idx in range(len(steps)):
            while pend and False: pass
            la = idx
            while la < len(steps) and la <= idx + 3:
                if la not in loaded:
                    kind = steps[la][0]
                    inflight = sum(1 for q in loaded if q >= idx and steps[q][0] == kind)
                    if inflight < 2: issue_load(la)
                    else: break
                la += 1
            kind, k, j = steps[idx]
            if kind == 'b' and j == 0 and k + 1 < len(mblocks) and k >= 1: load_brb(k + 1)
            if kind == 'b': branch_j(k, j, loaded[idx])
            else: wout_j(k, j, loaded[idx])
        S.barrier(); sc.close()

    SIL = Rot([0, 1])

    def ffn_phase(l, last):
        S.barrier(pool=True)
        sc = Scope(nc); sb = sc.sb
        w2_sb = sb("w2_sb", [128, FC, D], BF16)
        acb = [sb(f"acb{i}", [128, FC, 512], BF16) for i in range(2)]
        sil = [sb(f"sil{i}", [128, 512], F32) for i in range(2)]
        xj = [sb(f"xj{i}", [128, 512], F32) for i in range(2)]; XJ = Rot([0, 1])
        wsrc = w_f1[l].rearrange("(kc p) n -> p kc n", p=128)
        nblk = BLOCKS[:4] if last else BLOCKS
        w2_loads = [(h, cbh) for h in range(0, FC, 2) for cbh in range(2)]

        def w2_load(h, cbh):
            cs = slice(cbh * 512, cbh * 512 + 512)
            S.dma('pool', w2_sb[:, h:h + 2, cs], w_f2[l].rearrange("(kc p) n -> p kc n", p=128)[:, h:h + 2, cs], w=[('w2_sb', (h, cbh))])
        for wt in range(FC // 2):
            if wt >= 2:
                for _ in range(3):
                    if w2_loads: w2_load(*w2_loads.pop(0))
            i = Wrot.get()
            Wv = Wb[i][:, 0:8 * 512].rearrange("p (a b) -> p a b", a=8)
            S.dma('pool', Wv[:, :, 0:256], wsrc[:, :, wt * 256:wt * 256 + 256], w=[f"W{i}"])
            S.dma('pool', Wv[:, :, 256:512], wsrc[:, :, DFF + wt * 256:DFF + wt * 256 + 256], w=[f"W{i}"])
            for jj in range(2):
                fcx = wt * 2 + jj
                for bi, (t0, n) in enumerate(nblk):
                    pa = PSA.get(); pb_ = PSQ.get()
                    for kc in range(KC):
                        mm(psb[pa][:, 0:n], Wv[:, kc, jj * 128:(jj + 1) * 128], hT[:, kc, t0:t0 + n], kc == 0, kc == KC - 1,
                           [f"W{i}", ('hT', (kc, bi))], [f"ps{pa}"])
                    for kc in range(KC):
                        mm(psb[pb_][:, 0:n], Wv[:, kc, 256 + jj * 128:256 + (jj + 1) * 128], hT[:, kc, t0:t0 + n], kc == 0, kc == KC - 1,
                           [f"W{i}", ('hT', (kc, bi))], [f"ps{pb_}"])
                    si = SIL.get()
                    act(sil[si][:, 0:n], psb[pa][:, 0:n], AF.Silu, [f"ps{pa}"], [f"sil{si}"])
                    bi2 = STB.get()
                    tt(stb[bi2][:, 0:n], psb[pb_][:, 0:n], sil[si][:, 0:n], ALU.mult, [f"ps{pb_}", f"sil{si}"], [f"stb{bi2}"])
                    S.dma('sp', ACTS[fcx, :, t0:t0 + n], stb[bi2][:, 0:n], r=[f"stb{bi2}"], w=[('ACTS', (fcx, bi))])
        while w2_loads: w2_load(*w2_loads.pop(0))

        def load_acb(k):
            t0_, n_ = nblk[k]
            for hh in range(2):
                S.dma('sp', acb[k % 2][:, hh * 11:hh * 11 + 11, 0:n_], ACTS.rearrange("c p t -> p c t")[:, hh * 11:hh * 11 + 11, t0_:t0_ + n_],
                      r=['ACTS'], w=[(f"acb{k % 2}", hh)])
        load_acb(0)
        for bi, (t0, n) in enumerate(nblk):
            wch = 0 if t0 < L else 1
            b = bi % 2
            if bi + 1 < len(nblk): load_acb(bi + 1)
            def xload(j_):
                xi_ = XJ.get()
                S.dma('sp', xj[xi_][:, 0:n], XT[j_, :, t0:t0 + n], r=[('XT', (j_, bi))], w=[f"xj{xi_}"])
                return xi_
            xnext = xload(0)
            for j in range(8):
                xi = xnext
                if j + 1 < 8: xnext = xload(j + 1)
                pk = PSA.get()
                for kc in range(FC):
                    mm(psb[pk][:, 0:n], w2_sb[:, kc, j * 128:(j + 1) * 128], acb[b][:, kc, 0:n], kc == 0, kc == FC - 1,
                       ['w2_sb', (f"acb{b}", kc // 11)], [f"ps{pk}"])
                stt(xj[xi][:, 0:n], psb[pk][:, 0:n], modcol(5, j, wch), xj[xi][:, 0:n], ALU.mult, ALU.add,
                    [f"ps{pk}", modkey(), f"xj{xi}"], [f"xj{xi}"])
                S.dma('sp', XT[j, :, t0:t0 + n], xj[xi][:, 0:n], r=[f"xj{xi}"], w=[('XT', (j, bi))])
        S.barrier(); sc.close()

    def run_layers():
        for l in range(depth):
            last = (l == depth - 1) and depth == DEPTH
            CUR['l'] = l
            ada_next = adaln_steps(l + 1) if l + 1 < depth else None
            norm_phase(a1s[l % 2], f"a1_{l % 2}", 0)
            if stop_after == ('norm1', l): return
            asc = open_attn_scope()
            in_proj(l)
            S.barrier()
            if l == 0:
                if 'qT' in taps: tap('qT', A['qT'][:], [128, 4, T], BF16, ['qT'])
                if 'VE' in taps: tap('VE', A['VE'][:], [128, NT, 2, 192], BF16, ['VE'])
                if 'hT' in taps: tap('hT', hT[:], [128, KC, T], BF16, ['hT'])
                S.barrier()
            if stop_after == ('inproj', l):
                asc.close(); return
            attention(l, ada_next)
            if ada_next is not None:
                for _ in ada_next: pass
            S.barrier(); asc.close()
            if stop_after == ('attn', l): return
            pool_phase(l)
            if stop_after == ('pool', l): return
            hgrn_phase(l)
            if stop_after == ('hgrn', l): return
            merge_phase(l)
            if stop_after == ('merge', l): return
            norm_phase(a2s[l % 2], f"a2_{l % 2}", 3)
            ffn_phase(l, last)

    run_layers()
    if 'modT' in taps: tap('modT', modTs[0][:], [128, 48, 2], F32, ['modT0'])
    if 'cosT' in taps: tap('cosT', cosT[:], [128, L], F32, ['cosT'])
    if 'sinT' in taps: tap('sinT', sinT[:], [128, L], F32, ['sinT'])
    for nm, src in (('XT', XT), ('U', U), ('HQ', HQ), ('Z', Z), ('HGATE', HGATE), ('VH', VH), ('GT', GT), ('BR', BR), ('ACTS', ACTS)):
        if nm in taps:
            t_ = dram("tap_" + nm, list(src.shape), src.dtype, "ExternalOutput")
            tap_d[nm] = t_
            S.dma('sp', t_, src, r=[nm])

    scf = Scope(nc)
    xs = [scf.sb(f"xs{i}", [128, D], F32) for i in range(2)]
    xs2 = [scf.sb(f"xt2_{i}", [128, KC, 128], F32) for i in range(2)]
    def oload(i):
        S.dma('sp', xs2[i % 2][:], XTv[:, :, i * 128:(i + 1) * 128], r=['XT'], w=[f"xt2_{i % 2}"])
    oload(0)
    for i in range(16):
        b = i % 2
        for hh in range(2):
            pk = 2 * b + hh
            for q in range(4):
                kc = hh * 4 + q
                tr(psb[pk][:, q * 128:(q + 1) * 128], xs2[b][:, kc, :], ident_f[:], [f"xt2_{b}", 'ident_f'], [f"ps{pk}"])
            cp(xs[b][:, hh * 512:(hh + 1) * 512], psb[pk][:], [f"ps{pk}"], [f"xs{b}"])
        if i + 1 < 16: oload(i + 1)
        S.dma('sp', out_d[i * 128:(i + 1) * 128, :], xs[b][:], r=[f"xs{b}"], w=[('out', i)])
    S.final('sp')
    return nc, list(tap_d.keys())


_IN_NAMES = ["w_ada", "b_ada", "norm1_w", "w_in", "pool_w", "pool_scale", "q_norm_w", "k_norm_w", "hg_lb_logits", "hg_norm_w",
             "w_branch_pool", "w_branch_attn", "w_branch_hg", "w_out", "norm2_w", "w_ffn_in", "w_ffn_out"]


def make_in_maps(inputs):
    f = lambda a: np.ascontiguousarray(np.asarray(a, dtype=np.float32))
    shared = {k: f(inputs[k]) for k in _IN_NAMES}
    x = f(inputs["x"]); c = f(inputs["c"]); ctx = f(inputs["ctx"]); cc = f(inputs["c_ctx"]).reshape(8, 128)
    maps = []
    for b in range(8):
        m = dict(shared)
        m["x"] = x[b]; m["ctx"] = ctx[b]; m["c"] = c[b].reshape(8, 128); m["c_ctx"] = cc
        maps.append(m)
    return maps


def kernel(**inputs):
    nc, _ = build()
    res = run_bass_kernel_spmd(nc, make_in_maps(inputs), core_ids=list(range(8)))
    return np.stack([np.asarray(r["out"], dtype=np.float32) for r in res.results], axis=0)
```

```python
import math
import numpy as np
import concourse.bass as bass
import concourse.mybir as mybir
from concourse.bass_utils import run_bass_kernel_spmd

F32 = mybir.dt.float32; BF16 = mybir.dt.bfloat16; I32 = mybir.dt.int32
AF = mybir.ActivationFunctionType; ALU = mybir.AluOpType
AX = mybir.AxisListType

D = 1024; KC = 8; L = 2048; CT = 256; T = L + CT; DEPTH = 4
NT = T // 128
BLOCKS = [(0, 512), (512, 512), (1024, 512), (1536, 512), (2048, 256)]
IN_W = 5376; DFF = 2816; FC = DFF // 128
EPS = 1e-6
UW = 2336


def ucol(t):
    return t + 8 if t < L else t + 24


class Sched:
    EPOCH = 30000

    def __init__(self, nc, same_sync=True, ndma=12):
        self.nc = nc
        self.E = {'pe': nc.tensor, 'act': nc.scalar, 'dve': nc.vector, 'pool': nc.gpsimd, 'sp': nc.sync}
        self.cnt = {e: 0 for e in self.E}
        self.esem = {e: [] for e in self.E}
        self.waited = {e: {} for e in self.E}
        self.res = {}
        self.dsems = []; self.dval = []
        self.dpool = {}; self.drr = {}
        for q in ('sp', 'pool'):
            self.dpool[q] = []
            for i in range(ndma):
                self.dpool[q].append(len(self.dsems))
                self.dsems.append(nc.alloc_semaphore(f"d_{q}_{i}")); self.dval.append(0)
            self.drr[q] = 0
        self.same_sync = same_sync

    def _wait(self, e, tok, force=False):
        if tok is None: return
        if tok[0] == 'e':
            _, x, ep, v = tok
            if x == e and (e == 'pe' or not (self.same_sync or force)): return
            cur = self.waited[e].get(('e', x), (-1, 0))
            if (ep, v) <= cur: return
            self.waited[e][('e', x)] = (ep, v)
            self.E[e].wait_ge(self.esem[x][ep], v)
        else:
            _, i, v = tok
            if v == 0 or self.waited[e].get(('d', i), 0) >= v: return
            self.waited[e][('d', i)] = v
            self.E[e].wait_ge(self.dsems[i], v)

    @staticmethod
    def _split(key):
        return key if isinstance(key, tuple) else (key, None)

    def _ents(self, key):
        name, sub = self._split(key)
        d = self.res.get(name, {})
        if sub is None: return list(d.values())
        return [d[k] for k in (sub, None) if k in d]

    def _deps(self, r, w):
        deps = []
        for k in r:
            for ent in self._ents(k):
                if ent['w'] is not None: deps.append(ent['w'])
        for k in w:
            for ent in self._ents(k):
                if ent['w'] is not None: deps.append(ent['w'])
                deps.extend(ent['r'].values())
        return deps

    def _record(self, tok, r, w):
        rk = (tok[0], tok[1])
        for k in r:
            name, sub = self._split(k)
            ent = self.res.setdefault(name, {}).setdefault(sub, {'w': None, 'r': {}})
            ent['r'][rk] = tok
        for k in w:
            name, sub = self._split(k)
            if sub is None: self.res[name] = {None: {'w': tok, 'r': {}}}
            else: self.res.setdefault(name, {})[sub] = {'w': tok, 'r': {}}

    def op(self, e, fn, r=(), w=()):
        for tok in self._deps(r, w): self._wait(e, tok)
        inst = fn(self.E[e])
        k = self.cnt[e]; self.cnt[e] += 1
        ep, v = divmod(k, self.EPOCH); v += 1
        while len(self.esem[e]) <= ep:
            self.esem[e].append(self.nc.alloc_semaphore(f"s_{e}_{len(self.esem[e])}"))
        inst.then_inc(self.esem[e][ep], 1)
        tok = ('e', e, ep, v)
        self._record(tok, r, w)
        return tok

    def dma(self, q, out, in_, r=(), w=(), **kw):
        pool = self.dpool[q]; i = pool[self.drr[q] % len(pool)]; self.drr[q] += 1
        self._wait(q, ('d', i, self.dval[i]))
        for tok in self._deps(r, w): self._wait(q, tok)
        inst = self.E[q].dma_start(out=out, in_=in_, **kw)
        self.dval[i] += 16
        inst.then_inc(self.dsems[i], 16)
        tok = ('d', i, self.dval[i])
        self._record(tok, r, w)
        return tok

    def last_tok(self, e):
        k = self.cnt[e] - 1
        if k < 0: return None
        ep, v = divmod(k, self.EPOCH)
        return ('e', e, ep, v + 1)

    def barrier(self, pool=False):
        toks = [self.last_tok(e) for e in ('pe', 'act', 'dve', 'pool')]
        for i in self.dpool['sp']: toks.append(('d', i, self.dval[i]))
        for e in ('pe', 'act', 'dve', 'sp') + (('pool',) if pool else ()):
            for tok in toks: self._wait(e, tok, force=True)

    def final(self, e='sp'):
        toks = [self.last_tok(x) for x in ('pe', 'act', 'dve', 'pool')]
        toks += [('d', i, self.dval[i]) for i in range(len(self.dsems))]
        for tok in toks: self._wait(e, tok, force=True)


class Scope:
    uid = 0

    def __init__(self, nc):
        from contextlib import ExitStack
        self.nc = nc; self.es = ExitStack()

    def sb(self, name, shape, dtp):
        Scope.uid += 1
        return self.es.enter_context(self.nc.sbuf_tensor(f"{name}_{Scope.uid}", list(shape), dtp))

    def close(self):
        self.es.close()


class Rot:
    def __init__(self, items):
        self.items = items; self.i = 0

    def get(self):
        it = self.items[self.i % len(self.items)]; self.i += 1
        return it


def build(depth=DEPTH, taps=(), stop_after=None):
    nc = bass.Bass("TRN2", target_bir_lowering=False)
    S = Sched(nc)
    dram = lambda name, shape, dtp, kind="Internal": nc.dram_tensor(name, list(shape), dtp, kind=kind).ap()
    x_d = dram("x", [L, D], F32, "ExternalInput")
    ctx_d = dram("ctx", [CT, D], F32, "ExternalInput")
    c_d = dram("c", [8, 128], F32, "ExternalInput")
    cc_d = dram("c_ctx", [8, 128], F32, "ExternalInput")
    w_ada = dram("w_ada", [DEPTH, D, 6 * D], F32, "ExternalInput")
    b_ada = dram("b_ada", [DEPTH, 6 * D], F32, "ExternalInput")
    norm1_w = dram("norm1_w", [DEPTH, D], F32, "ExternalInput")
    w_in = dram("w_in", [DEPTH, D, IN_W], F32, "ExternalInput")
    pool_w = dram("pool_w", [DEPTH, 4, 64, 64], F32, "ExternalInput")
    pool_scale = dram("pool_scale", [DEPTH, 256], F32, "ExternalInput")
    q_norm_w = dram("q_norm_w", [DEPTH, 64], F32, "ExternalInput")
    k_norm_w = dram("k_norm_w", [DEPTH, 64], F32, "ExternalInput")
    hg_lb = dram("hg_lb_logits", [DEPTH, 2, 256], F32, "ExternalInput")
    hg_norm_w = dram("hg_norm_w", [DEPTH, 64], F32, "ExternalInput")
    w_bp = dram("w_branch_pool", [DEPTH, 256, D], F32, "ExternalInput")
    w_ba = dram("w_branch_attn", [DEPTH, 512, D], F32, "ExternalInput")
    w_bh = dram("w_branch_hg", [DEPTH, 256, D], F32, "ExternalInput")
    w_out = dram("w_out", [DEPTH, D, D], F32, "ExternalInput")
    norm2_w = dram("norm2_w", [DEPTH, D], F32, "ExternalInput")
    w_f1 = dram("w_ffn_in", [DEPTH, D, 2 * DFF], F32, "ExternalInput")
    w_f2 = dram("w_ffn_out", [DEPTH, DFF, D], F32, "ExternalInput")
    out_d = dram("out", [L, D], F32, "ExternalOutput")
    tap_d = {}
    XT = dram("XT", [8, 128, T], F32)
    U = dram("U", [2, 128, UW], F32)
    HQ = dram("HQ", [2, 128, T], F32)
    Z = dram("Z", [4, 128, T], F32)
    HGATE = dram("HGATE", [2, 128, T], BF16)
    VH = dram("VH", [T, 256], BF16)
    GT = dram("GT", [24, 128, T], BF16)
    BR = dram("BR", [8, 128, T], BF16)
    ACTS = dram("ACTS", [FC, 128, T], BF16)

    def sb(name, shape, dtp):
        return nc.alloc_sbuf_tensor(name, list(shape), dtp)

    Wb = [sb(f"W{i}", [128, 4096], BF16) for i in range(3)]
    Wrot = Rot([0, 1, 2])
    hT = sb("hT", [128, KC, T], BF16)
    ident_f = sb("ident_f", [128, 128], F32); ident_b = sb("ident_b", [128, 128], BF16)
    ones_b = sb("ones_b", [128, 128], BF16); blk_b = sb("blk_b", [128, 128], BF16)
    onesf = sb("onesf", [128, 128], F32); Rt = sb("Rt", [128, 128], BF16)
    cosT = sb("cosT", [128, L], F32); sinT = sb("sinT", [128, L], F32)
    maskF = sb("maskF", [128, 128], BF16); maskB = sb("maskB", [128, 128], BF16)
    Emask = sb("Emask", [128, 8], BF16); Erev = sb("Erev", [128, 8], BF16)
    m01 = sb("m01", [128, T], BF16)
    cst = sb("cst", [128, 128], F32); bada = sb("bada", [128, 192], F32)
    lbT = sb("lbT", [128, 16], F32); omlb = sb("omlb", [128, 16], F32); nomlb = sb("nomlb", [128, 16], F32)
    cT = sb("cT", [128, 16], BF16)
    modTs = [sb(f"modT{i}", [128, 48, 2], F32) for i in range(2)]
    a1s = [sb(f"a1_{i}", [128, 8, 2], F32) for i in range(2)]; a2s = [sb(f"a2_{i}", [128, 8, 2], F32) for i in range(2)]
    CUR = {'l': 0}
    rcE = sb("rcE", [128, 2, 16], F32)
    pwbd = sb("pwbd", [128, 2, 128], BF16)
    psb = [nc.alloc_psum_tensor(f"ps{i}", [128, 512], F32) for i in range(8)]
    stf = [sb(f"stf{i}", [128, 512], F32) for i in range(2)]; STF = Rot([0, 1])
    stb = [sb(f"stb{i}", [128, 512], BF16) for i in range(3)]; STB = Rot([0, 1, 2])

    V = lambda e: e

    def act(out, in_, func, r, w, **kw):
        return S.op('act', lambda e: e.activation(out=out, in_=in_, func=func, **kw), r, w)

    def tt(out, in0, in1, op, r, w, eng='dve'):
        return S.op(eng, lambda e: e.tensor_tensor(out=out, in0=in0, in1=in1, op=op), r, w)

    def ts(out, in0, s1, s2, op0, op1, r, w, eng='dve'):
        if op1 is None:
            return S.op(eng, lambda e: e.tensor_scalar(out=out, in0=in0, scalar1=s1, scalar2=None, op0=op0), r, w)
        return S.op(eng, lambda e: e.tensor_scalar(out=out, in0=in0, scalar1=s1, scalar2=s2, op0=op0, op1=op1), r, w)

    def stt(out, in0, scalar, in1, op0, op1, r, w):
        return S.op('dve', lambda e: e.scalar_tensor_tensor(out=out, in0=in0, scalar=scalar, in1=in1, op0=op0, op1=op1), r, w)

    def cp(out, in_, r, w, eng='dve'):
        return S.op(eng, lambda e: e.tensor_copy(out=out, in_=in_), r, w)

    def mm(out, lhsT, rhs, start, stop, r, w):
        return S.op('pe', lambda e: e.matmul(out, lhsT=lhsT, rhs=rhs, start=start, stop=stop), r, w)

    def tr(out, in_, ident, r, w):
        return S.op('pe', lambda e: e.transpose(out, in_, ident), r, w)

    def memset(ap, val, w, eng='dve'):
        return S.op(eng, lambda e: e.memset(ap, val), (), w)

    def load_w(src, shape_free, ncols_total):
        i = Wrot.get()
        a, b = shape_free
        dst = Wb[i][:, 0:a * b].rearrange("p (a b) -> p a b", a=a)
        S.dma('pool', dst, src, w=[f"W{i}"])
        return dst, f"W{i}"

    sc0 = Scope(nc); sb0 = sb; sb = sc0.sb
    xs = [sb(f"xs{i}", [128, D], F32) for i in range(2)]
    xs2 = [sb(f"xt2_{i}", [128, KC, 128], F32) for i in range(2)]
    iot = sb("iot", [128, 128], I32); pI = sb("pI", [128, 1], I32); cI = sb("cI", [128, 128], I32)
    pc = sb("pc", [128, 1], I32); ccI = sb("ccI", [128, 128], I32)
    tmpA = sb("tmpA", [128, 128], F32); tmpB = sb("tmpB", [128, 128], F32)
    S.op('pool', lambda e: e.iota(iot[:], pattern=[[1, 128]], base=0, channel_multiplier=-1), w=['iot'])
    S.op('pool', lambda e: e.iota(pI[:], pattern=[[0, 1]], base=0, channel_multiplier=1), w=['pI'])
    S.op('pool', lambda e: e.iota(cI[:], pattern=[[1, 128]], base=0, channel_multiplier=0), w=['cI'])
    ts(ident_f[:], iot[:], 0.0, None, ALU.is_equal, None, ['iot'], ['ident_f'])
    cp(ident_b[:], ident_f[:], ['ident_f'], ['ident_b'])
    memset(ones_b[:], 1.0, ['ones_b']); memset(onesf[:], 1.0, ['onesf'])
    memset(blk_b[:], 0.0, ['blk_b'])
    memset(blk_b[0:64, 0:64], 1.0, ['blk_b']); memset(blk_b[64:128, 64:128], 1.0, ['blk_b'])
    ts(tmpA[:], iot[:], 16.0, None, ALU.is_equal, None, ['iot'], ['tmpA'])
    ts(tmpB[:], iot[:], -16.0, None, ALU.is_equal, None, ['iot'], ['tmpB'])
    for b in range(4):
        memset(tmpA[:, 32 * b:32 * b + 16], 0.0, ['tmpA'])
        memset(tmpB[:, 32 * b + 16:32 * b + 32], 0.0, ['tmpB'])
    tt(Rt[:], tmpA[:], tmpB[:], ALU.subtract, ['tmpA', 'tmpB'], ['Rt'])
    ts(pc[:], pI[:], 4, None, ALU.arith_shift_right, None, ['pI'], ['pc'])
    ts(ccI[:], cI[:], 4, None, ALU.arith_shift_right, None, ['cI'], ['ccI'])
    tt(tmpA[:], ccI[:], pc[:, 0:1].broadcast_to([128, 128]), ALU.is_equal, ['ccI', 'pc'], ['tmpA'])
    ts(tmpB[:], iot[:], 0.0, None, ALU.is_ge, None, ['iot'], ['tmpB'])
    tt(maskF[:], tmpA[:], tmpB[:], ALU.mult, ['tmpA', 'tmpB'], ['maskF'])
    ts(tmpB[:], iot[:], 0.0, None, ALU.is_le, None, ['iot', 'maskF'], ['tmpB'])
    tt(maskB[:], tmpA[:], tmpB[:], ALU.mult, ['tmpA', 'tmpB'], ['maskB'])
    tt(Emask[:], cI[:, 0:8], pc[:, 0:1].broadcast_to([128, 8]), ALU.is_equal, ['cI', 'pc'], ['Emask'])
    ts(tmpB[:, 0:8], cI[:, 0:8], -1.0, 7.0, ALU.mult, ALU.add, ['cI', 'maskB'], ['tmpB'])
    tt(Erev[:], tmpB[:, 0:8], pc[:, 0:1].broadcast_to([128, 8]), ALU.is_equal, ['tmpB', 'pc'], ['Erev'])
    big_i = sb("big_i", [128, T], I32); big_f = sb("big_f", [128, T], F32)
    big_g = sb("big_g", [128, T], F32); big_h = sb("big_h", [128, T], F32)
    S.op('pool', lambda e: e.iota(big_i[:], pattern=[[1, T]], base=0, channel_multiplier=0), w=['big_i'])
    ts(big_i[:], big_i[:], 15, None, ALU.bitwise_and, None, ['big_i'], ['big_i'])
    ts(m01[:], big_i[:], 0.0, None, ALU.is_gt, None, ['big_i'], ['m01'])
    freq = sb("freq", [128, 1], F32); jI = sb("jI", [128, 1], I32)
    ts(jI[:], pI[:], 15, None, ALU.bitwise_and, None, ['pI'], ['jI'])
    cp(freq[:], jI[:], ['jI'], ['freq'])
    act(freq[:], freq[:], AF.Exp, ['freq'], ['freq'], scale=-math.log(10000.0) / 16.0)
    for q in range(4):
        pat = [[1, 32], [0, 64]] if q % 2 == 0 else [[0, 32], [1, 64]]
        S.op('pool', lambda e, q=q, pat=pat: e.iota(big_i[32 * q:32 * q + 32, 0:L], pattern=pat, base=0, channel_multiplier=0),
             r=['m01'], w=['big_i'])
    ts(big_f[:, 0:L], big_i[:, 0:L], freq[:, 0:1], None, ALU.mult, None, ['big_i', 'freq'], ['big_f'])
    TWO_PI = 2.0 * math.pi

    def sin_of(dst, shift, dkey):
        ts(big_g[:, 0:L], big_f[:, 0:L], shift, None, ALU.add, None, ['big_f'], ['big_g'])
        ki = big_i[:, 0:L]
        ts(ki, big_g[:, 0:L], 1.0 / TWO_PI, None, ALU.mult, None, ['big_g'], ['big_i'])
        cp(big_h[:, 0:L], ki, ['big_i'], ['big_h'])
        stt(big_g[:, 0:L], big_h[:, 0:L], -TWO_PI, big_g[:, 0:L], ALU.mult, ALU.add, ['big_h', 'big_g'], ['big_g'])
        ts(big_h[:, 0:L], big_g[:, 0:L], math.pi, -TWO_PI, ALU.is_gt, ALU.mult, ['big_g'], ['big_h'])
        tt(big_g[:, 0:L], big_g[:, 0:L], big_h[:, 0:L], ALU.add, ['big_g', 'big_h'], ['big_g'])
        ts(big_h[:, 0:L], big_g[:, 0:L], -math.pi, TWO_PI, ALU.is_lt, ALU.mult, ['big_g'], ['big_h'])
        tt(big_g[:, 0:L], big_g[:, 0:L], big_h[:, 0:L], ALU.add, ['big_g', 'big_h'], ['big_g'])
        ts(big_g[:, 0:L], big_g[:, 0:L], -3.141592, 3.141592, ALU.max, ALU.min, ['big_g'], ['big_g'])
        act(dst, big_g[:, 0:L], AF.Sin, ['big_g'], [dkey])

    sin_of(sinT[:], 0.0, 'sinT')
    sin_of(cosT[:], math.pi / 2.0, 'cosT')
    for ch in range(2):
        for half in range(2):
            w = [2, 4, 8, 16][2 * ch + half]
            rows = slice(64 * half, 64 * half + 64)
            memset(rcE[rows, ch, :], 1.0 / w, ['rcE'], eng='pool')
            for t in range(w // 2):
                memset(rcE[rows, ch, t:t + 1], 1.0 / (t + w // 2), ['rcE'], eng='pool')
            for i in range(8):
                if (8 - i) < w // 2:
                    memset(rcE[rows, ch, 8 + i:9 + i], 1.0 / ((8 - i) + w // 2), ['rcE'], eng='pool')
    memset(pwbd[:], 0.0, ['pwbd'])
    memset(big_h[:], 0.0, ['big_h'])
    if True:
        for ch in range(2):
            S.dma('sp', U[ch, :, 0:T], big_h[:, 0:T], r=['big_h'], w=[('U', ch)])
            S.dma('sp', U[ch, :, T:UW], big_h[:, 0:UW - T], r=['big_h'], w=[('U', ch)])

    stg = sb("stg", [128, 128], F32)
    for half in range(2):
        memset(stg[:], 0.0, ['stg'])
        S.dma('sp', stg[0:96, :], b_ada.rearrange("l (j p) -> (l j) p", p=128)[96 * half:96 * half + 96, :], w=['stg'])
        tr(psb[0][:, 0:128], stg[:], ident_f[:], ['stg', 'ident_f'], ['ps0'])
        cp(bada[:, 96 * half:96 * half + 96], psb[0][:, 0:96], ['ps0'], ['bada'])
    memset(stg[:], 0.0, ['stg'])
    S.dma('sp', stg[0:32, :], norm1_w.rearrange("l (k p) -> (l k) p", p=128), w=['stg'])
    S.dma('sp', stg[32:64, :], norm2_w.rearrange("l (k p) -> (l k) p", p=128), w=['stg'])
    S.dma('sp', stg[64:72, :], pool_scale.rearrange("l (k p) -> (l k) p", p=128), w=['stg'])
    S.dma('sp', stg[72:88, :], hg_lb.rearrange("l d (k p) -> (l d k) p", p=128), w=['stg'])
    for (r0, src) in ((88, q_norm_w), (92, k_norm_w), (96, hg_norm_w)):
        S.dma('sp', stg[r0:r0 + 4, 0:64], src[:, :], w=['stg'])
        S.dma('sp', stg[r0:r0 + 4, 64:128], src[:, :], w=['stg'])
    S.dma('sp', stg[100:108, :], c_d[:, :], w=['stg'])
    S.dma('sp', stg[108:116, :], cc_d[:, :], w=['stg'])
    tr(psb[0][:, 0:128], stg[:], ident_f[:], ['stg', 'ident_f'], ['ps0'])
    cp(cst[:], psb[0][:, 0:128], ['ps0'], ['cst'])
    C_N1, C_N2, C_PS, C_LB, C_QN, C_KN, C_HN, C_C = 0, 32, 64, 72, 88, 92, 96, 100
    act(cT[:], cst[:, C_C:C_C + 16], AF.Silu, ['cst'], ['cT'])
    ex = sb("ex", [128, 16], F32); ssum = sb("ssum", [128, 4], F32)
    act(ex[:], cst[:, C_LB:C_LB + 16], AF.Exp, ['cst'], ['ex'])
    exv = ex[:].rearrange("p (l m) -> p l m", l=4)
    tt(ssum[:], exv[:, 0, :], exv[:, 1, :], ALU.add, ['ex'], ['ssum'])
    tt(ssum[:], ssum[:], exv[:, 2, :], ALU.add, ['ex', 'ssum'], ['ssum'])
    tt(ssum[:], ssum[:], exv[:, 3, :], ALU.add, ['ex', 'ssum'], ['ssum'])
    S.op('dve', lambda e: e.reciprocal(out=ssum[:], in_=ssum[:]), ['ssum'], ['ssum'])
    lbv = lbT[:].rearrange("p (l m) -> p l m", l=4)
    memset(lbT[:], 0.0, ['lbT'])
    tt(lbv[:, 1, :], exv[:, 1, :], ssum[:], ALU.mult, ['ex', 'ssum'], ['lbT'])
    tt(ex[:, 8:12], ex[:, 4:8], ex[:, 8:12], ALU.add, ['ex', 'lbT'], ['ex'])
    tt(lbv[:, 2, :], exv[:, 2, :], ssum[:], ALU.mult, ['ex', 'ssum'], ['lbT'])
    tt(ex[:, 12:16], ex[:, 8:12], ex[:, 12:16], ALU.add, ['ex', 'lbT'], ['ex'])
    tt(lbv[:, 3, :], exv[:, 3, :], ssum[:], ALU.mult, ['ex', 'ssum'], ['lbT'])
    ts(omlb[:], lbT[:], -1.0, 1.0, ALU.mult, ALU.add, ['lbT'], ['omlb'])
    ts(nomlb[:], omlb[:], -1.0, None, ALU.mult, None, ['omlb'], ['nomlb'])

    def adaln_steps(l):
        pk = 7; p = l % 2
        modT = modTs[p]
        wsrc = w_ada[l].rearrange("(kc p) n -> p kc n", p=128)
        nxt = load_w(wsrc[:, :, 0:512], (8, 512), 512)
        yield
        for wt in range(12):
            Wv, wk = nxt
            if wt + 1 < 12:
                nxt = load_w(wsrc[:, :, (wt + 1) * 512:(wt + 2) * 512], (8, 512), 512)
            for jj in range(4):
                j = wt * 4 + jj
                for kc in range(KC):
                    mm(psb[pk][:, 2 * j:2 * j + 2], Wv[:, kc, jj * 128:(jj + 1) * 128],
                       cT[:].rearrange("p (w k) -> p k w", w=2)[:, kc, :], kc == 0, kc == KC - 1, [wk, 'cT'], [f"ps{pk}"])
            yield
        tt(modT[:], psb[pk][:, 0:96].rearrange("p (j w) -> p j w", w=2),
           bada[:, l * 48:(l + 1) * 48].unsqueeze(2).broadcast_to([128, 48, 2]), ALU.add, [f"ps{pk}", 'bada'], [f"modT{p}"])
        for (dst, dkey, scc, ncol) in ((a1s[p], f"a1_{p}", 8, C_N1), (a2s[p], f"a2_{p}", 32, C_N2)):
            for wch in range(2):
                stt(dst[:, :, wch], modT[:, scc:scc + 8, wch], 1.0, cst[:, ncol + l * 8:ncol + l * 8 + 8], ALU.add, ALU.mult,
                    [f"modT{p}", 'cst'], [dkey])
        yield

    def modcol(idx, kc, wch):
        return modTs[CUR['l'] % 2][:, idx * 8 + kc, wch:wch + 1]

    def modkey():
        return f"modT{CUR['l'] % 2}"

    def xload(i):
        src = x_d[i * 128:(i + 1) * 128, :] if i < 16 else ctx_d[(i - 16) * 128:(i - 15) * 128, :]
        S.dma('sp', xs[i % 2][:], src, w=[f"xs{i % 2}"])
    ada0 = adaln_steps(0)
    xload(0)
    for i in range(NT):
        b = i % 2
        next(ada0, None)
        for hh in range(2):
            pk = 2 * b + hh
            for q in range(4):
                kc = hh * 4 + q
                tr(psb[pk][:, q * 128:(q + 1) * 128], xs[b][:, kc * 128:(kc + 1) * 128], ident_f[:], [f"xs{b}", 'ident_f'], [f"ps{pk}"])
            cp(xs2[b][:, hh * 4:hh * 4 + 4, :], psb[pk][:].rearrange("p (q t) -> p q t", q=4), [f"ps{pk}"], [f"xt2_{b}"],
               eng='dve' if hh == 0 else 'dve')
        if i + 1 < NT: xload(i + 1)
        S.dma('sp', XT.rearrange("c p t -> p c t")[:, :, i * 128:(i + 1) * 128], xs2[b][:], r=[f"xt2_{b}"], w=[('XT', ('ld', i))])
    for _ in ada0: pass
    S.barrier()
    sc0.close(); sb = sb0

    XTv = XT.rearrange("c p t -> p c t")
    PSA = Rot([0, 1, 2, 3])

    def norm_phase(a_t, akey, sh_idx):
        sc = Scope(nc); sb = sc.sb
        xblk = [sb(f"xblk{i}", [128, KC, 512], F32) for i in range(2)]
        sqb = [sb(f"sqb{i}", [128, KC, 512], BF16) for i in range(2)]
        rstd_b = [sb(f"rstd{i}", [128, 512], F32) for i in range(2)]
        lnv_b = [sb(f"lnv{i}", [128, 512], F32) for i in range(2)]
        ntmp = [sb(f"ntmp{i}", [128, 512], F32) for i in range(8)]

        def stats(bi):
            t0, n = BLOCKS[bi]; b = bi % 2
            S.dma('sp', xblk[b][:, :, 0:n], XTv[:, :, t0:t0 + n], r=['XT'], w=[f"xblk{b}"])
            act(sqb[b][:, :, 0:n], xblk[b][:, :, 0:n], AF.Square, [f"xblk{b}"], [f"sqb{b}"])
            pk = PSA.get()
            for kc in range(KC):
                mm(psb[pk][:, 0:n], ones_b[:], sqb[b][:, kc, 0:n], kc == 0, kc == KC - 1, ['ones_b', f"sqb{b}"], [f"ps{pk}"])
            act(lnv_b[b][:, 0:n], psb[pk][:, 0:n], AF.Ln, [f"ps{pk}"], [f"lnv{b}"], scale=1.0 / D, bias=EPS)
            act(rstd_b[b][:, 0:n], lnv_b[b][:, 0:n], AF.Exp, [f"lnv{b}"], [f"rstd{b}"], scale=-0.5)

        def apply(bi):
            t0, n = BLOCKS[bi]; b = bi % 2
            wch = 0 if t0 < L else 1
            for kc in range(KC):
                stt(ntmp[kc][:, 0:n], xblk[b][:, kc, 0:n], a_t[:, kc, wch:wch + 1], rstd_b[b][:, 0:n], ALU.mult, ALU.mult,
                    [f"xblk{b}", f"rstd{b}", akey], [f"ntmp{kc}"])
            for kc in range(KC):
                if kc % 2 == 0:
                    act(hT[:, kc, t0:t0 + n], ntmp[kc][:, 0:n], AF.Identity, [f"ntmp{kc}", modkey()], [('hT', (kc, bi))],
                        bias=modcol(sh_idx, kc, wch), scale=1.0)
                else:
                    ts(hT[:, kc, t0:t0 + n], ntmp[kc][:, 0:n], modcol(sh_idx, kc, wch), None, ALU.add, None,
                       [f"ntmp{kc}", modkey()], [('hT', (kc, bi))])
        stats(0)
        for bi in range(len(BLOCKS)):
            if bi + 1 < len(BLOCKS): stats(bi + 1)
            apply(bi)
        S.barrier(); sc.close()

    def tap(name, src_ap, shape, dtp, r):
        if name not in taps: return
        t_ = dram("tap_" + name, shape, dtp, "ExternalOutput")
        tap_d[name] = t_
        S.dma('sp', t_, src_ap, r=r)

    qraw = [sb(f"qraw{i}", [128, 512], F32) for i in range(2)]
    qsq = [sb(f"qsq{i}", [128, 512], BF16) for i in range(2)]
    qn_b = [sb(f"qn{i}", [128, 512], BF16) for i in range(2)]
    qt1 = [sb(f"qt1_{i}", [128, 512], F32) for i in range(2)]
    qt2 = [sb(f"qt2_{i}", [128, 512], F32) for i in range(2)]
    QR = Rot([0, 1])
    A = {}

    def open_attn_scope():
        sc = Scope(nc); sb = sc.sb
        A['qT'] = sb("qT", [128, 4, T], BF16)
        A['kTz'] = sb("kTz", [128, 2, 2, T], BF16)
        A['aT'] = sb("aT", [128, 4, T], BF16)
        A['VE'] = sb("VE", [128, NT, 2, 192], BF16)
        A['pT'] = [sb(f"pT{i}", [128, 512], BF16) for i in range(4)]
        A['rden'] = sb("rden", [128, 512], F32); A['rbc'] = sb("rbc", [128, 512], F32)
        A['tmst'] = [sb(f"tmst{i}", [128, 4, 128], BF16) for i in range(2)]
        memset(A['VE'][:], 1.0, ['VE'])
        memset(A['kTz'][0:64, :, 1, :], 0.0, ['kTz'])
        return sc
    PT = Rot([0, 1, 2, 3]); TMST = Rot([0, 1])

    def qk_epilogue(pk, n, bi, t0, dest, dkey, wcol):
        latent = t0 < L
        i = QR.get()
        act(qn_b[i][:, 0:n], psb[pk][:, 0:n], AF.Copy, [f"ps{pk}", 'cst'], [f"qn{i}"], scale=wcol)
        act(qsq[i][:, 0:n], psb[pk][:, 0:n], AF.Square, [f"ps{pk}"], [f"qsq{i}"])

        def part2():
            p2 = PSQ.get()
            mm(psb[p2][:, 0:n], blk_b[:], qsq[i][:, 0:n], True, True, ['blk_b', f"qsq{i}"], [f"ps{p2}"])
            if latent:
                p3 = PSQ.get()
                mm(psb[p3][:, 0:n], Rt[:], qn_b[i][:, 0:n], True, True, ['Rt', f"qn{i}"], [f"ps{p3}"])
            act(qt1[i][:, 0:n], psb[p2][:, 0:n], AF.Ln, [f"ps{p2}"], [f"qt1_{i}"], scale=1.0 / 64, bias=EPS)
            act(qt1[i][:, 0:n], qt1[i][:, 0:n], AF.Exp, [f"qt1_{i}"], [f"qt1_{i}"], scale=-0.5)
            if latent:
                tt(qraw[i][:, 0:n], qn_b[i][:, 0:n], cosT[:, t0:t0 + n], ALU.mult, [f"qn{i}", 'cosT'], [f"qraw{i}"])
                tt(qt2[i][:, 0:n], psb[p3][:, 0:n], sinT[:, t0:t0 + n], ALU.mult, [f"ps{p3}", 'sinT'], [f"qt2_{i}"])
                tt(qraw[i][:, 0:n], qraw[i][:, 0:n], qt2[i][:, 0:n], ALU.add, [f"qraw{i}", f"qt2_{i}"], [f"qraw{i}"])
                tt(dest, qraw[i][:, 0:n], qt1[i][:, 0:n], ALU.mult, [f"qraw{i}", f"qt1_{i}"], [dkey])
            else:
                tt(dest, qn_b[i][:, 0:n], qt1[i][:, 0:n], ALU.mult, [f"qn{i}", f"qt1_{i}"], [dkey])
        return part2

    PSQ = Rot([4, 5, 6])

    def evac_dram(pk, n, dst_ap, dkey, func, dtp):
        if dtp == F32:
            i = STF.get(); buf = stf[i]; bk = f"stf{i}"
        else:
            i = STB.get(); buf = stb[i]; bk = f"stb{i}"
        if func is None:
            cp(buf[:, 0:n], psb[pk][:, 0:n], [f"ps{pk}"], [bk])
        else:
            act(buf[:, 0:n], psb[pk][:, 0:n], func, [f"ps{pk}"], [bk])
        S.dma('sp', dst_ap, buf[:, 0:n], r=[bk], w=[dkey])

    DEF = {'fn': None}

    def flush_deferred():
        if DEF['fn'] is not None:
            DEF['fn'](); DEF['fn'] = None

    def fm_chunk(Wv, wk, jj, epi):
        for bi, (t0, n) in enumerate(BLOCKS):
            pk = PSA.get()
            for kc in range(KC):
                mm(psb[pk][:, 0:n], Wv[:, kc, jj * 128:(jj + 1) * 128], hT[:, kc, t0:t0 + n], kc == 0, kc == KC - 1,
                   [wk, ('hT', (kc, bi))], [f"ps{pk}"])
            flush_deferred()
            r_ = epi(pk, n, bi, t0)
            DEF['fn'] = r_ if callable(r_) else None

    def tm_chunk(Wv, wk, jj, epi):
        for g4 in range(0, NT, 4):
            pk = PSA.get()
            nt4 = min(4, NT - g4)
            for q in range(nt4):
                i = g4 + q
                bi = min(i // 4, 4)
                for kc in range(KC):
                    mm(psb[pk][:, q * 128:(q + 1) * 128], hT[:, kc, i * 128:(i + 1) * 128], Wv[:, kc, jj * 128:(jj + 1) * 128],
                       kc == 0, kc == KC - 1, [wk, ('hT', (kc, bi))], [f"ps{pk}"])
            epi(pk, g4, nt4)

    def in_proj(l):
        wsrc = w_in[l].rearrange("(kc p) n -> p kc n", p=128)
        qcol = cst[:, C_QN + l:C_QN + l + 1]; kcol = cst[:, C_KN + l:C_KN + l + 1]

        def epi_for(j):
            if j in (0, 1):
                return lambda pk, n, bi, t0: evac_dram(pk, n, U[j, :, ucol(t0):ucol(t0) + n], ('U', (j, bi)), None, F32)
            if 2 <= j <= 5:
                return lambda pk, n, bi, t0: qk_epilogue(pk, n, bi, t0, A['qT'][:, j - 2, t0:t0 + n], ('qT', (j - 2, bi)), qcol)
            if j in (8, 9):
                return lambda pk, n, bi, t0: evac_dram(pk, n, HQ[j - 8, :, t0:t0 + n], ('HQ', (j - 8, bi)), AF.Copy, F32)
            if 12 <= j <= 15:
                return lambda pk, n, bi, t0: evac_dram(pk, n, Z[j - 12, :, t0:t0 + n], ('Z', (j - 12, bi)), None, F32)
            if j in (16, 17):
                return lambda pk, n, bi, t0: evac_dram(pk, n, HGATE[j - 16, :, t0:t0 + n], ('HGATE', (j - 16, bi)), AF.Silu, BF16)
            if j >= 18:
                return lambda pk, n, bi, t0: evac_dram(pk, n, GT[j - 18, :, t0:t0 + n], ('GT', (j - 18, bi)), AF.Sigmoid, BF16)
            return None

        def epi_v(pk, g4, nt4):
            for g in range(2):
                cp(A['VE'][:, g4:g4 + nt4, g, 64:128], psb[pk][:, 0:nt4 * 128].rearrange("p (q c) -> p q c", q=nt4)[:, :, g * 64:(g + 1) * 64],
                   [f"ps{pk}"], ['VE'])

        def epi_hi(hc):
            def f(pk, g4, nt4):
                i = TMST.get()
                act(A['tmst'][i][:, 0:nt4, :], psb[pk][:, 0:nt4 * 128].rearrange("p (q c) -> p q c", q=nt4), AF.Copy, [f"ps{pk}"], [f"tmst{i}"])
                S.dma('sp', VH.rearrange("(i p) c -> p i c", p=128)[:, g4:g4 + nt4, hc * 128:(hc + 1) * 128], A['tmst'][i][:, 0:nt4, :],
                      r=[f"tmst{i}"], w=[('VH', (hc, g4))])
            return f

        i = Wrot.get()
        Wk = Wb[i][:, 0:8 * 256].rearrange("p (a b) -> p a b", a=8)
        for g in range(2):
            for dup in range(2):
                S.dma('pool', Wk[:, :, g * 128 + dup * 64:g * 128 + dup * 64 + 64], wsrc[:, :, 768 + g * 64:768 + g * 64 + 64], w=[f"W{i}"])
        for g in range(2):
            def kepi(pk, n, bi, t0, g=g):
                p2 = qk_epilogue(pk, n, bi, t0, A['kTz'][:, g, 0, t0:t0 + n], ('kTz', (g, bi)), kcol)

                def fin():
                    p2()
                    cp(A['kTz'][64:128, g, 1, t0:t0 + n], A['kTz'][64:128, g, 0, t0:t0 + n], [('kTz', (g, bi))], [('kTz', (g, bi))])
                    memset(A['kTz'][64:128, g, 0, t0:t0 + n], 0.0, [('kTz', (g, bi))])
                return fin
            fm_chunk(Wk, f"W{i}", g, kepi)
        ntile = (IN_W + 511) // 512
        for wt in range(ntile):
            c0 = wt * 512; ncl = min(512, IN_W - c0)
            Wv, wk = load_w(wsrc[:, :, c0:c0 + ncl], (8, ncl), ncl)
            for jj in range(ncl // 128):
                j = wt * 4 + jj
                if j == 6: continue
                if j == 7: tm_chunk(Wv, wk, jj, epi_v)
                elif j in (10, 11): tm_chunk(Wv, wk, jj, epi_hi(j - 10))
                else: fm_chunk(Wv, wk, jj, epi_for(j))
        flush_deferred()

    def attention(l, ada_steps=None):
        LA = 3
        for bi, (t0, n) in enumerate(BLOCKS):
            latent = t0 < L
            if not latent and l == depth - 1 and depth == DEPTH:
                continue
            ktiles = list(range(NT)) if latent else [16, 17]
            items = [(h, ii, i) for h in range(8) for ii, i in enumerate(ktiles)]
            sc_bank = {}

            def emit_score(idx):
                h, ii, i = items[idx]
                g = h // 4; ch = h // 2; pb = (h % 2) * 64
                pk = PSA.get()
                mm(psb[pk][:, 0:n], A['kTz'][:, g, h % 2, i * 128:(i + 1) * 128], A['qT'][:, ch, t0:t0 + n], True, True,
                   [('kTz', (g, min(i // 4, 4))), ('qT', (ch, bi))], [f"ps{pk}"])
                sc_bank[idx] = pk

            def epi2(h):
                ch = h // 2; pb = (h % 2) * 64; po = 4 + (h % 2); r0 = 64 if h % 2 == 0 else 0
                mm(psb[6][:, 0:n], onesf[r0:r0 + 1, :], A['rden'][r0:r0 + 1, 0:n], True, True, ['onesf', ('rden', h % 2)], ['ps6'])
                cp(A['rbc'][pb:pb + 64, 0:n], psb[6][pb:pb + 64, 0:n], ['ps6'], [('rbc', h % 2)])
                tt(A['aT'][pb:pb + 64, ch, t0:t0 + n], psb[po][pb:pb + 64, 0:n], A['rbc'][pb:pb + 64, 0:n], ALU.mult,
                   [f"ps{po}", ('rbc', h % 2)], [('aT', (ch, bi, h % 2))])

            pending = []
            for idx in range(min(LA, len(items))): emit_score(idx)
            for idx, (h, ii, i) in enumerate(items):
                g = h // 4; po = 4 + (h % 2)
                voff = 64 if h % 2 == 0 else 0
                pk = sc_bank.pop(idx)
                pi = PT.get()
                act(A['pT'][pi][:, 0:n], psb[pk][:, 0:n], AF.Exp, [f"ps{pk}"], [f"pT{pi}"], scale=0.125)
                if idx + LA < len(items): emit_score(idx + LA)
                mm(psb[po][:, 0:n], A['VE'][:, i, g, voff:voff + 128], A['pT'][pi][:, 0:n], ii == 0, ii == len(ktiles) - 1,
                   ['VE', f"pT{pi}"], [f"ps{po}"])
                if ii == len(ktiles) - 1:
                    r0 = 64 if h % 2 == 0 else 0
                    S.op('dve', lambda e, r0=r0, po=po: e.reciprocal(out=A['rden'][r0:r0 + 1, 0:n], in_=psb[po][r0:r0 + 1, 0:n]),
                         [f"ps{po}"], [('rden', h % 2)])
                    pending.append((idx + min(8, 2 * len(ktiles) - 2), h))
                while pending and pending[0][0] <= idx:
                    epi2(pending.pop(0)[1])
                if ii == 0 and ada_steps is not None:
                    next(ada_steps, None)
            while pending:
                epi2(pending.pop(0)[1])
        for ch in range(4):
            S.dma('sp', BR[2 + ch, :, :], A['aT'][:, ch, :], r=['aT'], w=[('BR', 2 + ch)])

    def pool_phase(l):
        N = UW
        sc = Scope(nc); sb = sc.sb
        upad = sb("upad", [128, 2, UW], F32)
        s2 = sb("pl_s2", [128, UW], F32); s4 = sb("pl_s4", [128, UW], F32); s8 = sb("pl_s8", [128, UW], F32)
        ypool = sb("ypool", [128, 2, T], BF16)
        yedge = sb("yedge", [128, 16], F32)
        for ch in range(2):
            S.dma('sp', upad[:, ch, :], U[ch, :, :], r=['U'], w=[('upad', ch)])
            u = upad[:, ch, :]
            tt(s2[:, 1:N], u[:, 0:N - 1], u[:, 1:N], ALU.add, [('upad', ch)], ['pl_s2'])
            tt(s4[:, 2:N - 1], s2[:, 1:N - 2], s2[:, 3:N], ALU.add, ['pl_s2'], ['pl_s4'])
            if ch == 0:
                lv = ((s2, 'pl_s2'), (s4, 'pl_s4'))
            else:
                tt(s8[:, 4:N - 3], s4[:, 2:N - 5], s4[:, 6:N - 1], ALU.add, ['pl_s4'], ['pl_s8'])
                tt(s2[:, 8:N - 7], s8[:, 4:N - 11], s8[:, 12:N - 3], ALU.add, ['pl_s8', 'pl_s2'], ['pl_s2'])
                lv = ((s8, 'pl_s8'), (s2, 'pl_s2'))
            for half in range(2):
                w = [2, 4, 8, 16][2 * ch + half]
                rows = slice(64 * half, 64 * half + 64)
                src, skey = lv[half]
                for (ts0, tn, uc) in ((0, L, 8), (L, CT, L + 24)):
                    stt(ypool[rows, ch, ts0:ts0 + tn], src[rows, uc:uc + tn], 1.0 / w, u[rows, uc:uc + tn], ALU.mult, ALU.subtract,
                        [skey, ('upad', ch)], [('ypool', ch)])
                    for (e0, tb) in ((0, 0), (tn - 8, 8)):
                        tt(yedge[rows, 0:8], src[rows, uc + e0:uc + e0 + 8], rcE[rows, ch, tb:tb + 8], ALU.mult, [skey, 'rcE'], ['yedge'])
                        tt(ypool[rows, ch, ts0 + e0:ts0 + e0 + 8], yedge[rows, 0:8], u[rows, uc + e0:uc + e0 + 8], ALU.subtract,
                           ['yedge', ('upad', ch)], [('ypool', ch)])
        for g in range(4):
            ch, half = g // 2, g % 2
            S.dma('pool', pwbd[64 * half:64 * half + 64, ch, 64 * half:64 * half + 64], pool_w[l, g, :, :], w=['pwbd'])
        for ch in range(2):
            for bi, (t0, n) in enumerate(BLOCKS):
                pk = PSA.get()
                mm(psb[pk][:, 0:n], pwbd[:, ch, :], ypool[:, ch, t0:t0 + n], True, True, ['pwbd', ('ypool', ch)], [f"ps{pk}"])
                i = STB.get()
                ts(stb[i][:, 0:n], psb[pk][:, 0:n], cst[:, C_PS + l * 2 + ch:C_PS + l * 2 + ch + 1], None, ALU.mult, None, [f"ps{pk}", 'cst'], [f"stb{i}"])
                S.dma('sp', BR[ch, :, t0:t0 + n], stb[i][:, 0:n], r=[f"stb{i}"], w=[('BR', (ch, bi))])
        S.barrier(); sc.close()

    NCH = T // 16
    VBLK = Rot([0, 1]); ATM = Rot([0, 1, 2])
    Sbf = [hT[:, 4 * d:4 * d + 4, :].rearrange("p a b -> p (a b)").rearrange("p (v n) -> p v n", v=64) for d in range(2)]

    def nat(i, n, d=0):
        if d == 0:
            return (i - 16) * 8 + n if i >= 16 else 16 + i * 8 + n
        return 128 + (i - 16) * 8 + n if i >= 16 else i * 8 + n

    def hgrn_phase(l):
        sc = Scope(nc); sb = sc.sb
        qdec = [sb(f"qdec{d}", [128, T], BF16) for d in range(2)]
        ktil = [sb(f"ktil{d}", [128, T], BF16) for d in range(2)]
        kdecTM = [sb(f"kdecTM{d}", [128, NT, 128], BF16) for d in range(2)]
        abuf = [sb(f"abuf{d}", [128, NCH], F32) for d in range(2)]
        Vh = sb("Vh", [128, NT, 256], BF16)
        S.dma('sp', Vh[:], VH.rearrange("(i p) c -> p i c", p=128), r=['VH'], w=['Vh'])
        for hp in range(2):
            sc1 = Scope(nc); sb = sc1.sb
            hz = sb("hz", [128, T], F32); hlogf = sb("hlogf", [128, T], F32)
            hG = sb("hG", [128, T], F32); hD = sb("hD", [128, T], F32)
            hE = hlogf; hq_sb = sb("hq_sb", [128, T], F32)
            kdecT = sb("kdecT", [128, T], BF16)
            hsg = hz
            S.dma('sp', hq_sb[:], HQ[hp, :, :], r=['HQ'], w=['hq_sb'])
            HALF = T // 2
            for d in range(2):
                lcol = l * 4 + d * 2 + hp
                ocol = omlb[:, lcol:lcol + 1]
                H = [(hf, slice(hf * HALF, (hf + 1) * HALF)) for hf in range(2)]
                for hf, sl in H:
                    S.dma('sp', hz[:, sl], Z[d * 2 + hp, :, sl], r=['Z'], w=[('hz', hf)])
                for hf, sl in H:
                    act(hsg[:, sl], hz[:, sl], AF.Sigmoid, [('hz', hf)], [('hz', hf)])
                for hf, sl in H:
                    ts(hlogf[:, sl], hsg[:, sl], ocol, lbT[:, lcol:lcol + 1], ALU.mult, ALU.add, [('hz', hf), 'omlb', 'lbT'], [('hlogf', hf)])
                for hf, sl in H:
                    act(hlogf[:, sl], hlogf[:, sl], AF.Ln, [('hlogf', hf)], [('hlogf', hf)])
                for hf, sl in H:
                    ts(hz[:, sl], hsg[:, sl], -1.0, 1.0, ALU.mult, ALU.add, [('hz', hf)], [('hz', hf)])
                for hf, sl in H:
                    S.op('dve', lambda e: e.tensor_tensor_scan(out=hG[:, sl], data0=m01[:, sl], data1=hlogf[:, sl], initial=0.0,
                                                              op0=ALU.mult, op1=ALU.add), ['m01', ('hlogf', hf)], [('hG', hf)])
                loff = 16 if d == 0 else 0; coff = 0 if d == 0 else 128
                act(abuf[d][:, loff:loff + 72], hG[:, 15:HALF:16], AF.Exp, [('hG', 0)], [f"abuf{d}"])
                act(abuf[d][:, loff + 72:loff + 128], hG[:, HALF + 15:L:16], AF.Exp, [('hG', 1)], [f"abuf{d}"])
                act(abuf[d][:, coff:coff + 16], hG[:, L + 15:T:16], AF.Exp, [('hG', 1)], [f"abuf{d}"])
                for hf, sl in H:
                    G3 = hG[:, sl].rearrange("p (c s) -> p c s", s=16)
                    tt(hD[:, sl].rearrange("p (c s) -> p c s", s=16), G3[:, :, 15:16].broadcast_to([128, HALF // 16, 16]), G3, ALU.subtract,
                       [('hG', hf)], [('hD', hf)], eng='pool')
                if d == 0:
                    Gd, GLd, gk, glk = hG, hD, 'hG', 'hD'
                else:
                    for hf, sl in H:
                        tt(hD[:, sl], hD[:, sl], hlogf[:, sl], ALU.add, [('hD', hf), ('hlogf', hf)], [('hD', hf)])
                    for hf, sl in H:
                        tt(hG[:, sl], hG[:, sl], hlogf[:, sl], ALU.subtract, [('hG', hf), ('hlogf', hf)], [('hG', hf)])
                    Gd, GLd, gk, glk = hD, hG, 'hD', 'hG'
                for hf, sl in H:
                    act(hE[:, sl], Gd[:, sl], AF.Exp, [(gk, hf)], [('hlogf', hf)])
                for hf, sl in H:
                    tt(qdec[d][:, sl], hq_sb[:, sl], hE[:, sl], ALU.mult, ['hq_sb', ('hlogf', hf)], [(f"qdec{d}", hf)], eng='pool')
                for hf, sl in H:
                    act(hE[:, sl], Gd[:, sl], AF.Exp, [(gk, hf)], [('hlogf', hf)], scale=-1.0)
                for hf, sl in H:
                    stt(ktil[d][:, sl], hz[:, sl], ocol, hE[:, sl], ALU.mult, ALU.mult, [('hz', hf), ('hlogf', hf), 'omlb'], [(f"ktil{d}", hf)])
                for hf, sl in H:
                    act(hE[:, sl], GLd[:, sl], AF.Exp, [(glk, hf)], [('hlogf', hf)])
                for hf, sl in H:
                    stt(kdecT[:, sl], hz[:, sl], ocol, hE[:, sl], ALU.mult, ALU.mult, [('hz', hf), ('hlogf', hf), 'omlb'], [('kdecT', hf)])
                for g3 in range(6):
                    pk = PSA.get()
                    pv = psb[pk][:].bitcast(BF16)
                    for q in range(3):
                        i = g3 * 3 + q
                        tr(pv[:, q * 128:(q + 1) * 128], kdecT[:, i * 128:(i + 1) * 128], ident_b[:], [('kdecT', g3 // 3), 'ident_b'], [f"ps{pk}"])
                    act(kdecTM[d][:, g3 * 3:g3 * 3 + 3, :], pv[:, 0:3 * 128].rearrange("p (q c) -> p q c", q=3), AF.Copy, [f"ps{pk}"], [f"kdecTM{d}"])
            S.barrier(); sc1.close()
            sc2 = Scope(nc); sb = sc2.sb
            hgate_sb = sb("hgate_sb", [128, T], BF16)
            osum = sb("osum", [128, T], F32)
            Vblk = [sb(f"Vblk{i}", [128, 2, 8, 64], BF16) for i in range(2)]
            ATm = [sb(f"ATm{i}", [128, 128], BF16) for i in range(3)]
            kvbufs = [sb(f"kvbuf{d}", [128, 64, NCH], BF16) for d in range(2)]
            VR = 8
            a_rep = sb("a_rep", [128, VR, NCH], F32)
            S.dma('sp', hgate_sb[:], HGATE[hp, :, :], r=['HGATE'], w=['hgate_sb'])
            for i in range(NT):
                vi = VBLK.get()
                tt(Vblk[vi][:], Vh[:, i, hp * 128:(hp + 1) * 128].rearrange("p (h v) -> p h v", h=2).unsqueeze(2).broadcast_to([128, 2, 8, 64]),
                   Emask[:].unsqueeze(1).unsqueeze(3).broadcast_to([128, 2, 8, 64]), ALU.mult, ['Vh', 'Emask'], [f"Vblk{vi}"])
                for d in range(2):
                    j0 = nat(i, 0, d)
                    pk = PSA.get()
                    for h2 in range(2):
                        mm(psb[pk][h2 * 64:(h2 + 1) * 64, :], kdecTM[d][:, i, h2 * 64:(h2 + 1) * 64],
                           Vblk[vi][:, h2, :, :].rearrange("p n v -> p (n v)"), True, True, [f"kdecTM{d}", f"Vblk{vi}"], [f"ps{pk}"])
                    act(kvbufs[d][:, :, j0:j0 + 8], psb[pk][:].rearrange("p (n v) -> p v n", n=8), AF.Copy, [f"ps{pk}"], [f"kvbuf{d}"])
            items = [(i, h2, d) for i in range(NT) for h2 in range(2) for d in range(2)]
            abank = {}

            def emit_A(idx):
                i, h2, d = items[idx]
                rows = slice(h2 * 64, h2 * 64 + 64)
                pk = PSA.get()
                mm(psb[pk][:, 0:128], ktil[d][rows, i * 128:(i + 1) * 128], qdec[d][rows, i * 128:(i + 1) * 128], True, True,
                   [f"ktil{d}", f"qdec{d}"], [f"ps{pk}"])
                abank[idx] = pk
            for idx in range(2): emit_A(idx)
            for idx, (i, h2, d) in enumerate(items):
                rows = slice(h2 * 64, h2 * 64 + 64)
                po = 4 + (i % 2)
                pk = abank.pop(idx)
                ai = ATM.get()
                tt(ATm[ai][:], psb[pk][:, 0:128], (maskF if d == 0 else maskB)[:], ALU.mult, [f"ps{pk}", 'maskF', 'maskB'], [f"ATm{ai}"])
                if idx + 2 < len(items): emit_A(idx + 2)
                mm(psb[po][rows, 0:128], Vh[:, i, hp * 128 + h2 * 64:hp * 128 + h2 * 64 + 64], ATm[ai][:], d == 0, d == 1,
                   ['Vh', f"ATm{ai}"], [(f"ps{po}", h2)])
                if h2 == 1 and d == 1:
                    act(osum[:, i * 128:(i + 1) * 128], psb[po][:, 0:128], AF.Copy, [f"ps{po}"], [('osum', i)])
            for d in range(2):
                kvbuf = kvbufs[d]; kvk = f"kvbuf{d}"
                cp(a_rep[:], abuf[d][:].unsqueeze(1).broadcast_to([128, VR, NCH]), [f"abuf{d}"], ['a_rep'])
                rc = 0 if d == 0 else NCH - 1
                memset(a_rep[:, :, rc:rc + 1], 0.0, ['a_rep'])
                af = a_rep[:].rearrange("p v n -> p (v n)")
                for g4 in range(64 // VR):
                    kf = kvbuf[:, VR * g4:VR * g4 + VR, :].rearrange("p v n -> p (v n)")
                    of = Sbf[d][:, VR * g4:VR * g4 + VR, :].rearrange("p v n -> p (v n)")
                    if d == 0:
                        S.op('dve', lambda e: e.tensor_tensor_scan(out=of, data0=af, data1=kf, initial=0.0, op0=ALU.mult, op1=ALU.add),
                             [kvk, 'a_rep'], [f"Sbf{d}"])
                    else:
                        NF = VR * NCH
                        S.op('dve', lambda e: e.tensor_tensor_scan(out=of[:, NF - 1::-1], data0=af[:, NF - 1::-1], data1=kf[:, NF - 1::-1],
                                                                  initial=0.0, op0=ALU.mult, op1=ALU.add), [kvk, 'a_rep'], [f"Sbf{d}"])
            for i in range(NT):
                po = 4 + (i % 2)
                for h2 in range(2):
                    rows = slice(h2 * 64, h2 * 64 + 64)
                    items = []
                    for d in range(2):
                        for n in range(8):
                            m = nat(i, n, d)
                            if d == 0:
                                if m == 0: continue
                                js = m - 1
                            else:
                                if m == NCH - 1: continue
                                js = m + 1
                            items.append((d, n, js))
                    seen = set()
                    for k, (d, n, js) in enumerate(items):
                        mm(psb[po][rows, n * 16:(n + 1) * 16], Sbf[d][rows, :, js], qdec[d][rows, i * 128 + n * 16:i * 128 + (n + 1) * 16],
                           k == 0, k == len(items) - 1, [f"Sbf{d}", f"qdec{d}"], [(f"ps{po}", h2)])
                tt(osum[:, i * 128:(i + 1) * 128], psb[po][:, 0:128], osum[:, i * 128:(i + 1) * 128], ALU.add, [f"ps{po}", ('osum', i)], [('osum', i)])
            hcol = cst[:, C_HN + l:C_HN + l + 1]
            for bi, (t0, n) in enumerate(BLOCKS):
                i = QR.get()
                act(qsq[i][:, 0:n], osum[:, t0:t0 + n], AF.Square, ['osum'], [f"qsq{i}"])
                p2 = PSQ.get()
                mm(psb[p2][:, 0:n], blk_b[:], qsq[i][:, 0:n], True, True, ['blk_b', f"qsq{i}"], [f"ps{p2}"])
                act(qt1[i][:, 0:n], psb[p2][:, 0:n], AF.Ln, [f"ps{p2}"], [f"qt1_{i}"], scale=1.0 / 64, bias=EPS)
                act(qt1[i][:, 0:n], qt1[i][:, 0:n], AF.Exp, [f"qt1_{i}"], [f"qt1_{i}"], scale=-0.5)
                stt(qt2[i][:, 0:n], osum[:, t0:t0 + n], hcol, qt1[i][:, 0:n], ALU.mult, ALU.mult, ['osum', f"qt1_{i}", 'cst'], [f"qt2_{i}"])
                si = STB.get()
                tt(stb[si][:, 0:n], qt2[i][:, 0:n], hgate_sb[:, t0:t0 + n], ALU.mult, [f"qt2_{i}", 'hgate_sb'], [f"stb{si}"])
                S.dma('sp', BR[6 + hp, :, t0:t0 + n], stb[si][:, 0:n], r=[f"stb{si}"], w=[('BR', (6 + hp, bi))])
            if 'osum' in taps and l == 0 and hp == 0: tap('osum', osum[:], [128, T], F32, ['osum'])
            S.barrier(); sc2.close()
        S.barrier(); sc.close()

    GTB = Rot([0, 1]); MT = Rot([0, 1, 2])

    def merge_phase(l):
        S.barrier(pool=True)
        sc = Scope(nc); sb = sc.sb
        wbr = sb("wbr", [128, 8, D], BF16)
        wo_sb = sb("wo_sb", [128, 8, D], BF16)
        brb = [sb(f"brb{i}", [128, 8, 512], BF16) for i in range(2)]
        gtb = [sb(f"gtb{i}", [128, 3, 512], BF16) for i in range(2)]
        yT = [sb(f"yT{i}", [128, 8, 512], BF16) for i in range(2)]
        mt = [sb(f"mt{i}", [128, 512], F32) for i in range(3)]
        xj = [sb(f"xj{i}", [128, 512], F32) for i in range(2)]; XJ = Rot([0, 1])
        c2 = [sb(f"c2_{i}", [128, 512], F32) for i in range(2)]; C2 = Rot([0, 1])
        for cbh in range(2):
            cs = slice(cbh * 512, cbh * 512 + 512)
            S.dma('pool', wbr[:, 0:2, cs], w_bp[l].rearrange("(kc p) n -> p kc n", p=128)[:, :, cs], w=[('wbr', (0, cbh))])
            S.dma('pool', wbr[:, 2:6, cs], w_ba[l].rearrange("(kc p) n -> p kc n", p=128)[:, :, cs], w=[('wbr', (1, cbh))])
            S.dma('pool', wbr[:, 6:8, cs], w_bh[l].rearrange("(kc p) n -> p kc n", p=128)[:, :, cs], w=[('wbr', (2, cbh))])
        for cbh in range(2):
            cs = slice(cbh * 512, cbh * 512 + 512)
            for h in range(2):
                S.dma('pool', wo_sb[:, h * 4:h * 4 + 4, cs], w_out[l].rearrange("(kc p) n -> p kc n", p=128)[:, h * 4:h * 4 + 4, cs],
                      w=[('wo_sb', (h, cbh))])
        groups = ((0, 2), (2, 6), (6, 8))
        mblocks = [(bi, t0, n) for bi, (t0, n) in enumerate(BLOCKS) if not (t0 >= L and l == depth - 1 and depth == DEPTH)]

        def load_brb(k):
            bi_, t0_, n_ = mblocks[k]
            S.dma('sp', brb[bi_ % 2][:, :, 0:n_], BR.rearrange("c p t -> p c t")[:, :, t0_:t0_ + n_], r=['BR'], w=[f"brb{bi_ % 2}"])
        def branch_load(k, j):
            bi, t0, n = mblocks[k]
            gi = GTB.get()
            S.dma('sp', gtb[gi][:, :, 0:n], GT.rearrange("(g j) p t -> j p g t", g=3)[j, :, :, t0:t0 + n], r=['GT'], w=[f"gtb{gi}"])
            return gi

        def wout_load(k, j):
            bi, t0, n = mblocks[k]
            xi = XJ.get()
            S.dma('sp', xj[xi][:, 0:n], XT[j, :, t0:t0 + n], r=[('XT', (j, bi))], w=[f"xj{xi}"])
            return xi

        def branch_j(k, j, gi):
            bi, t0, n = mblocks[k]; b = bi % 2
            pks = []
            for gidx, (k0, k1) in enumerate(groups):
                pk = PSA.get() if gidx < 2 else PSQ.get()
                for kc in range(k0, k1):
                    mm(psb[pk][:, 0:n], wbr[:, kc, j * 128:(j + 1) * 128], brb[b][:, kc, 0:n], kc == k0, kc == k1 - 1,
                       [('wbr', (gidx, j // 4)), f"brb{b}"], [f"ps{pk}"])
                pks.append(pk)
            m0 = MT.get(); m1 = MT.get(); ci = C2.get()
            tt(mt[m0][:, 0:n], psb[pks[0]][:, 0:n], gtb[gi][:, 0, 0:n], ALU.mult, [f"ps{pks[0]}", f"gtb{gi}"], [f"mt{m0}"])
            tt(mt[m1][:, 0:n], psb[pks[1]][:, 0:n], gtb[gi][:, 1, 0:n], ALU.mult, [f"ps{pks[1]}", f"gtb{gi}"], [f"mt{m1}"])
            act(c2[ci][:, 0:n], psb[pks[2]][:, 0:n], AF.Copy, [f"ps{pks[2]}"], [f"c2_{ci}"])
            tt(c2[ci][:, 0:n], c2[ci][:, 0:n], gtb[gi][:, 2, 0:n], ALU.mult, [f"c2_{ci}", f"gtb{gi}"], [f"c2_{ci}"], eng='pool')
            tt(mt[m0][:, 0:n], mt[m0][:, 0:n], mt[m1][:, 0:n], ALU.add, [f"mt{m0}", f"mt{m1}"], [f"mt{m0}"])
            tt(yT[b][:, j, 0:n], mt[m0][:, 0:n], c2[ci][:, 0:n], ALU.add, [f"mt{m0}", f"c2_{ci}"], [(f"yT{b}", j)], eng='pool')

        def wout_j(k, j, xi):
            bi, t0, n = mblocks[k]; b = bi % 2
            wch = 0 if t0 < L else 1
            pk = PSQ.get()
            for kc in range(KC):
                mm(psb[pk][:, 0:n], wo_sb[:, kc, j * 128:(j + 1) * 128], yT[b][:, kc, 0:n], kc == 0, kc == KC - 1,
                   [('wo_sb', (kc // 4, j // 4)), (f"yT{b}", kc)], [f"ps{pk}"])
            stt(xj[xi][:, 0:n], psb[pk][:, 0:n], modcol(2, j, wch), xj[xi][:, 0:n], ALU.mult, ALU.add,
                [f"ps{pk}", modkey(), f"xj{xi}"], [f"xj{xi}"])
            S.dma('sp', XT[j, :, t0:t0 + n], xj[xi][:, 0:n], r=[f"xj{xi}"], w=[('XT', (j, bi))])

        load_brb(0)
        if len(mblocks) > 1: load_brb(1)
        steps = [('b', 0, j) for j in range(8)]
        for k in range(len(mblocks)):
            for j in range(8):
                if k + 1 < len(mblocks): steps.append(('b', k + 1, j))
                steps.append(('w', k, j))
        loaded = {}

        def issue_load(idx):
            kind, k, j = steps[idx]
            loaded[idx] = branch_load(k, j) if kind == 'b' else wout_load(k, j)
        nb = {'b': 0, 'w': 0}
        pend = []
        for idx in range(len(steps)):
            while pend and False: pass
            la = idx
            while la < len(steps) and la <= idx + 3:
                if la not in loaded:
                    kind = steps[la][0]
                    inflight = sum(1 for q in loaded if q >= idx and steps[q][0] == kind)
                    if inflight < 2: issue_load(la)
                    else: break
                la += 1
            kind, k, j = steps[idx]
            if kind == 'b' and j == 0 and k + 1 < len(mblocks) and k >= 1: load_brb(k + 1)
            if kind == 'b': branch_j(k, j, loaded[idx])
            else: wout_j(k, j, loaded[idx])
        S.barrier(); sc.close()

    SIL = Rot([0, 1])

    def ffn_phase(l, last):
        S.barrier(pool=True)
        sc = Scope(nc); sb = sc.sb
        w2_sb = sb("w2_sb", [128, FC, D], BF16)
        acb = [sb(f"acb{i}", [128, FC, 512], BF16) for i in range(2)]
        sil = [sb(f"sil{i}", [128, 512], F32) for i in range(2)]
        xj = [sb(f"xj{i}", [128, 512], F32) for i in range(2)]; XJ = Rot([0, 1])
        wsrc = w_f1[l].rearrange("(kc p) n -> p kc n", p=128)
        nblk = BLOCKS[:4] if last else BLOCKS
        w2_loads = [(h, cbh) for h in range(0, FC, 2) for cbh in range(2)]

        def w2_load(h, cbh):
            cs = slice(cbh * 512, cbh * 512 + 512)
            S.dma('pool', w2_sb[:, h:h + 2, cs], w_f2[l].rearrange("(kc p) n -> p kc n", p=128)[:, h:h + 2, cs], w=[('w2_sb', (h, cbh))])
        for wt in range(FC // 2):
            if wt >= 2:
                for _ in range(3):
                    if w2_loads: w2_load(*w2_loads.pop(0))
            i = Wrot.get()
            Wv = Wb[i][:, 0:8 * 512].rearrange("p (a b) -> p a b", a=8)
            S.dma('pool', Wv[:, :, 0:256], wsrc[:, :, wt * 256:wt * 256 + 256], w=[f"W{i}"])
            S.dma('pool', Wv[:, :, 256:512], wsrc[:, :, DFF + wt * 256:DFF + wt * 256 + 256], w=[f"W{i}"])
            for jj in range(2):
                fcx = wt * 2 + jj
                for bi, (t0, n) in enumerate(nblk):
                    pa = PSA.get(); pb_ = PSQ.get()
                    for kc in range(KC):
                        mm(psb[pa][:, 0:n], Wv[:, kc, jj * 128:(jj + 1) * 128], hT[:, kc, t0:t0 + n], kc == 0, kc == KC - 1,
                           [f"W{i}", ('hT', (kc, bi))], [f"ps{pa}"])
                    for kc in range(KC):
                        mm(psb[pb_][:, 0:n], Wv[:, kc, 256 + jj * 128:256 + (jj + 1) * 128], hT[:, kc, t0:t0 + n], kc == 0, kc == KC - 1,
                           [f"W{i}", ('hT', (kc, bi))], [f"ps{pb_}"])
                    si = SIL.get()
                    act(sil[si][:, 0:n], psb[pa][:, 0:n], AF.Silu, [f"ps{pa}"], [f"sil{si}"])
                    bi2 = STB.get()
                    tt(stb[bi2][:, 0:n], psb[pb_][:, 0:n], sil[si][:, 0:n], ALU.mult, [f"ps{pb_}", f"sil{si}"], [f"stb{bi2}"])
                    S.dma('sp', ACTS[fcx, :, t0:t0 + n], stb[bi2][:, 0:n], r=[f"stb{bi2}"], w=[('ACTS', (fcx, bi))])
        while w2_loads: w2_load(*w2_loads.pop(0))

        def load_acb(k):
            t0_, n_ = nblk[k]
            for hh in range(2):
                S.dma('sp', acb[k % 2][:, hh * 11:hh * 11 + 11, 0:n_], ACTS.rearrange("c p t -> p c t")[:, hh * 11:hh * 11 + 11, t0_:t0_ + n_],
                      r=['ACTS'], w=[(f"acb{k % 2}", hh)])
        load_acb(0)
        for bi, (t0, n) in enumerate(nblk):
            wch = 0 if t0 < L else 1
            b = bi % 2
            if bi + 1 < len(nblk): load_acb(bi + 1)
            def xload(j_):
                xi_ = XJ.get()
                S.dma('sp', xj[xi_][:, 0:n], XT[j_, :, t0:t0 + n], r=[('XT', (j_, bi))], w=[f"xj{xi_}"])
                return xi_
            xnext = xload(0)
            for j in range(8):
                xi = xnext
                if j + 1 < 8: xnext = xload(j + 1)
                pk = PSA.get()
                for kc in range(FC):
                    mm(psb[pk][:, 0:n], w2_sb[:, kc, j * 128:(j + 1) * 128], acb[b][:, kc, 0:n], kc == 0, kc == FC - 1,
                       ['w2_sb', (f"acb{b}", kc // 11)], [f"ps{pk}"])
                stt(xj[xi][:, 0:n], psb[pk][:, 0:n], modcol(5, j, wch), xj[xi][:, 0:n], ALU.mult, ALU.add,
                    [f"ps{pk}", modkey(), f"xj{xi}"], [f"xj{xi}"])
                S.dma('sp', XT[j, :, t0:t0 + n], xj[xi][:, 0:n], r=[f"xj{xi}"], w=[('XT', (j, bi))])
        S.barrier(); sc.close()

    def run_layers():
        for l in range(depth):
            last = (l == depth - 1) and depth == DEPTH
            CUR['l'] = l
            ada_next = adaln_steps(l + 1) if l + 1 < depth else None
            norm_phase(a1s[l % 2], f"a1_{l % 2}", 0)
            if stop_after == ('norm1', l): return
            asc = open_attn_scope()
            in_proj(l)
            S.barrier()
            if l == 0:
                if 'qT' in taps: tap('qT', A['qT'][:], [128, 4, T], BF16, ['qT'])
                if 'VE' in taps: tap('VE', A['VE'][:], [128, NT, 2, 192], BF16, ['VE'])
                if 'hT' in taps: tap('hT', hT[:], [128, KC, T], BF16, ['hT'])
                S.barrier()
            if stop_after == ('inproj', l):
                asc.close(); return
            attention(l, ada_next)
            if ada_next is not None:
                for _ in ada_next: pass
            S.barrier(); asc.close()
            if stop_after == ('attn', l): return
            pool_phase(l)
            if stop_after == ('pool', l): return
            hgrn_phase(l)
            if stop_after == ('hgrn', l): return
            merge_phase(l)
            if stop_after == ('merge', l): return
            norm_phase(a2s[l % 2], f"a2_{l % 2}", 3)
            ffn_phase(l, last)

    run_layers()
    if 'modT' in taps: tap('modT', modTs[0][:], [128, 48, 2], F32, ['modT0'])
    if 'cosT' in taps: tap('cosT', cosT[:], [128, L], F32, ['cosT'])
    if 'sinT' in taps: tap('sinT', sinT[:], [128, L], F32, ['sinT'])
    for nm, src in (('XT', XT), ('U', U), ('HQ', HQ), ('Z', Z), ('HGATE', HGATE), ('VH', VH), ('GT', GT), ('BR', BR), ('ACTS', ACTS)):
        if nm in taps:
            t_ = dram("tap_" + nm, list(src.shape), src.dtype, "ExternalOutput")
            tap_d[nm] = t_
            S.dma('sp', t_, src, r=[nm])

    scf = Scope(nc)
    xs = [scf.sb(f"xs{i}", [128, D], F32) for i in range(2)]
    xs2 = [scf.sb(f"xt2_{i}", [128, KC, 128], F32) for i in range(2)]
    def oload(i):
        S.dma('sp', xs2[i % 2][:], XTv[:, :, i * 128:(i + 1) * 128], r=['XT'], w=[f"xt2_{i % 2}"])
    oload(0)
    for i in range(16):
        b = i % 2
        for hh in range(2):
            pk = 2 * b + hh
            for q in range(4):
                kc = hh * 4 + q
                tr(psb[pk][:, q * 128:(q + 1) * 128], xs2[b][:, kc, :], ident_f[:], [f"xt2_{b}", 'ident_f'], [f"ps{pk}"])
            cp(xs[b][:, hh * 512:(hh + 1) * 512], psb[pk][:], [f"ps{pk}"], [f"xs{b}"])
        if i + 1 < 16: oload(i + 1)
        S.dma('sp', out_d[i * 128:(i + 1) * 128, :], xs[b][:], r=[f"xs{b}"], w=[('out', i)])
    S.final('sp')
    return nc, list(tap_d.keys())


_IN_NAMES = ["w_ada", "b_ada", "norm1_w", "w_in", "pool_w", "pool_scale", "q_norm_w", "k_norm_w", "hg_lb_logits", "hg_norm_w",
             "w_branch_pool", "w_branch_attn", "w_branch_hg", "w_out", "norm2_w", "w_ffn_in", "w_ffn_out"]


def make_in_maps(inputs):
    f = lambda a: np.ascontiguousarray(np.asarray(a, dtype=np.float32))
    shared = {k: f(inputs[k]) for k in _IN_NAMES}
    x = f(inputs["x"]); c = f(inputs["c"]); ctx = f(inputs["ctx"]); cc = f(inputs["c_ctx"]).reshape(8, 128)
    maps = []
    for b in range(8):
        m = dict(shared)
        m["x"] = x[b]; m["ctx"] = ctx[b]; m["c"] = c[b].reshape(8, 128); m["c_ctx"] = cc
        maps.append(m)
    return maps


def kernel(**inputs):
    nc, _ = build()
    res = run_bass_kernel_spmd(nc, make_in_maps(inputs), core_ids=list(range(8)))
    return np.stack([np.asarray(r["out"], dtype=np.float32) for r in res.results], axis=0)
```

```python
import math
import numpy as np
import concourse.bass as bass
import concourse.mybir as mybir
from concourse.bass_utils import run_bass_kernel_spmd

F32 = mybir.dt.float32; BF16 = mybir.dt.bfloat16; I32 = mybir.dt.int32
AF = mybir.ActivationFunctionType; ALU = mybir.AluOpType
AX = mybir.AxisListType

D = 1024; KC = 8; L = 2048; CT = 256; T = L + CT; DEPTH = 4
NT = T // 128
BLOCKS = [(0, 512), (512, 512), (1024, 512), (1536, 512), (2048, 256)]
IN_W = 5376; DFF = 2816; FC = DFF // 128
EPS = 1e-6
UW = 2336


def ucol(t):
    return t + 8 if t < L else t + 24


class Sched:
    EPOCH = 30000

    def __init__(self, nc, same_sync=True, ndma=12):
        self.nc = nc
        self.E = {'pe': nc.tensor, 'act': nc.scalar, 'dve': nc.vector, 'pool': nc.gpsimd, 'sp': nc.sync}
        self.cnt = {e: 0 for e in self.E}
        self.esem = {e: [] for e in self.E}
        self.waited = {e: {} for e in self.E}
        self.res = {}
        self.dsems = []; self.dval = []
        self.dpool = {}; self.drr = {}
        for q in ('sp', 'pool'):
            self.dpool[q] = []
            for i in range(ndma):
                self.dpool[q].append(len(self.dsems))
                self.dsems.append(nc.alloc_semaphore(f"d_{q}_{i}")); self.dval.append(0)
            self.drr[q] = 0
        self.same_sync = same_sync

    def _wait(self, e, tok, force=False):
        if tok is None: return
        if tok[0] == 'e':
            _, x, ep, v = tok
            if x == e and (e == 'pe' or not (self.same_sync or force)): return
            cur = self.waited[e].get(('e', x), (-1, 0))
            if (ep, v) <= cur: return
            self.waited[e][('e', x)] = (ep, v)
            self.E[e].wait_ge(self.esem[x][ep], v)
        else:
            _, i, v = tok
            if v == 0 or self.waited[e].get(('d', i), 0) >= v: return
            self.waited[e][('d', i)] = v
            self.E[e].wait_ge(self.dsems[i], v)

    @staticmethod
    def _split(key):
        return key if isinstance(key, tuple) else (key, None)

    def _ents(self, key):
        name, sub = self._split(key)
        d = self.res.get(name, {})
        if sub is None: return list(d.values())
        return [d[k] for k in (sub, None) if k in d]

    def _deps(self, r, w):
        deps = []
        for k in r:
            for ent in self._ents(k):
                if ent['w'] is not None: deps.append(ent['w'])
        for k in w:
            for ent in self._ents(k):
                if ent['w'] is not None: deps.append(ent['w'])
                deps.extend(ent['r'].values())
        return deps

    def _record(self, tok, r, w):
        rk = (tok[0], tok[1])
        for k in r:
            name, sub = self._split(k)
            ent = self.res.setdefault(name, {}).setdefault(sub, {'w': None, 'r': {}})
            ent['r'][rk] = tok
        for k in w:
            name, sub = self._split(k)
            if sub is None: self.res[name] = {None: {'w': tok, 'r': {}}}
            else: self.res.setdefault(name, {})[sub] = {'w': tok, 'r': {}}

    def op(self, e, fn, r=(), w=()):
        for tok in self._deps(r, w): self._wait(e, tok)
        inst = fn(self.E[e])
        k = self.cnt[e]; self.cnt[e] += 1
        ep, v = divmod(k, self.EPOCH); v += 1
        while len(self.esem[e]) <= ep:
            self.esem[e].append(self.nc.alloc_semaphore(f"s_{e}_{len(self.esem[e])}"))
        inst.then_inc(self.esem[e][ep], 1)
        tok = ('e', e, ep, v)
        self._record(tok, r, w)
        return tok

    def dma(self, q, out, in_, r=(), w=(), **kw):
        pool = self.dpool[q]; i = pool[self.drr[q] % len(pool)]; self.drr[q] += 1
        self._wait(q, ('d', i, self.dval[i]))
        for tok in self._deps(r, w): self._wait(q, tok)
        inst = self.E[q].dma_start(out=out, in_=in_, **kw)
        self.dval[i] += 16
        inst.then_inc(self.dsems[i], 16)
        tok = ('d', i, self.dval[i])
        self._record(tok, r, w)
        return tok

    def last_tok(self, e):
        k = self.cnt[e] - 1
        if k < 0: return None
        ep, v = divmod(k, self.EPOCH)
        return ('e', e, ep, v + 1)

    def barrier(self, pool=False):
        toks = [self.last_tok(e) for e in ('pe', 'act', 'dve', 'pool')]
        for i in self.dpool['sp']: toks.append(('d', i, self.dval[i]))
        for e in ('pe', 'act', 'dve', 'sp') + (('pool',) if pool else ()):
            for tok in toks: self._wait(e, tok, force=True)

    def final(self, e='sp'):
        toks = [self.last_tok(x) for x in ('pe', 'act', 'dve', 'pool')]
        toks += [('d', i, self.dval[i]) for i in range(len(self.dsems))]
        for tok in toks: self._wait(e, tok, force=True)


class Scope:
    uid = 0

    def __init__(self, nc):
        from contextlib import ExitStack
        self.nc = nc; self.es = ExitStack()

    def sb(self, name, shape, dtp):
        Scope.uid += 1
        return self.es.enter_context(self.nc.sbuf_tensor(f"{name}_{Scope.uid}", list(shape), dtp))

    def close(self):
        self.es.close()


class Rot:
    def __init__(self, items):
        self.items = items; self.i = 0

    def get(self):
        it = self.items[self.i % len(self.items)]; self.i += 1
        return it


def build(depth=DEPTH, taps=(), stop_after=None):
    nc = bass.Bass("TRN2", target_bir_lowering=False)
    S = Sched(nc)
    dram = lambda name, shape, dtp, kind="Internal": nc.dram_tensor(name, list(shape), dtp, kind=kind).ap()
    x_d = dram("x", [L, D], F32, "ExternalInput")
    ctx_d = dram("ctx", [CT, D], F32, "ExternalInput")
    c_d = dram("c", [8, 128], F32, "ExternalInput")
    cc_d = dram("c_ctx", [8, 128], F32, "ExternalInput")
    w_ada = dram("w_ada", [DEPTH, D, 6 * D], F32, "ExternalInput")
    b_ada = dram("b_ada", [DEPTH, 6 * D], F32, "ExternalInput")
    norm1_w = dram("norm1_w", [DEPTH, D], F32, "ExternalInput")
    w_in = dram("w_in", [DEPTH, D, IN_W], F32, "ExternalInput")
    pool_w = dram("pool_w", [DEPTH, 4, 64, 64], F32, "ExternalInput")
    pool_scale = dram("pool_scale", [DEPTH, 256], F32, "ExternalInput")
    q_norm_w = dram("q_norm_w", [DEPTH, 64], F32, "ExternalInput")
    k_norm_w = dram("k_norm_w", [DEPTH, 64], F32, "ExternalInput")
    hg_lb = dram("hg_lb_logits", [DEPTH, 2, 256], F32, "ExternalInput")
    hg_norm_w = dram("hg_norm_w", [DEPTH, 64], F32, "ExternalInput")
    w_bp = dram("w_branch_pool", [DEPTH, 256, D], F32, "ExternalInput")
    w_ba = dram("w_branch_attn", [DEPTH, 512, D], F32, "ExternalInput")
    w_bh = dram("w_branch_hg", [DEPTH, 256, D], F32, "ExternalInput")
    w_out = dram("w_out", [DEPTH, D, D], F32, "ExternalInput")
    norm2_w = dram("norm2_w", [DEPTH, D], F32, "ExternalInput")
    w_f1 = dram("w_ffn_in", [DEPTH, D, 2 * DFF], F32, "ExternalInput")
    w_f2 = dram("w_ffn_out", [DEPTH, DFF, D], F32, "ExternalInput")
    out_d = dram("out", [L, D], F32, "ExternalOutput")
    tap_d = {}
    XT = dram("XT", [8, 128, T], F32)
    U = dram("U", [2, 128, UW], F32)
    HQ = dram("HQ", [2, 128, T], F32)
    Z = dram("Z", [4, 128, T], F32)
    HGATE = dram("HGATE", [2, 128, T], BF16)
    VH = dram("VH", [T, 256], BF16)
    GT = dram("GT", [24, 128, T], BF16)
    BR = dram("BR", [8, 128, T], BF16)
    ACTS = dram("ACTS", [FC, 128, T], BF16)

    def sb(name, shape, dtp):
        return nc.alloc_sbuf_tensor(name, list(shape), dtp)

    Wb = [sb(f"W{i}", [128, 4096], BF16) for i in range(3)]
    Wrot = Rot([0, 1, 2])
    hT = sb("hT", [128, KC, T], BF16)
    ident_f = sb("ident_f", [128, 128], F32); ident_b = sb("ident_b", [128, 128], BF16)
    ones_b = sb("ones_b", [128, 128], BF16); blk_b = sb("blk_b", [128, 128], BF16)
    onesf = sb("onesf", [128, 128], F32); Rt = sb("Rt", [128, 128], BF16)
    cosT = sb("cosT", [128, L], F32); sinT = sb("sinT", [128, L], F32)
    maskF = sb("maskF", [128, 128], BF16); maskB = sb("maskB", [128, 128], BF16)
    Emask = sb("Emask", [128, 8], BF16); Erev = sb("Erev", [128, 8], BF16)
    m01 = sb("m01", [128, T], BF16)
    cst = sb("cst", [128, 128], F32); bada = sb("bada", [128, 192], F32)
    lbT = sb("lbT", [128, 16], F32); omlb = sb("omlb", [128, 16], F32); nomlb = sb("nomlb", [128, 16], F32)
    cT = sb("cT", [128, 16], BF16)
    modTs = [sb(f"modT{i}", [128, 48, 2], F32) for i in range(2)]
    a1s = [sb(f"a1_{i}", [128, 8, 2], F32) for i in range(2)]; a2s = [sb(f"a2_{i}", [128, 8, 2], F32) for i in range(2)]
    CUR = {'l': 0}
    rcE = sb("rcE", [128, 2, 16], F32)
    pwbd = sb("pwbd", [128, 2, 128], BF16)
    psb = [nc.alloc_psum_tensor(f"ps{i}", [128, 512], F32) for i in range(8)]
    stf = [sb(f"stf{i}", [128, 512], F32) for i in range(2)]; STF = Rot([0, 1])
    stb = [sb(f"stb{i}", [128, 512], BF16) for i in range(3)]; STB = Rot([0, 1, 2])

    V = lambda e: e

    def act(out, in_, func, r, w, **kw):
        return S.op('act', lambda e: e.activation(out=out, in_=in_, func=func, **kw), r, w)

    def tt(out, in0, in1, op, r, w, eng='dve'):
        return S.op(eng, lambda e: e.tensor_tensor(out=out, in0=in0, in1=in1, op=op), r, w)

    def ts(out, in0, s1, s2, op0, op1, r, w, eng='dve'):
        if op1 is None:
            return S.op(eng, lambda e: e.tensor_scalar(out=out, in0=in0, scalar1=s1, scalar2=None, op0=op0), r, w)
        return S.op(eng, lambda e: e.tensor_scalar(out=out, in0=in0, scalar1=s1, scalar2=s2, op0=op0, op1=op1), r, w)

    def stt(out, in0, scalar, in1, op0, op1, r, w):
        return S.op('dve', lambda e: e.scalar_tensor_tensor(out=out, in0=in0, scalar=scalar, in1=in1, op0=op0, op1=op1), r, w)

    def cp(out, in_, r, w, eng='dve'):
        return S.op(eng, lambda e: e.tensor_copy(out=out, in_=in_), r, w)

    def mm(out, lhsT, rhs, start, stop, r, w):
        return S.op('pe', lambda e: e.matmul(out, lhsT=lhsT, rhs=rhs, start=start, stop=stop), r, w)

    def tr(out, in_, ident, r, w):
        return S.op('pe', lambda e: e.transpose(out, in_, ident), r, w)

    def memset(ap, val, w, eng='dve'):
        return S.op(eng, lambda e: e.memset(ap, val), (), w)

    def load_w(src, shape_free, ncols_total):
        i = Wrot.get()
        a, b = shape_free
        dst = Wb[i][:, 0:a * b].rearrange("p (a b) -> p a b", a=a)
        S.dma('pool', dst, src, w=[f"W{i}"])
        return dst, f"W{i}"

    sc0 = Scope(nc); sb0 = sb; sb = sc0.sb
    xs = [sb(f"xs{i}", [128, D], F32) for i in range(2)]
    xs2 = [sb(f"xt2_{i}", [128, KC, 128], F32) for i in range(2)]
    iot = sb("iot", [128, 128], I32); pI = sb("pI", [128, 1], I32); cI = sb("cI", [128, 128], I32)
    pc = sb("pc", [128, 1], I32); ccI = sb("ccI", [128, 128], I32)
    tmpA = sb("tmpA", [128, 128], F32); tmpB = sb("tmpB", [128, 128], F32)
    S.op('pool', lambda e: e.iota(iot[:], pattern=[[1, 128]], base=0, channel_multiplier=-1), w=['iot'])
    S.op('pool', lambda e: e.iota(pI[:], pattern=[[0, 1]], base=0, channel_multiplier=1), w=['pI'])
    S.op('pool', lambda e: e.iota(cI[:], pattern=[[1, 128]], base=0, channel_multiplier=0), w=['cI'])
    ts(ident_f[:], iot[:], 0.0, None, ALU.is_equal, None, ['iot'], ['ident_f'])
    cp(ident_b[:], ident_f[:], ['ident_f'], ['ident_b'])
    memset(ones_b[:], 1.0, ['ones_b']); memset(onesf[:], 1.0, ['onesf'])
    memset(blk_b[:], 0.0, ['blk_b'])
    memset(blk_b[0:64, 0:64], 1.0, ['blk_b']); memset(blk_b[64:128, 64:128], 1.0, ['blk_b'])
    ts(tmpA[:], iot[:], 16.0, None, ALU.is_equal, None, ['iot'], ['tmpA'])
    ts(tmpB[:], iot[:], -16.0, None, ALU.is_equal, None, ['iot'], ['tmpB'])
    for b in range(4):
        memset(tmpA[:, 32 * b:32 * b + 16], 0.0, ['tmpA'])
        memset(tmpB[:, 32 * b + 16:32 * b + 32], 0.0, ['tmpB'])
    tt(Rt[:], tmpA[:], tmpB[:], ALU.subtract, ['tmpA', 'tmpB'], ['Rt'])
    ts(pc[:], pI[:], 4, None, ALU.arith_shift_right, None, ['pI'], ['pc'])
    ts(ccI[:], cI[:], 4, None, ALU.arith_shift_right, None, ['cI'], ['ccI'])
    tt(tmpA[:], ccI[:], pc[:, 0:1].broadcast_to([128, 128]), ALU.is_equal, ['ccI', 'pc'], ['tmpA'])
    ts(tmpB[:], iot[:], 0.0, None, ALU.is_ge, None, ['iot'], ['tmpB'])
    tt(maskF[:], tmpA[:], tmpB[:], ALU.mult, ['tmpA', 'tmpB'], ['maskF'])
    ts(tmpB[:], iot[:], 0.0, None, ALU.is_le, None, ['iot', 'maskF'], ['tmpB'])
    tt(maskB[:], tmpA[:], tmpB[:], ALU.mult, ['tmpA', 'tmpB'], ['maskB'])
    tt(Emask[:], cI[:, 0:8], pc[:, 0:1].broadcast_to([128, 8]), ALU.is_equal, ['cI', 'pc'], ['Emask'])
    ts(tmpB[:, 0:8], cI[:, 0:8], -1.0, 7.0, ALU.mult, ALU.add, ['cI', 'maskB'], ['tmpB'])
    tt(Erev[:], tmpB[:, 0:8], pc[:, 0:1].broadcast_to([128, 8]), ALU.is_equal, ['tmpB', 'pc'], ['Erev'])
    big_i = sb("big_i", [128, T], I32); big_f = sb("big_f", [128, T], F32)
    big_g = sb("big_g", [128, T], F32); big_h = sb("big_h", [128, T], F32)
    S.op('pool', lambda e: e.iota(big_i[:], pattern=[[1, T]], base=0, channel_multiplier=0), w=['big_i'])
    ts(big_i[:], big_i[:], 15, None, ALU.bitwise_and, None, ['big_i'], ['big_i'])
    ts(m01[:], big_i[:], 0.0, None, ALU.is_gt, None, ['big_i'], ['m01'])
    freq = sb("freq", [128, 1], F32); jI = sb("jI", [128, 1], I32)
    ts(jI[:], pI[:], 15, None, ALU.bitwise_and, None, ['pI'], ['jI'])
    cp(freq[:], jI[:], ['jI'], ['freq'])
    act(freq[:], freq[:], AF.Exp, ['freq'], ['freq'], scale=-math.log(10000.0) / 16.0)
    for q in range(4):
        pat = [[1, 32], [0, 64]] if q % 2 == 0 else [[0, 32], [1, 64]]
        S.op('pool', lambda e, q=q, pat=pat: e.iota(big_i[32 * q:32 * q + 32, 0:L], pattern=pat, base=0, channel_multiplier=0),
             r=['m01'], w=['big_i'])
    ts(big_f[:, 0:L], big_i[:, 0:L], freq[:, 0:1], None, ALU.mult, None, ['big_i', 'freq'], ['big_f'])
    TWO_PI = 2.0 * math.pi

    def sin_of(dst, shift, dkey):
        ts(big_g[:, 0:L], big_f[:, 0:L], shift, None, ALU.add, None, ['big_f'], ['big_g'])
        ki = big_i[:, 0:L]
        ts(ki, big_g[:, 0:L], 1.0 / TWO_PI, None, ALU.mult, None, ['big_g'], ['big_i'])
        cp(big_h[:, 0:L], ki, ['big_i'], ['big_h'])
        stt(big_g[:, 0:L], big_h[:, 0:L], -TWO_PI, big_g[:, 0:L], ALU.mult, ALU.add, ['big_h', 'big_g'], ['big_g'])
        ts(big_h[:, 0:L], big_g[:, 0:L], math.pi, -TWO_PI, ALU.is_gt, ALU.mult, ['big_g'], ['big_h'])
        tt(big_g[:, 0:L], big_g[:, 0:L], big_h[:, 0:L], ALU.add, ['big_g', 'big_h'], ['big_g'])
        ts(big_h[:, 0:L], big_g[:, 0:L], -math.pi, TWO_PI, ALU.is_lt, ALU.mult, ['big_g'], ['big_h'])
        tt(big_g[:, 0:L], big_g[:, 0:L], big_h[:, 0:L], ALU.add, ['big_g', 'big_h'], ['big_g'])
        ts(big_g[:, 0:L], big_g[:, 0:L], -3.141592, 3.141592, ALU.max, ALU.min, ['big_g'], ['big_g'])
        act(dst, big_g[:, 0:L], AF.Sin, ['big_g'], [dkey])

    sin_of(sinT[:], 0.0, 'sinT')
    sin_of(cosT[:], math.pi / 2.0, 'cosT')
    for ch in range(2):
        for half in range(2):
            w = [2, 4, 8, 16][2 * ch + half]
            rows = slice(64 * half, 64 * half + 64)
            memset(rcE[rows, ch, :], 1.0 / w, ['rcE'], eng='pool')
            for t in range(w // 2):
                memset(rcE[rows, ch, t:t + 1], 1.0 / (t + w // 2), ['rcE'], eng='pool')
            for i in range(8):
                if (8 - i) < w // 2:
                    memset(rcE[rows, ch, 8 + i:9 + i], 1.0 / ((8 - i) + w // 2), ['rcE'], eng='pool')
    memset(pwbd[:], 0.0, ['pwbd'])
    memset(big_h[:], 0.0, ['big_h'])
    if True:
        for ch in range(2):
            S.dma('sp', U[ch, :, 0:T], big_h[:, 0:T], r=['big_h'], w=[('U', ch)])
            S.dma('sp', U[ch, :, T:UW], big_h[:, 0:UW - T], r=['big_h'], w=[('U', ch)])

    stg = sb("stg", [128, 128], F32)
    for half in range(2):
        memset(stg[:], 0.0, ['stg'])
        S.dma('sp', stg[0:96, :], b_ada.rearrange("l (j p) -> (l j) p", p=128)[96 * half:96 * half + 96, :], w=['stg'])
        tr(psb[0][:, 0:128], stg[:], ident_f[:], ['stg', 'ident_f'], ['ps0'])
        cp(bada[:, 96 * half:96 * half + 96], psb[0][:, 0:96], ['ps0'], ['bada'])
    memset(stg[:], 0.0, ['stg'])
    S.dma('sp', stg[0:32, :], norm1_w.rearrange("l (k p) -> (l k) p", p=128), w=['stg'])
    S.dma('sp', stg[32:64, :], norm2_w.rearrange("l (k p) -> (l k) p", p=128), w=['stg'])
    S.dma('sp', stg[64:72, :], pool_scale.rearrange("l (k p) -> (l k) p", p=128), w=['stg'])
    S.dma('sp', stg[72:88, :], hg_lb.rearrange("l d (k p) -> (l d k) p", p=128), w=['stg'])
    for (r0, src) in ((88, q_norm_w), (92, k_norm_w), (96, hg_norm_w)):
        S.dma('sp', stg[r0:r0 + 4, 0:64], src[:, :], w=['stg'])
        S.dma('sp', stg[r0:r0 + 4, 64:128], src[:, :], w=['stg'])
    S.dma('sp', stg[100:108, :], c_d[:, :], w=['stg'])
    S.dma('sp', stg[108:116, :], cc_d[:, :], w=['stg'])
    tr(psb[0][:, 0:128], stg[:], ident_f[:], ['stg', 'ident_f'], ['ps0'])
    cp(cst[:], psb[0][:, 0:128], ['ps0'], ['cst'])
    C_N1, C_N2, C_PS, C_LB, C_QN, C_KN, C_HN, C_C = 0, 32, 64, 72, 88, 92, 96, 100
    act(cT[:], cst[:, C_C:C_C + 16], AF.Silu, ['cst'], ['cT'])
    ex = sb("ex", [128, 16], F32); ssum = sb("ssum", [128, 4], F32)
    act(ex[:], cst[:, C_LB:C_LB + 16], AF.Exp, ['cst'], ['ex'])
    exv = ex[:].rearrange("p (l m) -> p l m", l=4)
    tt(ssum[:], exv[:, 0, :], exv[:, 1, :], ALU.add, ['ex'], ['ssum'])
    tt(ssum[:], ssum[:], exv[:, 2, :], ALU.add, ['ex', 'ssum'], ['ssum'])
    tt(ssum[:], ssum[:], exv[:, 3, :], ALU.add, ['ex', 'ssum'], ['ssum'])
    S.op('dve', lambda e: e.reciprocal(out=ssum[:], in_=ssum[:]), ['ssum'], ['ssum'])
    lbv = lbT[:].rearrange("p (l m) -> p l m", l=4)
    memset(lbT[:], 0.0, ['lbT'])
    tt(lbv[:, 1, :], exv[:, 1, :], ssum[:], ALU.mult, ['ex', 'ssum'], ['lbT'])
    tt(ex[:, 8:12], ex[:, 4:8], ex[:, 8:12], ALU.add, ['ex', 'lbT'], ['ex'])
    tt(lbv[:, 2, :], exv[:, 2, :], ssum[:], ALU.mult, ['ex', 'ssum'], ['lbT'])
    tt(ex[:, 12:16], ex[:, 8:12], ex[:, 12:16], ALU.add, ['ex', 'lbT'], ['ex'])
    tt(lbv[:, 3, :], exv[:, 3, :], ssum[:], ALU.mult, ['ex', 'ssum'], ['lbT'])
    ts(omlb[:], lbT[:], -1.0, 1.0, ALU.mult, ALU.add, ['lbT'], ['omlb'])
    ts(nomlb[:], omlb[:], -1.0, None, ALU.mult, None, ['omlb'], ['nomlb'])

    def adaln_steps(l):
        pk = 7; p = l % 2
        modT = modTs[p]
        wsrc = w_ada[l].rearrange("(kc p) n -> p kc n", p=128)
        nxt = load_w(wsrc[:, :, 0:512], (8, 512), 512)
        yield
        for wt in range(12):
            Wv, wk = nxt
            if wt + 1 < 12:
                nxt = load_w(wsrc[:, :, (wt + 1) * 512:(wt + 2) * 512], (8, 512), 512)
            for jj in range(4):
                j = wt * 4 + jj
                for kc in range(KC):
                    mm(psb[pk][:, 2 * j:2 * j + 2], Wv[:, kc, jj * 128:(jj + 1) * 128],
                       cT[:].rearrange("p (w k) -> p k w", w=2)[:, kc, :], kc == 0, kc == KC - 1, [wk, 'cT'], [f"ps{pk}"])
            yield
        tt(modT[:], psb[pk][:, 0:96].rearrange("p (j w) -> p j w", w=2),
           bada[:, l * 48:(l + 1) * 48].unsqueeze(2).broadcast_to([128, 48, 2]), ALU.add, [f"ps{pk}", 'bada'], [f"modT{p}"])
        for (dst, dkey, scc, ncol) in ((a1s[p], f"a1_{p}", 8, C_N1), (a2s[p], f"a2_{p}", 32, C_N2)):
            for wch in range(2):
                stt(dst[:, :, wch], modT[:, scc:scc + 8, wch], 1.0, cst[:, ncol + l * 8:ncol + l * 8 + 8], ALU.add, ALU.mult,
                    [f"modT{p}", 'cst'], [dkey])
        yield

    def modcol(idx, kc, wch):
        return modTs[CUR['l'] % 2][:, idx * 8 + kc, wch:wch + 1]

    def modkey():
        return f"modT{CUR['l'] % 2}"

    def xload(i):
        src = x_d[i * 128:(i + 1) * 128, :] if i < 16 else ctx_d[(i - 16) * 128:(i - 15) * 128, :]
        S.dma('sp', xs[i % 2][:], src, w=[f"xs{i % 2}"])
    ada0 = adaln_steps(0)
    xload(0)
    for i in range(NT):
        b = i % 2
        next(ada0, None)
        for hh in range(2):
            pk = 2 * b + hh
            for q in range(4):
                kc = hh * 4 + q
                tr(psb[pk][:, q * 128:(q + 1) * 128], xs[b][:, kc * 128:(kc + 1) * 128], ident_f[:], [f"xs{b}", 'ident_f'], [f"ps{pk}"])
            cp(xs2[b][:, hh * 4:hh * 4 + 4, :], psb[pk][:].rearrange("p (q t) -> p q t", q=4), [f"ps{pk}"], [f"xt2_{b}"],
               eng='dve' if hh == 0 else 'dve')
        if i + 1 < NT: xload(i + 1)
        S.dma('sp', XT.rearrange("c p t -> p c t")[:, :, i * 128:(i + 1) * 128], xs2[b][:], r=[f"xt2_{b}"], w=[('XT', ('ld', i))])
    for _ in ada0: pass
    S.barrier()
    sc0.close(); sb = sb0

    XTv = XT.rearrange("c p t -> p c t")
    PSA = Rot([0, 1, 2, 3])

    def norm_phase(a_t, akey, sh_idx):
        sc = Scope(nc); sb = sc.sb
        xblk = [sb(f"xblk{i}", [128, KC, 512], F32) for i in range(2)]
        sqb = [sb(f"sqb{i}", [128, KC, 512], BF16) for i in range(2)]
        rstd_b = [sb(f"rstd{i}", [128, 512], F32) for i in range(2)]
        lnv_b = [sb(f"lnv{i}", [128, 512], F32) for i in range(2)]
        ntmp = [sb(f"ntmp{i}", [128, 512], F32) for i in range(8)]

        def stats(bi):
            t0, n = BLOCKS[bi]; b = bi % 2
            S.dma('sp', xblk[b][:, :, 0:n], XTv[:, :, t0:t0 + n], r=['XT'], w=[f"xblk{b}"])
            act(sqb[b][:, :, 0:n], xblk[b][:, :, 0:n], AF.Square, [f"xblk{b}"], [f"sqb{b}"])
            pk = PSA.get()
            for kc in range(KC):
                mm(psb[pk][:, 0:n], ones_b[:], sqb[b][:, kc, 0:n], kc == 0, kc == KC - 1, ['ones_b', f"sqb{b}"], [f"ps{pk}"])
            act(lnv_b[b][:, 0:n], psb[pk][:, 0:n], AF.Ln, [f"ps{pk}"], [f"lnv{b}"], scale=1.0 / D, bias=EPS)
            act(rstd_b[b][:, 0:n], lnv_b[b][:, 0:n], AF.Exp, [f"lnv{b}"], [f"rstd{b}"], scale=-0.5)

        def apply(bi):
            t0, n = BLOCKS[bi]; b = bi % 2
            wch = 0 if t0 < L else 1
            for kc in range(KC):
                stt(ntmp[kc][:, 0:n], xblk[b][:, kc, 0:n], a_t[:, kc, wch:wch + 1], rstd_b[b][:, 0:n], ALU.mult, ALU.mult,
                    [f"xblk{b}", f"rstd{b}", akey], [f"ntmp{kc}"])
            for kc in range(KC):
                if kc % 2 == 0:
                    act(hT[:, kc, t0:t0 + n], ntmp[kc][:, 0:n], AF.Identity, [f"ntmp{kc}", modkey()], [('hT', (kc, bi))],
                        bias=modcol(sh_idx, kc, wch), scale=1.0)
                else:
                    ts(hT[:, kc, t0:t0 + n], ntmp[kc][:, 0:n], modcol(sh_idx, kc, wch), None, ALU.add, None,
                       [f"ntmp{kc}", modkey()], [('hT', (kc, bi))])
        stats(0)
        for bi in range(len(BLOCKS)):
            if bi + 1 < len(BLOCKS): stats(bi + 1)
            apply(bi)
        S.barrier(); sc.close()

    def tap(name, src_ap, shape, dtp, r):
        if name not in taps: return
        t_ = dram("tap_" + name, shape, dtp, "ExternalOutput")
        tap_d[name] = t_
        S.dma('sp', t_, src_ap, r=r)

    qraw = [sb(f"qraw{i}", [128, 512], F32) for i in range(2)]
    qsq = [sb(f"qsq{i}", [128, 512], BF16) for i in range(2)]
    qn_b = [sb(f"qn{i}", [128, 512], BF16) for i in range(2)]
    qt1 = [sb(f"qt1_{i}", [128, 512], F32) for i in range(2)]
    qt2 = [sb(f"qt2_{i}", [128, 512], F32) for i in range(2)]
    QR = Rot([0, 1])
    A = {}

    def open_attn_scope():
        sc = Scope(nc); sb = sc.sb
        A['qT'] = sb("qT", [128, 4, T], BF16)
        A['kTz'] = sb("kTz", [128, 2, 2, T], BF16)
        A['aT'] = sb("aT", [128, 4, T], BF16)
        A['VE'] = sb("VE", [128, NT, 2, 192], BF16)
        A['pT'] = [sb(f"pT{i}", [128, 512], BF16) for i in range(4)]
        A['rden'] = sb("rden", [128, 512], F32); A['rbc'] = sb("rbc", [128, 512], F32)
        A['tmst'] = [sb(f"tmst{i}", [128, 4, 128], BF16) for i in range(2)]
        memset(A['VE'][:], 1.0, ['VE'])
        memset(A['kTz'][0:64, :, 1, :], 0.0, ['kTz'])
        return sc
    PT = Rot([0, 1, 2, 3]); TMST = Rot([0, 1])

    def qk_epilogue(pk, n, bi, t0, dest, dkey, wcol):
        latent = t0 < L
        i = QR.get()
        act(qn_b[i][:, 0:n], psb[pk][:, 0:n], AF.Copy, [f"ps{pk}", 'cst'], [f"qn{i}"], scale=wcol)
        act(qsq[i][:, 0:n], psb[pk][:, 0:n], AF.Square, [f"ps{pk}"], [f"qsq{i}"])

        def part2():
            p2 = PSQ.get()
            mm(psb[p2][:, 0:n], blk_b[:], qsq[i][:, 0:n], True, True, ['blk_b', f"qsq{i}"], [f"ps{p2}"])
            if latent:
                p3 = PSQ.get()
                mm(psb[p3][:, 0:n], Rt[:], qn_b[i][:, 0:n], True, True, ['Rt', f"qn{i}"], [f"ps{p3}"])
            act(qt1[i][:, 0:n], psb[p2][:, 0:n], AF.Ln, [f"ps{p2}"], [f"qt1_{i}"], scale=1.0 / 64, bias=EPS)
            act(qt1[i][:, 0:n], qt1[i][:, 0:n], AF.Exp, [f"qt1_{i}"], [f"qt1_{i}"], scale=-0.5)
            if latent:
                tt(qraw[i][:, 0:n], qn_b[i][:, 0:n], cosT[:, t0:t0 + n], ALU.mult, [f"qn{i}", 'cosT'], [f"qraw{i}"])
                tt(qt2[i][:, 0:n], psb[p3][:, 0:n], sinT[:, t0:t0 + n], ALU.mult, [f"ps{p3}", 'sinT'], [f"qt2_{i}"])
                tt(qraw[i][:, 0:n], qraw[i][:, 0:n], qt2[i][:, 0:n], ALU.add, [f"qraw{i}", f"qt2_{i}"], [f"qraw{i}"])
                tt(dest, qraw[i][:, 0:n], qt1[i][:, 0:n], ALU.mult, [f"qraw{i}", f"qt1_{i}"], [dkey])
            else:
                tt(dest, qn_b[i][:, 0:n], qt1[i][:, 0:n], ALU.mult, [f"qn{i}", f"qt1_{i}"], [dkey])
        return part2

    PSQ = Rot([4, 5, 6])

    def evac_dram(pk, n, dst_ap, dkey, func, dtp):
        if dtp == F32:
            i = STF.get(); buf = stf[i]; bk = f"stf{i}"
        else:
            i = STB.get(); buf = stb[i]; bk = f"stb{i}"
        if func is None:
            cp(buf[:, 0:n], psb[pk][:, 0:n], [f"ps{pk}"], [bk])
        else:
            act(buf[:, 0:n], psb[pk][:, 0:n], func, [f"ps{pk}"], [bk])
        S.dma('sp', dst_ap, buf[:, 0:n], r=[bk], w=[dkey])

    DEF = {'fn': None}

    def flush_deferred():
        if DEF['fn'] is not None:
            DEF['fn'](); DEF['fn'] = None

    def fm_chunk(Wv, wk, jj, epi):
        for bi, (t0, n) in enumerate(BLOCKS):
            pk = PSA.get()
            for kc in range(KC):
                mm(psb[pk][:, 0:n], Wv[:, kc, jj * 128:(jj + 1) * 128], hT[:, kc, t0:t0 + n], kc == 0, kc == KC - 1,
                   [wk, ('hT', (kc, bi))], [f"ps{pk}"])
            flush_deferred()
            r_ = epi(pk, n, bi, t0)
            DEF['fn'] = r_ if callable(r_) else None

    def tm_chunk(Wv, wk, jj, epi):
        for g4 in range(0, NT, 4):
            pk = PSA.get()
            nt4 = min(4, NT - g4)
            for q in range(nt4):
                i = g4 + q
                bi = min(i // 4, 4)
                for kc in range(KC):
                    mm(psb[pk][:, q * 128:(q + 1) * 128], hT[:, kc, i * 128:(i + 1) * 128], Wv[:, kc, jj * 128:(jj + 1) * 128],
                       kc == 0, kc == KC - 1, [wk, ('hT', (kc, bi))], [f"ps{pk}"])
            epi(pk, g4, nt4)

    def in_proj(l):
        wsrc = w_in[l].rearrange("(kc p) n -> p kc n", p=128)
        qcol = cst[:, C_QN + l:C_QN + l + 1]; kcol = cst[:, C_KN + l:C_KN + l + 1]

        def epi_for(j):
            if j in (0, 1):
                return lambda pk, n, bi, t0: evac_dram(pk, n, U[j, :, ucol(t0):ucol(t0) + n], ('U', (j, bi)), None, F32)
            if 2 <= j <= 5:
                return lambda pk, n, bi, t0: qk_epilogue(pk, n, bi, t0, A['qT'][:, j - 2, t0:t0 + n], ('qT', (j - 2, bi)), qcol)
            if j in (8, 9):
                return lambda pk, n, bi, t0: evac_dram(pk, n, HQ[j - 8, :, t0:t0 + n], ('HQ', (j - 8, bi)), AF.Copy, F32)
            if 12 <= j <= 15:
                return lambda pk, n, bi, t0: evac_dram(pk, n, Z[j - 12, :, t0:t0 + n], ('Z', (j - 12, bi)), None, F32)
            if j in (16, 17):
                return lambda pk, n, bi, t0: evac_dram(pk, n, HGATE[j - 16, :, t0:t0 + n], ('HGATE', (j - 16, bi)), AF.Silu, BF16)
            if j >= 18:
                return lambda pk, n, bi, t0: evac_dram(pk, n, GT[j - 18, :, t0:t0 + n], ('GT', (j - 18, bi)), AF.Sigmoid, BF16)
            return None

        def epi_v(pk, g4, nt4):
            for g in range(2):
                cp(A['VE'][:, g4:g4 + nt4, g, 64:128], psb[pk][:, 0:nt4 * 128].rearrange("p (q c) -> p q c", q=nt4)[:, :, g * 64:(g + 1) * 64],
                   [f"ps{pk}"], ['VE'])

        def epi_hi(hc):
            def f(pk, g4, nt4):
                i = TMST.get()
                act(A['tmst'][i][:, 0:nt4, :], psb[pk][:, 0:nt4 * 128].rearrange("p (q c) -> p q c", q=nt4), AF.Copy, [f"ps{pk}"], [f"tmst{i}"])
                S.dma('sp', VH.rearrange("(i p) c -> p i c", p=128)[:, g4:g4 + nt4, hc * 128:(hc + 1) * 128], A['tmst'][i][:, 0:nt4, :],
                      r=[f"tmst{i}"], w=[('VH', (hc, g4))])
            return f

        i = Wrot.get()
        Wk = Wb[i][:, 0:8 * 256].rearrange("p (a b) -> p a b", a=8)
        for g in range(2):
            for dup in range(2):
                S.dma('pool', Wk[:, :, g * 128 + dup * 64:g * 128 + dup * 64 + 64], wsrc[:, :, 768 + g * 64:768 + g * 64 + 64], w=[f"W{i}"])
        for g in range(2):
            def kepi(pk, n, bi, t0, g=g):
                p2 = qk_epilogue(pk, n, bi, t0, A['kTz'][:, g, 0, t0:t0 + n], ('kTz', (g, bi)), kcol)

                def fin():
                    p2()
                    cp(A['kTz'][64:128, g, 1, t0:t0 + n], A['kTz'][64:128, g, 0, t0:t0 + n], [('kTz', (g, bi))], [('kTz', (g, bi))])
                    memset(A['kTz'][64:128, g, 0, t0:t0 + n], 0.0, [('kTz', (g, bi))])
                return fin
            fm_chunk(Wk, f"W{i}", g, kepi)
        ntile = (IN_W + 511) // 512
        for wt in range(ntile):
            c0 = wt * 512; ncl = min(512, IN_W - c0)
            Wv, wk = load_w(wsrc[:, :, c0:c0 + ncl], (8, ncl), ncl)
            for jj in range(ncl // 128):
                j = wt * 4 + jj
                if j == 6: continue
                if j == 7: tm_chunk(Wv, wk, jj, epi_v)
                elif j in (10, 11): tm_chunk(Wv, wk, jj, epi_hi(j - 10))
                else: fm_chunk(Wv, wk, jj, epi_for(j))
        flush_deferred()

    def attention(l, ada_steps=None):
        LA = 2
        for bi, (t0, n) in enumerate(BLOCKS):
            latent = t0 < L
            if not latent and l == depth - 1 and depth == DEPTH:
                continue
            ktiles = list(range(NT)) if latent else [16, 17]
            items = [(h, ii, i) for h in range(8) for ii, i in enumerate(ktiles)]
            sc_bank = {}

            def emit_score(idx):
                h, ii, i = items[idx]
                g = h // 4; ch = h // 2; pb = (h % 2) * 64
                pk = PSA.get()
                mm(psb[pk][:, 0:n], A['kTz'][:, g, h % 2, i * 128:(i + 1) * 128], A['qT'][:, ch, t0:t0 + n], True, True,
                   [('kTz', (g, min(i // 4, 4))), ('qT', (ch, bi))], [f"ps{pk}"])
                sc_bank[idx] = pk

            def epi2(h):
                ch = h // 2; pb = (h % 2) * 64; po = 4 + (h % 2); r0 = 64 if h % 2 == 0 else 0
                mm(psb[6][:, 0:n], onesf[r0:r0 + 1, :], A['rden'][r0:r0 + 1, 0:n], True, True, ['onesf', ('rden', h % 2)], ['ps6'])
                cp(A['rbc'][pb:pb + 64, 0:n], psb[6][pb:pb + 64, 0:n], ['ps6'], [('rbc', h % 2)])
                tt(A['aT'][pb:pb + 64, ch, t0:t0 + n], psb[po][pb:pb + 64, 0:n], A['rbc'][pb:pb + 64, 0:n], ALU.mult,
                   [f"ps{po}", ('rbc', h % 2)], [('aT', (ch, bi, h % 2))])

            pending = []
            for idx in range(min(LA, len(items))): emit_score(idx)
            for idx, (h, ii, i) in enumerate(items):
                g = h // 4; po = 4 + (h % 2)
                voff = 64 if h % 2 == 0 else 0
                pk = sc_bank.pop(idx)
                pi = PT.get()
                act(A['pT'][pi][:, 0:n], psb[pk][:, 0:n], AF.Exp, [f"ps{pk}"], [f"pT{pi}"], scale=0.125)
                if idx + LA < len(items): emit_score(idx + LA)
                mm(psb[po][:, 0:n], A['VE'][:, i, g, voff:voff + 128], A['pT'][pi][:, 0:n], ii == 0, ii == len(ktiles) - 1,
                   ['VE', f"pT{pi}"], [f"ps{po}"])
                if ii == len(ktiles) - 1:
                    r0 = 64 if h % 2 == 0 else 0
                    S.op('dve', lambda e, r0=r0, po=po: e.reciprocal(out=A['rden'][r0:r0 + 1, 0:n], in_=psb[po][r0:r0 + 1, 0:n]),
                         [f"ps{po}"], [('rden', h % 2)])
                    pending.append((idx + min(8, 2 * len(ktiles) - 2), h))
                while pending and pending[0][0] <= idx:
                    epi2(pending.pop(0)[1])
                if ii == 0 and ada_steps is not None:
                    next(ada_steps, None)
            while pending:
                epi2(pending.pop(0)[1])
        for ch in range(4):
            S.dma('sp', BR[2 + ch, :, :], A['aT'][:, ch, :], r=['aT'], w=[('BR', 2 + ch)])

    def pool_phase(l):
        N = UW
        sc = Scope(nc); sb = sc.sb
        upad = sb("upad", [128, 2, UW], F32)
        s2 = sb("pl_s2", [128, UW], F32); s4 = sb("pl_s4", [128, UW], F32); s8 = sb("pl_s8", [128, UW], F32)
        ypool = sb("ypool", [128, 2, T], BF16)
        yedge = sb("yedge", [128, 16], F32)
        for ch in range(2):
            S.dma('sp', upad[:, ch, :], U[ch, :, :], r=['U'], w=[('upad', ch)])
            u = upad[:, ch, :]
            tt(s2[:, 1:N], u[:, 0:N - 1], u[:, 1:N], ALU.add, [('upad', ch)], ['pl_s2'])
            tt(s4[:, 2:N - 1], s2[:, 1:N - 2], s2[:, 3:N], ALU.add, ['pl_s2'], ['pl_s4'])
            if ch == 0:
                lv = ((s2, 'pl_s2'), (s4, 'pl_s4'))
            else:
                tt(s8[:, 4:N - 3], s4[:, 2:N - 5], s4[:, 6:N - 1], ALU.add, ['pl_s4'], ['pl_s8'])
                tt(s2[:, 8:N - 7], s8[:, 4:N - 11], s8[:, 12:N - 3], ALU.add, ['pl_s8', 'pl_s2'], ['pl_s2'])
                lv = ((s8, 'pl_s8'), (s2, 'pl_s2'))
            for half in range(2):
                w = [2, 4, 8, 16][2 * ch + half]
                rows = slice(64 * half, 64 * half + 64)
                src, skey = lv[half]
                for (ts0, tn, uc) in ((0, L, 8), (L, CT, L + 24)):
                    stt(ypool[rows, ch, ts0:ts0 + tn], src[rows, uc:uc + tn], 1.0 / w, u[rows, uc:uc + tn], ALU.mult, ALU.subtract,
                        [skey, ('upad', ch)], [('ypool', ch)])
                    for (e0, tb) in ((0, 0), (tn - 8, 8)):
                        tt(yedge[rows, 0:8], src[rows, uc + e0:uc + e0 + 8], rcE[rows, ch, tb:tb + 8], ALU.mult, [skey, 'rcE'], ['yedge'])
                        tt(ypool[rows, ch, ts0 + e0:ts0 + e0 + 8], yedge[rows, 0:8], u[rows, uc + e0:uc + e0 + 8], ALU.subtract,
                           ['yedge', ('upad', ch)], [('ypool', ch)])
        for g in range(4):
            ch, half = g // 2, g % 2
            S.dma('pool', pwbd[64 * half:64 * half + 64, ch, 64 * half:64 * half + 64], pool_w[l, g, :, :], w=['pwbd'])
        for ch in range(2):
            for bi, (t0, n) in enumerate(BLOCKS):
                pk = PSA.get()
                mm(psb[pk][:, 0:n], pwbd[:, ch, :], ypool[:, ch, t0:t0 + n], True, True, ['pwbd', ('ypool', ch)], [f"ps{pk}"])
                i = STB.get()
                ts(stb[i][:, 0:n], psb[pk][:, 0:n], cst[:, C_PS + l * 2 + ch:C_PS + l * 2 + ch + 1], None, ALU.mult, None, [f"ps{pk}", 'cst'], [f"stb{i}"])
                S.dma('sp', BR[ch, :, t0:t0 + n], stb[i][:, 0:n], r=[f"stb{i}"], w=[('BR', (ch, bi))])
        S.barrier(); sc.close()

    NCH = T // 16
    VBLK = Rot([0, 1]); ATM = Rot([0, 1, 2])
    Sbf = [hT[:, 4 * d:4 * d + 4, :].rearrange("p a b -> p (a b)").rearrange("p (v n) -> p v n", v=64) for d in range(2)]

    def nat(i, n, d=0):
        if d == 0:
            return (i - 16) * 8 + n if i >= 16 else 16 + i * 8 + n
        return 128 + (i - 16) * 8 + n if i >= 16 else i * 8 + n

    def hgrn_phase(l):
        sc = Scope(nc); sb = sc.sb
        qdec = [sb(f"qdec{d}", [128, T], BF16) for d in range(2)]
        ktil = [sb(f"ktil{d}", [128, T], BF16) for d in range(2)]
        kdecTM = [sb(f"kdecTM{d}", [128, NT, 128], BF16) for d in range(2)]
        abuf = [sb(f"abuf{d}", [128, NCH], F32) for d in range(2)]
        Vh = sb("Vh", [128, NT, 256], BF16)
        S.dma('sp', Vh[:], VH.rearrange("(i p) c -> p i c", p=128), r=['VH'], w=['Vh'])
        for hp in range(2):
            sc1 = Scope(nc); sb = sc1.sb
            hz = sb("hz", [128, T], F32); hlogf = sb("hlogf", [128, T], F32)
            hG = sb("hG", [128, T], F32); hD = sb("hD", [128, T], F32)
            hE = hlogf; hq_sb = sb("hq_sb", [128, T], F32)
            kdecT = sb("kdecT", [128, T], BF16)
            hsg = hz
            S.dma('sp', hq_sb[:], HQ[hp, :, :], r=['HQ'], w=['hq_sb'])
            HALF = T // 2
            for d in range(2):
                lcol = l * 4 + d * 2 + hp
                ocol = omlb[:, lcol:lcol + 1]
                H = [(hf, slice(hf * HALF, (hf + 1) * HALF)) for hf in range(2)]
                for hf, sl in H:
                    S.dma('sp', hz[:, sl], Z[d * 2 + hp, :, sl], r=['Z'], w=[('hz', hf)])
                for hf, sl in H:
                    act(hsg[:, sl], hz[:, sl], AF.Sigmoid, [('hz', hf)], [('hz', hf)])
                for hf, sl in H:
                    ts(hlogf[:, sl], hsg[:, sl], ocol, lbT[:, lcol:lcol + 1], ALU.mult, ALU.add, [('hz', hf), 'omlb', 'lbT'], [('hlogf', hf)])
                for hf, sl in H:
                    act(hlogf[:, sl], hlogf[:, sl], AF.Ln, [('hlogf', hf)], [('hlogf', hf)])
                for hf, sl in H:
                    ts(hz[:, sl], hsg[:, sl], -1.0, 1.0, ALU.mult, ALU.add, [('hz', hf)], [('hz', hf)])
                for hf, sl in H:
                    S.op('dve', lambda e: e.tensor_tensor_scan(out=hG[:, sl], data0=m01[:, sl], data1=hlogf[:, sl], initial=0.0,
                                                              op0=ALU.mult, op1=ALU.add), ['m01', ('hlogf', hf)], [('hG', hf)])
                loff = 16 if d == 0 else 0; coff = 0 if d == 0 else 128
                act(abuf[d][:, loff:loff + 72], hG[:, 15:HALF:16], AF.Exp, [('hG', 0)], [f"abuf{d}"])
                act(abuf[d][:, loff + 72:loff + 128], hG[:, HALF + 15:L:16], AF.Exp, [('hG', 1)], [f"abuf{d}"])
                act(abuf[d][:, coff:coff + 16], hG[:, L + 15:T:16], AF.Exp, [('hG', 1)], [f"abuf{d}"])
                for hf, sl in H:
                    G3 = hG[:, sl].rearrange("p (c s) -> p c s", s=16)
                    tt(hD[:, sl].rearrange("p (c s) -> p c s", s=16), G3[:, :, 15:16].broadcast_to([128, HALF // 16, 16]), G3, ALU.subtract,
                       [('hG', hf)], [('hD', hf)], eng='pool')
                if d == 0:
                    Gd, GLd, gk, glk = hG, hD, 'hG', 'hD'
                else:
                    for hf, sl in H:
                        tt(hD[:, sl], hD[:, sl], hlogf[:, sl], ALU.add, [('hD', hf), ('hlogf', hf)], [('hD', hf)])
                    for hf, sl in H:
                        tt(hG[:, sl], hG[:, sl], hlogf[:, sl], ALU.subtract, [('hG', hf), ('hlogf', hf)], [('hG', hf)])
                    Gd, GLd, gk, glk = hD, hG, 'hD', 'hG'
                for hf, sl in H:
                    act(hE[:, sl], Gd[:, sl], AF.Exp, [(gk, hf)], [('hlogf', hf)])
                for hf, sl in H:
                    tt(qdec[d][:, sl], hq_sb[:, sl], hE[:, sl], ALU.mult, ['hq_sb', ('hlogf', hf)], [(f"qdec{d}", hf)], eng='pool')
                for hf, sl in H:
                    act(hE[:, sl], Gd[:, sl], AF.Exp, [(gk, hf)], [('hlogf', hf)], scale=-1.0)
                for hf, sl in H:
                    stt(ktil[d][:, sl], hz[:, sl], ocol, hE[:, sl], ALU.mult, ALU.mult, [('hz', hf), ('hlogf', hf), 'omlb'], [(f"ktil{d}", hf)])
                for hf, sl in H:
                    act(hE[:, sl], GLd[:, sl], AF.Exp, [(glk, hf)], [('hlogf', hf)])
                for hf, sl in H:
                    stt(kdecT[:, sl], hz[:, sl], ocol, hE[:, sl], ALU.mult, ALU.mult, [('hz', hf), ('hlogf', hf), 'omlb'], [('kdecT', hf)])
                for g3 in range(6):
                    pk = PSA.get()
                    pv = psb[pk][:].bitcast(BF16)
                    for q in range(3):
                        i = g3 * 3 + q
                        tr(pv[:, q * 128:(q + 1) * 128], kdecT[:, i * 128:(i + 1) * 128], ident_b[:], [('kdecT', g3 // 3), 'ident_b'], [f"ps{pk}"])
                    act(kdecTM[d][:, g3 * 3:g3 * 3 + 3, :], pv[:, 0:3 * 128].rearrange("p (q c) -> p q c", q=3), AF.Copy, [f"ps{pk}"], [f"kdecTM{d}"])
            S.barrier(); sc1.close()
            sc2 = Scope(nc); sb = sc2.sb
            hgate_sb = sb("hgate_sb", [128, T], BF16)
            osum = sb("osum", [128, T], F32)
            Vblk = [sb(f"Vblk{i}", [128, 2, 8, 64], BF16) for i in range(2)]
            ATm = [sb(f"ATm{i}", [128, 128], BF16) for i in range(3)]
            kvbufs = [sb(f"kvbuf{d}", [128, 64, NCH], BF16) for d in range(2)]
            VR = 8
            a_rep = sb("a_rep", [128, VR, NCH], F32)
            S.dma('sp', hgate_sb[:], HGATE[hp, :, :], r=['HGATE'], w=['hgate_sb'])
            for i in range(NT):
                vi = VBLK.get()
                tt(Vblk[vi][:], Vh[:, i, hp * 128:(hp + 1) * 128].rearrange("p (h v) -> p h v", h=2).unsqueeze(2).broadcast_to([128, 2, 8, 64]),
                   Emask[:].unsqueeze(1).unsqueeze(3).broadcast_to([128, 2, 8, 64]), ALU.mult, ['Vh', 'Emask'], [f"Vblk{vi}"])
                for d in range(2):
                    j0 = nat(i, 0, d)
                    pk = PSA.get()
                    for h2 in range(2):
                        mm(psb[pk][h2 * 64:(h2 + 1) * 64, :], kdecTM[d][:, i, h2 * 64:(h2 + 1) * 64],
                           Vblk[vi][:, h2, :, :].rearrange("p n v -> p (n v)"), True, True, [f"kdecTM{d}", f"Vblk{vi}"], [f"ps{pk}"])
                    act(kvbufs[d][:, :, j0:j0 + 8], psb[pk][:].rearrange("p (n v) -> p v n", n=8), AF.Copy, [f"ps{pk}"], [f"kvbuf{d}"])
            items = [(i, h2, d) for i in range(NT) for h2 in range(2) for d in range(2)]
            abank = {}

            def emit_A(idx):
                i, h2, d = items[idx]
                rows = slice(h2 * 64, h2 * 64 + 64)
                pk = PSA.get()
                mm(psb[pk][:, 0:128], ktil[d][rows, i * 128:(i + 1) * 128], qdec[d][rows, i * 128:(i + 1) * 128], True, True,
                   [f"ktil{d}", f"qdec{d}"], [f"ps{pk}"])
                abank[idx] = pk
            for idx in range(2): emit_A(idx)
            for idx, (i, h2, d) in enumerate(items):
                rows = slice(h2 * 64, h2 * 64 + 64)
                po = 4 + (i % 2)
                pk = abank.pop(idx)
                ai = ATM.get()
                tt(ATm[ai][:], psb[pk][:, 0:128], (maskF if d == 0 else maskB)[:], ALU.mult, [f"ps{pk}", 'maskF', 'maskB'], [f"ATm{ai}"])
                if idx + 2 < len(items): emit_A(idx + 2)
                mm(psb[po][rows, 0:128], Vh[:, i, hp * 128 + h2 * 64:hp * 128 + h2 * 64 + 64], ATm[ai][:], d == 0, d == 1,
                   ['Vh', f"ATm{ai}"], [(f"ps{po}", h2)])
                if h2 == 1 and d == 1:
                    act(osum[:, i * 128:(i + 1) * 128], psb[po][:, 0:128], AF.Copy, [f"ps{po}"], [('osum', i)])
            for d in range(2):
                kvbuf = kvbufs[d]; kvk = f"kvbuf{d}"
                cp(a_rep[:], abuf[d][:].unsqueeze(1).broadcast_to([128, VR, NCH]), [f"abuf{d}"], ['a_rep'])
                rc = 0 if d == 0 else NCH - 1
                memset(a_rep[:, :, rc:rc + 1], 0.0, ['a_rep'])
                af = a_rep[:].rearrange("p v n -> p (v n)")
                for g4 in range(64 // VR):
                    kf = kvbuf[:, VR * g4:VR * g4 + VR, :].rearrange("p v n -> p (v n)")
                    of = Sbf[d][:, VR * g4:VR * g4 + VR, :].rearrange("p v n -> p (v n)")
                    if d == 0:
                        S.op('dve', lambda e: e.tensor_tensor_scan(out=of, data0=af, data1=kf, initial=0.0, op0=ALU.mult, op1=ALU.add),
                             [kvk, 'a_rep'], [f"Sbf{d}"])
                    else:
                        NF = VR * NCH
                        S.op('dve', lambda e: e.tensor_tensor_scan(out=of[:, NF - 1::-1], data0=af[:, NF - 1::-1], data1=kf[:, NF - 1::-1],
                                                                  initial=0.0, op0=ALU.mult, op1=ALU.add), [kvk, 'a_rep'], [f"Sbf{d}"])
            for i in range(NT):
                po = 4 + (i % 2)
                for h2 in range(2):
                    rows = slice(h2 * 64, h2 * 64 + 64)
                    items = []
                    for d in range(2):
                        for n in range(8):
                            m = nat(i, n, d)
                            if d == 0:
                                if m == 0: continue
                                js = m - 1
                            else:
                                if m == NCH - 1: continue
                                js = m + 1
                            items.append((d, n, js))
                    seen = set()
                    for k, (d, n, js) in enumerate(items):
                        mm(psb[po][rows, n * 16:(n + 1) * 16], Sbf[d][rows, :, js], qdec[d][rows, i * 128 + n * 16:i * 128 + (n + 1) * 16],
                           k == 0, k == len(items) - 1, [f"Sbf{d}", f"qdec{d}"], [(f"ps{po}", h2)])
                tt(osum[:, i * 128:(i + 1) * 128], psb[po][:, 0:128], osum[:, i * 128:(i + 1) * 128], ALU.add, [f"ps{po}", ('osum', i)], [('osum', i)])
            hcol = cst[:, C_HN + l:C_HN + l + 1]
            rset = {}

            def ro_stats(bi):
                t0, n = BLOCKS[bi]
                i = QR.get(); rset[bi] = i
                act(qsq[i][:, 0:n], osum[:, t0:t0 + n], AF.Square, ['osum'], [f"qsq{i}"])
                p2 = PSQ.get()
                mm(psb[p2][:, 0:n], blk_b[:], qsq[i][:, 0:n], True, True, ['blk_b', f"qsq{i}"], [f"ps{p2}"])
                act(qt1[i][:, 0:n], psb[p2][:, 0:n], AF.Ln, [f"ps{p2}"], [f"qt1_{i}"], scale=1.0 / 64, bias=EPS)
                act(qt1[i][:, 0:n], qt1[i][:, 0:n], AF.Exp, [f"qt1_{i}"], [f"qt1_{i}"], scale=-0.5)

            def ro_apply(bi):
                t0, n = BLOCKS[bi]; i = rset[bi]
                stt(qt2[i][:, 0:n], osum[:, t0:t0 + n], hcol, qt1[i][:, 0:n], ALU.mult, ALU.mult, ['osum', f"qt1_{i}", 'cst'], [f"qt2_{i}"])
                si = STB.get()
                tt(stb[si][:, 0:n], qt2[i][:, 0:n], hgate_sb[:, t0:t0 + n], ALU.mult, [f"qt2_{i}", 'hgate_sb'], [f"stb{si}"])
                S.dma('sp', BR[6 + hp, :, t0:t0 + n], stb[si][:, 0:n], r=[f"stb{si}"], w=[('BR', (6 + hp, bi))])
            ro_stats(0)
            for bi in range(len(BLOCKS)):
                if bi + 1 < len(BLOCKS): ro_stats(bi + 1)
                ro_apply(bi)
            if 'osum' in taps and l == 0 and hp == 0: tap('osum', osum[:], [128, T], F32, ['osum'])
            S.barrier(); sc2.close()
        S.barrier(); sc.close()

    GTB = Rot([0, 1]); MT = Rot([0, 1, 2])

    def merge_phase(l):
        S.barrier(pool=True)
        sc = Scope(nc); sb = sc.sb
        MT6 = Rot([0, 1, 2, 3, 4, 5])
        wbr = sb("wbr", [128, 8, D], BF16)
        wo_sb = sb("wo_sb", [128, 8, D], BF16)
        brb = [sb(f"brb{i}", [128, 8, 512], BF16) for i in range(2)]
        gtb = [sb(f"gtb{i}", [128, 3, 512], BF16) for i in range(2)]
        yT = [sb(f"yT{i}", [128, 8, 512], BF16) for i in range(2)]
        mt = [sb(f"mt{i}", [128, 512], F32) for i in range(6)]
        xj = [sb(f"xj{i}", [128, 512], F32) for i in range(2)]; XJ = Rot([0, 1])
        for cbh in range(2):
            cs = slice(cbh * 512, cbh * 512 + 512)
            S.dma('pool', wbr[:, 0:2, cs], w_bp[l].rearrange("(kc p) n -> p kc n", p=128)[:, :, cs], w=[('wbr', (0, cbh))])
            S.dma('pool', wbr[:, 2:6, cs], w_ba[l].rearrange("(kc p) n -> p kc n", p=128)[:, :, cs], w=[('wbr', (1, cbh))])
            S.dma('pool', wbr[:, 6:8, cs], w_bh[l].rearrange("(kc p) n -> p kc n", p=128)[:, :, cs], w=[('wbr', (2, cbh))])
        for cbh in range(2):
            cs = slice(cbh * 512, cbh * 512 + 512)
            for h in range(2):
                S.dma('pool', wo_sb[:, h * 4:h * 4 + 4, cs], w_out[l].rearrange("(kc p) n -> p kc n", p=128)[:, h * 4:h * 4 + 4, cs],
                      w=[('wo_sb', (h, cbh))])
        groups = ((0, 2), (2, 6), (6, 8))
        mblocks = [(bi, t0, n) for bi, (t0, n) in enumerate(BLOCKS) if not (t0 >= L and l == depth - 1 and depth == DEPTH)]

        def load_brb(k):
            bi_, t0_, n_ = mblocks[k]
            S.dma('sp', brb[bi_ % 2][:, :, 0:n_], BR.rearrange("c p t -> p c t")[:, :, t0_:t0_ + n_], r=['BR'], w=[f"brb{bi_ % 2}"])
        def branch_load(k, j):
            bi, t0, n = mblocks[k]
            gi = GTB.get()
            S.dma('sp', gtb[gi][:, :, 0:n], GT.rearrange("(g j) p t -> j p g t", g=3)[j, :, :, t0:t0 + n], r=['GT'], w=[f"gtb{gi}"])
            return gi

        def wout_load(k, j):
            bi, t0, n = mblocks[k]
            xi = XJ.get()
            S.dma('sp', xj[xi][:, 0:n], XT[j, :, t0:t0 + n], r=[('XT', (j, bi))], w=[f"xj{xi}"])
            return xi

        def branch_j(k, j, gi):
            bi, t0, n = mblocks[k]; b = bi % 2
            pks = []
            for gidx, (k0, k1) in enumerate(groups):
                pk = PSA.get() if gidx < 2 else PSQ.get()
                for kc in range(k0, k1):
                    mm(psb[pk][:, 0:n], wbr[:, kc, j * 128:(j + 1) * 128], brb[b][:, kc, 0:n], kc == k0, kc == k1 - 1,
                       [('wbr', (gidx, j // 4)), f"brb{b}"], [f"ps{pk}"])
                pks.append(pk)
            m0 = MT6.get(); m1 = MT6.get(); m2 = MT6.get()
            for mi_, gx in ((m0, 0), (m1, 1), (m2, 2)):
                tt(mt[mi_][:, 0:n], psb[pks[gx]][:, 0:n], gtb[gi][:, gx, 0:n], ALU.mult, [f"ps{pks[gx]}", f"gtb{gi}"], [f"mt{mi_}"])
            tt(mt[m0][:, 0:n], mt[m0][:, 0:n], mt[m1][:, 0:n], ALU.add, [f"mt{m0}", f"mt{m1}"], [f"mt{m0}"], eng='pool')
            tt(yT[b][:, j, 0:n], mt[m0][:, 0:n], mt[m2][:, 0:n], ALU.add, [f"mt{m0}", f"mt{m2}"], [(f"yT{b}", j)], eng='pool')

        def wout_j(k, j, xi):
            bi, t0, n = mblocks[k]; b = bi % 2
            wch = 0 if t0 < L else 1
            pk = PSQ.get()
            for kc in range(KC):
                mm(psb[pk][:, 0:n], wo_sb[:, kc, j * 128:(j + 1) * 128], yT[b][:, kc, 0:n], kc == 0, kc == KC - 1,
                   [('wo_sb', (kc // 4, j // 4)), (f"yT{b}", kc)], [f"ps{pk}"])
            stt(xj[xi][:, 0:n], psb[pk][:, 0:n], modcol(2, j, wch), xj[xi][:, 0:n], ALU.mult, ALU.add,
                [f"ps{pk}", modkey(), f"xj{xi}"], [f"xj{xi}"])
            S.dma('sp', XT[j, :, t0:t0 + n], xj[xi][:, 0:n], r=[f"xj{xi}"], w=[('XT', (j, bi))])

        load_brb(0)
        if len(mblocks) > 1: load_brb(1)
        steps = [('b', 0, j) for j in range(8)]
        for k in range(len(mblocks)):
            for j in range(8):
                if k + 1 < len(mblocks): steps.append(('b', k + 1, j))
                steps.append(('w', k, j))
        loaded = {}

        def issue_load(idx):
            kind, k, j = steps[idx]
            loaded[idx] = branch_load(k, j) if kind == 'b' else wout_load(k, j)
        nb = {'b': 0, 'w': 0}
        pend = []
        for idx in range(len(steps)):
            while pend and False: pass
            la = idx
            while la < len(steps) and la <= idx + 3:
                if la not in loaded:
                    kind = steps[la][0]
                    inflight = sum(1 for q in loaded if q >= idx and steps[q][0] == kind)
                    if inflight < 2: issue_load(la)
                    else: break
                la += 1
            kind, k, j = steps[idx]
            if kind == 'b' and j == 0 and k + 1 < len(mblocks) and k >= 1: load_brb(k + 1)
            if kind == 'b': branch_j(k, j, loaded[idx])
            else: wout_j(k, j, loaded[idx])
        S.barrier(); sc.close()

    SIL = Rot([0, 1])

    def ffn_phase(l, last):
        S.barrier(pool=True)
        sc = Scope(nc); sb = sc.sb
        w2_sb = sb("w2_sb", [128, FC, D], BF16)
        acb = [sb(f"acb{i}", [128, FC, 512], BF16) for i in range(2)]
        sil = [sb(f"sil{i}", [128, 512], F32) for i in range(2)]
        xj = [sb(f"xj{i}", [128, 512], F32) for i in range(2)]; XJ = Rot([0, 1])
        wsrc = w_f1[l].rearrange("(kc p) n -> p kc n", p=128)
        nblk = BLOCKS[:4] if last else BLOCKS
        w2_loads = [(h, cbh) for h in range(0, FC, 2) for cbh in range(2)]

        def w2_load(h, cbh):
            cs = slice(cbh * 512, cbh * 512 + 512)
            S.dma('pool', w2_sb[:, h:h + 2, cs], w_f2[l].rearrange("(kc p) n -> p kc n", p=128)[:, h:h + 2, cs], w=[('w2_sb', (h, cbh))])
        for wt in range(FC // 2):
            if wt >= 2:
                for _ in range(3):
                    if w2_loads: w2_load(*w2_loads.pop(0))
            i = Wrot.get()
            Wv = Wb[i][:, 0:8 * 512].rearrange("p (a b) -> p a b", a=8)
            S.dma('pool', Wv[:, :, 0:256], wsrc[:, :, wt * 256:wt * 256 + 256], w=[f"W{i}"])
            S.dma('pool', Wv[:, :, 256:512], wsrc[:, :, DFF + wt * 256:DFF + wt * 256 + 256], w=[f"W{i}"])
            for jj in range(2):
                fcx = wt * 2 + jj
                for bi, (t0, n) in enumerate(nblk):
                    pa = PSA.get(); pb_ = PSQ.get()
                    for kc in range(KC):
                        mm(psb[pa][:, 0:n], Wv[:, kc, jj * 128:(jj + 1) * 128], hT[:, kc, t0:t0 + n], kc == 0, kc == KC - 1,
                           [f"W{i}", ('hT', (kc, bi))], [f"ps{pa}"])
                    for kc in range(KC):
                        mm(psb[pb_][:, 0:n], Wv[:, kc, 256 + jj * 128:256 + (jj + 1) * 128], hT[:, kc, t0:t0 + n], kc == 0, kc == KC - 1,
                           [f"W{i}", ('hT', (kc, bi))], [f"ps{pb_}"])
                    si = SIL.get()
                    act(sil[si][:, 0:n], psb[pa][:, 0:n], AF.Silu, [f"ps{pa}"], [f"sil{si}"])
                    bi2 = STB.get()
                    tt(stb[bi2][:, 0:n], psb[pb_][:, 0:n], sil[si][:, 0:n], ALU.mult, [f"ps{pb_}", f"sil{si}"], [f"stb{bi2}"])
                    S.dma('sp', ACTS[fcx, :, t0:t0 + n], stb[bi2][:, 0:n], r=[f"stb{bi2}"], w=[('ACTS', (fcx, bi))])
        while w2_loads: w2_load(*w2_loads.pop(0))

        def load_acb(k):
            t0_, n_ = nblk[k]
            for hh in range(2):
                S.dma('sp', acb[k % 2][:, hh * 11:hh * 11 + 11, 0:n_], ACTS.rearrange("c p t -> p c t")[:, hh * 11:hh * 11 + 11, t0_:t0_ + n_],
                      r=['ACTS'], w=[(f"acb{k % 2}", hh)])
        load_acb(0)
        for bi, (t0, n) in enumerate(nblk):
            wch = 0 if t0 < L else 1
            b = bi % 2
            if bi + 1 < len(nblk): load_acb(bi + 1)
            def xload(j_):
                xi_ = XJ.get()
                S.dma('sp', xj[xi_][:, 0:n], XT[j_, :, t0:t0 + n], r=[('XT', (j_, bi))], w=[f"xj{xi_}"])
                return xi_
            xnext = xload(0)
            for j in range(8):
                xi = xnext
                if j + 1 < 8: xnext = xload(j + 1)
                pk = PSA.get()
                for kc in range(FC):
                    mm(psb[pk][:, 0:n], w2_sb[:, kc, j * 128:(j + 1) * 128], acb[b][:, kc, 0:n], kc == 0, kc == FC - 1,
                       ['w2_sb', (f"acb{b}", kc // 11)], [f"ps{pk}"])
                stt(xj[xi][:, 0:n], psb[pk][:, 0:n], modcol(5, j, wch), xj[xi][:, 0:n], ALU.mult, ALU.add,
                    [f"ps{pk}", modkey(), f"xj{xi}"], [f"xj{xi}"])
                S.dma('sp', XT[j, :, t0:t0 + n], xj[xi][:, 0:n], r=[f"xj{xi}"], w=[('XT', (j, bi))])
        S.barrier(); sc.close()

    def run_layers():
        for l in range(depth):
            last = (l == depth - 1) and depth == DEPTH
            CUR['l'] = l
            ada_next = adaln_steps(l + 1) if l + 1 < depth else None
            norm_phase(a1s[l % 2], f"a1_{l % 2}", 0)
            if stop_after == ('norm1', l): return
            asc = open_attn_scope()
            in_proj(l)
            if l == 0 and taps:
                S.barrier()
            if l == 0:
                if 'qT' in taps: tap('qT', A['qT'][:], [128, 4, T], BF16, ['qT'])
                if 'VE' in taps: tap('VE', A['VE'][:], [128, NT, 2, 192], BF16, ['VE'])
                if 'hT' in taps: tap('hT', hT[:], [128, KC, T], BF16, ['hT'])
                S.barrier()
            if stop_after == ('inproj', l):
                asc.close(); return
            attention(l, ada_next)
            if ada_next is not None:
                for _ in ada_next: pass
            S.barrier(); asc.close()
            if stop_after == ('attn', l): return
            pool_phase(l)
            if stop_after == ('pool', l): return
            hgrn_phase(l)
            if stop_after == ('hgrn', l): return
            merge_phase(l)
            if stop_after == ('merge', l): return
            norm_phase(a2s[l % 2], f"a2_{l % 2}", 3)
            ffn_phase(l, last)

    run_layers()
    if 'modT' in taps: tap('modT', modTs[0][:], [128, 48, 2], F32, ['modT0'])
    if 'cosT' in taps: tap('cosT', cosT[:], [128, L], F32, ['cosT'])
    if 'sinT' in taps: tap('sinT', sinT[:], [128, L], F32, ['sinT'])
    for nm, src in (('XT', XT), ('U', U), ('HQ', HQ), ('Z', Z), ('HGATE', HGATE), ('VH', VH), ('GT', GT), ('BR', BR), ('ACTS', ACTS)):
        if nm in taps:
            t_ = dram("tap_" + nm, list(src.shape), src.dtype, "ExternalOutput")
            tap_d[nm] = t_
            S.dma('sp', t_, src, r=[nm])

    scf = Scope(nc)
    xs = [scf.sb(f"xs{i}", [128, D], F32) for i in range(2)]
    xs2 = [scf.sb(f"xt2_{i}", [128, KC, 128], F32) for i in range(2)]
    def oload(i):
        S.dma('sp', xs2[i % 2][:], XTv[:, :, i * 128:(i + 1) * 128], r=['XT'], w=[f"xt2_{i % 2}"])
    oload(0)
    for i in range(16):
        b = i % 2
        for hh in range(2):
            pk = 2 * b + hh
            for q in range(4):
                kc = hh * 4 + q
                tr(psb[pk][:, q * 128:(q + 1) * 128], xs2[b][:, kc, :], ident_f[:], [f"xt2_{b}", 'ident_f'], [f"ps{pk}"])
            cp(xs[b][:, hh * 512:(hh + 1) * 512], psb[pk][:], [f"ps{pk}"], [f"xs{b}"])
        if i + 1 < 16: oload(i + 1)
        S.dma('sp', out_d[i * 128:(i + 1) * 128, :], xs[b][:], r=[f"xs{b}"], w=[('out', i)])
    S.final('sp')
    return nc, list(tap_d.keys())


_IN_NAMES = ["w_ada", "b_ada", "norm1_w", "w_in", "pool_w", "pool_scale", "q_norm_w", "k_norm_w", "hg_lb_logits", "hg_norm_w",
             "w_branch_pool", "w_branch_attn", "w_branch_hg", "w_out", "norm2_w", "w_ffn_in", "w_ffn_out"]


def make_in_maps(inputs):
    f = lambda a: np.ascontiguousarray(np.asarray(a, dtype=np.float32))
    shared = {k: f(inputs[k]) for k in _IN_NAMES}
    x = f(inputs["x"]); c = f(inputs["c"]); ctx = f(inputs["ctx"]); cc = f(inputs["c_ctx"]).reshape(8, 128)
    maps = []
    for b in range(8):
        m = dict(shared)
        m["x"] = x[b]; m["ctx"] = ctx[b]; m["c"] = c[b].reshape(8, 128); m["c_ctx"] = cc
        maps.append(m)
    return maps


def kernel(**inputs):
    nc, _ = build()
    res = run_bass_kernel_spmd(nc, make_in_maps(inputs), core_ids=list(range(8)))
    return np.stack([np.asarray(r["out"], dtype=np.float32) for r in res.results], axis=0)
```

```python
import math
import numpy as np
import concourse.bass as bass
import concourse.mybir as mybir
from concourse.bass_utils import run_bass_kernel_spmd

F32 = mybir.dt.float32; BF16 = mybir.dt.bfloat16; I32 = mybir.dt.int32
AF = mybir.ActivationFunctionType; ALU = mybir.AluOpType
AX = mybir.AxisListType

D = 1024; KC = 8; L = 2048; CT = 256; T = L + CT; DEPTH = 4
NT = T // 128
BLOCKS = [(0, 512), (512, 512), (1024, 512), (1536, 512), (2048, 256)]
IN_W = 5376; DFF = 2816; FC = DFF // 128
EPS = 1e-6
UW = 2336


def ucol(t):
    return t + 8 if t < L else t + 24


class Sched:
    EPOCH = 30000

    def __init__(self, nc, same_sync=True, ndma=12):
        self.nc = nc
        self.E = {'pe': nc.tensor, 'act': nc.scalar, 'dve': nc.vector, 'pool': nc.gpsimd, 'sp': nc.sync}
        self.cnt = {e: 0 for e in self.E}
        self.esem = {e: [] for e in self.E}
        self.waited = {e: {} for e in self.E}
        self.res = {}
        self.dsems = []; self.dval = []
        self.dpool = {}; self.drr = {}
        for q in ('sp', 'pool'):
            self.dpool[q] = []
            for i in range(ndma):
                self.dpool[q].append(len(self.dsems))
                self.dsems.append(nc.alloc_semaphore(f"d_{q}_{i}")); self.dval.append(0)
            self.drr[q] = 0
        self.same_sync = same_sync

    def _wait(self, e, tok, force=False):
        if tok is None: return
        if tok[0] == 'e':
            _, x, ep, v = tok
            if x == e and (e == 'pe' or not (self.same_sync or force)): return
            cur = self.waited[e].get(('e', x), (-1, 0))
            if (ep, v) <= cur: return
            self.waited[e][('e', x)] = (ep, v)
            self.E[e].wait_ge(self.esem[x][ep], v)
        else:
            _, i, v = tok
            if v == 0 or self.waited[e].get(('d', i), 0) >= v: return
            self.waited[e][('d', i)] = v
            self.E[e].wait_ge(self.dsems[i], v)

    @staticmethod
    def _split(key):
        return key if isinstance(key, tuple) else (key, None)

    def _ents(self, key):
        name, sub = self._split(key)
        d = self.res.get(name, {})
        if sub is None: return list(d.values())
        return [d[k] for k in (sub, None) if k in d]

    def _deps(self, r, w):
        deps = []
        for k in r:
            for ent in self._ents(k):
                if ent['w'] is not None: deps.append(ent['w'])
        for k in w:
            for ent in self._ents(k):
                if ent['w'] is not None: deps.append(ent['w'])
                deps.extend(ent['r'].values())
        return deps

    def _record(self, tok, r, w):
        rk = (tok[0], tok[1])
        for k in r:
            name, sub = self._split(k)
            ent = self.res.setdefault(name, {}).setdefault(sub, {'w': None, 'r': {}})
            ent['r'][rk] = tok
        for k in w:
            name, sub = self._split(k)
            if sub is None: self.res[name] = {None: {'w': tok, 'r': {}}}
            else: self.res.setdefault(name, {})[sub] = {'w': tok, 'r': {}}

    def op(self, e, fn, r=(), w=()):
        for tok in self._deps(r, w): self._wait(e, tok)
        inst = fn(self.E[e])
        k = self.cnt[e]; self.cnt[e] += 1
        ep, v = divmod(k, self.EPOCH); v += 1
        while len(self.esem[e]) <= ep:
            self.esem[e].append(self.nc.alloc_semaphore(f"s_{e}_{len(self.esem[e])}"))
        inst.then_inc(self.esem[e][ep], 1)
        tok = ('e', e, ep, v)
        self._record(tok, r, w)
        return tok

    def dma(self, q, out, in_, r=(), w=(), **kw):
        pool = self.dpool[q]; i = pool[self.drr[q] % len(pool)]; self.drr[q] += 1
        self._wait(q, ('d', i, self.dval[i]))
        for tok in self._deps(r, w): self._wait(q, tok)
        inst = self.E[q].dma_start(out=out, in_=in_, **kw)
        self.dval[i] += 16
        inst.then_inc(self.dsems[i], 16)
        tok = ('d', i, self.dval[i])
        self._record(tok, r, w)
        return tok

    def last_tok(self, e):
        k = self.cnt[e] - 1
        if k < 0: return None
        ep, v = divmod(k, self.EPOCH)
        return ('e', e, ep, v + 1)

    def barrier(self, pool=False):
        toks = [self.last_tok(e) for e in ('pe', 'act', 'dve', 'pool')]
        for i in self.dpool['sp']: toks.append(('d', i, self.dval[i]))
        for e in ('pe', 'act', 'dve', 'sp') + (('pool',) if pool else ()):
            for tok in toks: self._wait(e, tok, force=True)

    def final(self, e='sp'):
        toks = [self.last_tok(x) for x in ('pe', 'act', 'dve', 'pool')]
        toks += [('d', i, self.dval[i]) for i in range(len(self.dsems))]
        for tok in toks: self._wait(e, tok, force=True)


class Scope:
    uid = 0

    def __init__(self, nc):
        from contextlib import ExitStack
        self.nc = nc; self.es = ExitStack()

    def sb(self, name, shape, dtp):
        Scope.uid += 1
        return self.es.enter_context(self.nc.sbuf_tensor(f"{name}_{Scope.uid}", list(shape), dtp))

    def close(self):
        self.es.close()


class Rot:
    def __init__(self, items):
        self.items = items; self.i = 0

    def get(self):
        it = self.items[self.i % len(self.items)]; self.i += 1
        return it


def build(depth=DEPTH, taps=(), stop_after=None):
    nc = bass.Bass("TRN2", target_bir_lowering=False)
    S = Sched(nc)
    dram = lambda name, shape, dtp, kind="Internal": nc.dram_tensor(name, list(shape), dtp, kind=kind).ap()
    x_d = dram("x", [L, D], F32, "ExternalInput")
    ctx_d = dram("ctx", [CT, D], F32, "ExternalInput")
    c_d = dram("c", [8, 128], F32, "ExternalInput")
    cc_d = dram("c_ctx", [8, 128], F32, "ExternalInput")
    w_ada = dram("w_ada", [DEPTH, D, 6 * D], F32, "ExternalInput")
    b_ada = dram("b_ada", [DEPTH, 6 * D], F32, "ExternalInput")
    norm1_w = dram("norm1_w", [DEPTH, D], F32, "ExternalInput")
    w_in = dram("w_in", [DEPTH, D, IN_W], F32, "ExternalInput")
    pool_w = dram("pool_w", [DEPTH, 4, 64, 64], F32, "ExternalInput")
    pool_scale = dram("pool_scale", [DEPTH, 256], F32, "ExternalInput")
    q_norm_w = dram("q_norm_w", [DEPTH, 64], F32, "ExternalInput")
    k_norm_w = dram("k_norm_w", [DEPTH, 64], F32, "ExternalInput")
    hg_lb = dram("hg_lb_logits", [DEPTH, 2, 256], F32, "ExternalInput")
    hg_norm_w = dram("hg_norm_w", [DEPTH, 64], F32, "ExternalInput")
    w_bp = dram("w_branch_pool", [DEPTH, 256, D], F32, "ExternalInput")
    w_ba = dram("w_branch_attn", [DEPTH, 512, D], F32, "ExternalInput")
    w_bh = dram("w_branch_hg", [DEPTH, 256, D], F32, "ExternalInput")
    w_out = dram("w_out", [DEPTH, D, D], F32, "ExternalInput")
    norm2_w = dram("norm2_w", [DEPTH, D], F32, "ExternalInput")
    w_f1 = dram("w_ffn_in", [DEPTH, D, 2 * DFF], F32, "ExternalInput")
    w_f2 = dram("w_ffn_out", [DEPTH, DFF, D], F32, "ExternalInput")
    out_d = dram("out", [L, D], F32, "ExternalOutput")
    tap_d = {}
    XT = dram("XT", [8, 128, T], F32)
    U = dram("U", [2, 128, UW], F32)
    HQ = dram("HQ", [2, 128, T], F32)
    Z = dram("Z", [4, 128, T], F32)
    HGATE = dram("HGATE", [2, 128, T], BF16)
    VH = dram("VH", [T, 256], BF16)
    GT = dram("GT", [24, 128, T], BF16)
    BR = dram("BR", [8, 128, T], BF16)
    ACTS = dram("ACTS", [FC, 128, T], BF16)

    def sb(name, shape, dtp):
        return nc.alloc_sbuf_tensor(name, list(shape), dtp)

    Wb = [sb(f"W{i}", [128, 4096], BF16) for i in range(3)]
    Wrot = Rot([0, 1, 2])
    hT = sb("hT", [128, KC, T], BF16)
    ident_f = sb("ident_f", [128, 128], F32); ident_b = sb("ident_b", [128, 128], BF16)
    ones_b = sb("ones_b", [128, 128], BF16); blk_b = sb("blk_b", [128, 128], BF16)
    onesf = sb("onesf", [128, 128], F32); Rt = sb("Rt", [128, 128], BF16)
    cosT = sb("cosT", [128, L], F32); sinT = sb("sinT", [128, L], F32)
    maskF = sb("maskF", [128, 128], BF16); maskB = sb("maskB", [128, 128], BF16)
    Emask = sb("Emask", [128, 8], BF16); Erev = sb("Erev", [128, 8], BF16)
    m01 = sb("m01", [128, T], BF16)
    cst = sb("cst", [128, 128], F32); bada = sb("bada", [128, 192], F32)
    lbT = sb("lbT", [128, 16], F32); omlb = sb("omlb", [128, 16], F32); nomlb = sb("nomlb", [128, 16], F32)
    cT = sb("cT", [128, 16], BF16)
    modTs = [sb(f"modT{i}", [128, 48, 2], F32) for i in range(2)]
    a1s = [sb(f"a1_{i}", [128, 8, 2], F32) for i in range(2)]; a2s = [sb(f"a2_{i}", [128, 8, 2], F32) for i in range(2)]
    CUR = {'l': 0}
    rcE = sb("rcE", [128, 2, 16], F32)
    pwbd = sb("pwbd", [128, 2, 128], BF16)
    psb = [nc.alloc_psum_tensor(f"ps{i}", [128, 512], F32) for i in range(8)]
    stf = [sb(f"stf{i}", [128, 512], F32) for i in range(2)]; STF = Rot([0, 1])
    stb = [sb(f"stb{i}", [128, 512], BF16) for i in range(3)]; STB = Rot([0, 1, 2])

    V = lambda e: e

    def act(out, in_, func, r, w, **kw):
        return S.op('act', lambda e: e.activation(out=out, in_=in_, func=func, **kw), r, w)

    def tt(out, in0, in1, op, r, w, eng='dve'):
        return S.op(eng, lambda e: e.tensor_tensor(out=out, in0=in0, in1=in1, op=op), r, w)

    def ts(out, in0, s1, s2, op0, op1, r, w, eng='dve'):
        if op1 is None:
            return S.op(eng, lambda e: e.tensor_scalar(out=out, in0=in0, scalar1=s1, scalar2=None, op0=op0), r, w)
        return S.op(eng, lambda e: e.tensor_scalar(out=out, in0=in0, scalar1=s1, scalar2=s2, op0=op0, op1=op1), r, w)

    def stt(out, in0, scalar, in1, op0, op1, r, w):
        return S.op('dve', lambda e: e.scalar_tensor_tensor(out=out, in0=in0, scalar=scalar, in1=in1, op0=op0, op1=op1), r, w)

    def cp(out, in_, r, w, eng='dve'):
        return S.op(eng, lambda e: e.tensor_copy(out=out, in_=in_), r, w)

    def mm(out, lhsT, rhs, start, stop, r, w):
        return S.op('pe', lambda e: e.matmul(out, lhsT=lhsT, rhs=rhs, start=start, stop=stop), r, w)

    def tr(out, in_, ident, r, w):
        return S.op('pe', lambda e: e.transpose(out, in_, ident), r, w)

    def memset(ap, val, w, eng='dve'):
        return S.op(eng, lambda e: e.memset(ap, val), (), w)

    def load_w(src, shape_free, ncols_total):
        i = Wrot.get()
        a, b = shape_free
        dst = Wb[i][:, 0:a * b].rearrange("p (a b) -> p a b", a=a)
        S.dma('pool', dst, src, w=[f"W{i}"])
        return dst, f"W{i}"

    sc0 = Scope(nc); sb0 = sb; sb = sc0.sb
    xs = [sb(f"xs{i}", [128, D], F32) for i in range(2)]
    xs2 = [sb(f"xt2_{i}", [128, KC, 128], F32) for i in range(2)]
    iot = sb("iot", [128, 128], I32); pI = sb("pI", [128, 1], I32); cI = sb("cI", [128, 128], I32)
    pc = sb("pc", [128, 1], I32); ccI = sb("ccI", [128, 128], I32)
    tmpA = sb("tmpA", [128, 128], F32); tmpB = sb("tmpB", [128, 128], F32)
    S.op('pool', lambda e: e.iota(iot[:], pattern=[[1, 128]], base=0, channel_multiplier=-1), w=['iot'])
    S.op('pool', lambda e: e.iota(pI[:], pattern=[[0, 1]], base=0, channel_multiplier=1), w=['pI'])
    S.op('pool', lambda e: e.iota(cI[:], pattern=[[1, 128]], base=0, channel_multiplier=0), w=['cI'])
    ts(ident_f[:], iot[:], 0.0, None, ALU.is_equal, None, ['iot'], ['ident_f'])
    cp(ident_b[:], ident_f[:], ['ident_f'], ['ident_b'])
    memset(ones_b[:], 1.0, ['ones_b']); memset(onesf[:], 1.0, ['onesf'])
    memset(blk_b[:], 0.0, ['blk_b'])
    memset(blk_b[0:64, 0:64], 1.0, ['blk_b']); memset(blk_b[64:128, 64:128], 1.0, ['blk_b'])
    ts(tmpA[:], iot[:], 16.0, None, ALU.is_equal, None, ['iot'], ['tmpA'])
    ts(tmpB[:], iot[:], -16.0, None, ALU.is_equal, None, ['iot'], ['tmpB'])
    for b in range(4):
        memset(tmpA[:, 32 * b:32 * b + 16], 0.0, ['tmpA'])
        memset(tmpB[:, 32 * b + 16:32 * b + 32], 0.0, ['tmpB'])
    tt(Rt[:], tmpA[:], tmpB[:], ALU.subtract, ['tmpA', 'tmpB'], ['Rt'])
    ts(pc[:], pI[:], 4, None, ALU.arith_shift_right, None, ['pI'], ['pc'])
    ts(ccI[:], cI[:], 4, None, ALU.arith_shift_right, None, ['cI'], ['ccI'])
    tt(tmpA[:], ccI[:], pc[:, 0:1].broadcast_to([128, 128]), ALU.is_equal, ['ccI', 'pc'], ['tmpA'])
    ts(tmpB[:], iot[:], 0.0, None, ALU.is_ge, None, ['iot'], ['tmpB'])
    tt(maskF[:], tmpA[:], tmpB[:], ALU.mult, ['tmpA', 'tmpB'], ['maskF'])
    ts(tmpB[:], iot[:], 0.0, None, ALU.is_le, None, ['iot', 'maskF'], ['tmpB'])
    tt(maskB[:], tmpA[:], tmpB[:], ALU.mult, ['tmpA', 'tmpB'], ['maskB'])
    tt(Emask[:], cI[:, 0:8], pc[:, 0:1].broadcast_to([128, 8]), ALU.is_equal, ['cI', 'pc'], ['Emask'])
    ts(tmpB[:, 0:8], cI[:, 0:8], -1.0, 7.0, ALU.mult, ALU.add, ['cI', 'maskB'], ['tmpB'])
    tt(Erev[:], tmpB[:, 0:8], pc[:, 0:1].broadcast_to([128, 8]), ALU.is_equal, ['tmpB', 'pc'], ['Erev'])
    big_i = sb("big_i", [128, T], I32); big_f = sb("big_f", [128, T], F32)
    big_g = sb("big_g", [128, T], F32); big_h = sb("big_h", [128, T], F32)
    S.op('pool', lambda e: e.iota(big_i[:], pattern=[[1, T]], base=0, channel_multiplier=0), w=['big_i'])
    ts(big_i[:], big_i[:], 15, None, ALU.bitwise_and, None, ['big_i'], ['big_i'])
    ts(m01[:], big_i[:], 0.0, None, ALU.is_gt, None, ['big_i'], ['m01'])
    freq = sb("freq", [128, 1], F32); jI = sb("jI", [128, 1], I32)
    ts(jI[:], pI[:], 15, None, ALU.bitwise_and, None, ['pI'], ['jI'])
    cp(freq[:], jI[:], ['jI'], ['freq'])
    act(freq[:], freq[:], AF.Exp, ['freq'], ['freq'], scale=-math.log(10000.0) / 16.0)
    for q in range(4):
        pat = [[1, 32], [0, 64]] if q % 2 == 0 else [[0, 32], [1, 64]]
        S.op('pool', lambda e, q=q, pat=pat: e.iota(big_i[32 * q:32 * q + 32, 0:L], pattern=pat, base=0, channel_multiplier=0),
             r=['m01'], w=['big_i'])
    ts(big_f[:, 0:L], big_i[:, 0:L], freq[:, 0:1], None, ALU.mult, None, ['big_i', 'freq'], ['big_f'])
    TWO_PI = 2.0 * math.pi

    def sin_of(dst, shift, dkey):
        ts(big_g[:, 0:L], big_f[:, 0:L], shift, None, ALU.add, None, ['big_f'], ['big_g'])
        ki = big_i[:, 0:L]
        ts(ki, big_g[:, 0:L], 1.0 / TWO_PI, None, ALU.mult, None, ['big_g'], ['big_i'])
        cp(big_h[:, 0:L], ki, ['big_i'], ['big_h'])
        stt(big_g[:, 0:L], big_h[:, 0:L], -TWO_PI, big_g[:, 0:L], ALU.mult, ALU.add, ['big_h', 'big_g'], ['big_g'])
        ts(big_h[:, 0:L], big_g[:, 0:L], math.pi, -TWO_PI, ALU.is_gt, ALU.mult, ['big_g'], ['big_h'])
        tt(big_g[:, 0:L], big_g[:, 0:L], big_h[:, 0:L], ALU.add, ['big_g', 'big_h'], ['big_g'])
        ts(big_h[:, 0:L], big_g[:, 0:L], -math.pi, TWO_PI, ALU.is_lt, ALU.mult, ['big_g'], ['big_h'])
        tt(big_g[:, 0:L], big_g[:, 0:L], big_h[:, 0:L], ALU.add, ['big_g', 'big_h'], ['big_g'])
        ts(big_g[:, 0:L], big_g[:, 0:L], -3.141592, 3.141592, ALU.max, ALU.min, ['big_g'], ['big_g'])
        act(dst, big_g[:, 0:L], AF.Sin, ['big_g'], [dkey])

    sin_of(sinT[:], 0.0, 'sinT')
    sin_of(cosT[:], math.pi / 2.0, 'cosT')
    for ch in range(2):
        for half in range(2):
            w = [2, 4, 8, 16][2 * ch + half]
            rows = slice(64 * half, 64 * half + 64)
            memset(rcE[rows, ch, :], 1.0 / w, ['rcE'], eng='pool')
            for t in range(w // 2):
                memset(rcE[rows, ch, t:t + 1], 1.0 / (t + w // 2), ['rcE'], eng='pool')
            for i in range(8):
                if (8 - i) < w // 2:
                    memset(rcE[rows, ch, 8 + i:9 + i], 1.0 / ((8 - i) + w // 2), ['rcE'], eng='pool')
    memset(pwbd[:], 0.0, ['pwbd'])
    memset(big_h[:], 0.0, ['big_h'])
    if True:
        for ch in range(2):
            S.dma('sp', U[ch, :, 0:T], big_h[:, 0:T], r=['big_h'], w=[('U', ch)])
            S.dma('sp', U[ch, :, T:UW], big_h[:, 0:UW - T], r=['big_h'], w=[('U', ch)])

    stg = sb("stg", [128, 128], F32)
    for half in range(2):
        memset(stg[:], 0.0, ['stg'])
        S.dma('sp', stg[0:96, :], b_ada.rearrange("l (j p) -> (l j) p", p=128)[96 * half:96 * half + 96, :], w=['stg'])
        tr(psb[0][:, 0:128], stg[:], ident_f[:], ['stg', 'ident_f'], ['ps0'])
        cp(bada[:, 96 * half:96 * half + 96], psb[0][:, 0:96], ['ps0'], ['bada'])
    memset(stg[:], 0.0, ['stg'])
    S.dma('sp', stg[0:32, :], norm1_w.rearrange("l (k p) -> (l k) p", p=128), w=['stg'])
    S.dma('sp', stg[32:64, :], norm2_w.rearrange("l (k p) -> (l k) p", p=128), w=['stg'])
    S.dma('sp', stg[64:72, :], pool_scale.rearrange("l (k p) -> (l k) p", p=128), w=['stg'])
    S.dma('sp', stg[72:88, :], hg_lb.rearrange("l d (k p) -> (l d k) p", p=128), w=['stg'])
    for (r0, src) in ((88, q_norm_w), (92, k_norm_w), (96, hg_norm_w)):
        S.dma('sp', stg[r0:r0 + 4, 0:64], src[:, :], w=['stg'])
        S.dma('sp', stg[r0:r0 + 4, 64:128], src[:, :], w=['stg'])
    S.dma('sp', stg[100:108, :], c_d[:, :], w=['stg'])
    S.dma('sp', stg[108:116, :], cc_d[:, :], w=['stg'])
    tr(psb[0][:, 0:128], stg[:], ident_f[:], ['stg', 'ident_f'], ['ps0'])
    cp(cst[:], psb[0][:, 0:128], ['ps0'], ['cst'])
    C_N1, C_N2, C_PS, C_LB, C_QN, C_KN, C_HN, C_C = 0, 32, 64, 72, 88, 92, 96, 100
    act(cT[:], cst[:, C_C:C_C + 16], AF.Silu, ['cst'], ['cT'])
    ex = sb("ex", [128, 16], F32); ssum = sb("ssum", [128, 4], F32)
    act(ex[:], cst[:, C_LB:C_LB + 16], AF.Exp, ['cst'], ['ex'])
    exv = ex[:].rearrange("p (l m) -> p l m", l=4)
    tt(ssum[:], exv[:, 0, :], exv[:, 1, :], ALU.add, ['ex'], ['ssum'])
    tt(ssum[:], ssum[:], exv[:, 2, :], ALU.add, ['ex', 'ssum'], ['ssum'])
    tt(ssum[:], ssum[:], exv[:, 3, :], ALU.add, ['ex', 'ssum'], ['ssum'])
    S.op('dve', lambda e: e.reciprocal(out=ssum[:], in_=ssum[:]), ['ssum'], ['ssum'])
    lbv = lbT[:].rearrange("p (l m) -> p l m", l=4)
    memset(lbT[:], 0.0, ['lbT'])
    tt(lbv[:, 1, :], exv[:, 1, :], ssum[:], ALU.mult, ['ex', 'ssum'], ['lbT'])
    tt(ex[:, 8:12], ex[:, 4:8], ex[:, 8:12], ALU.add, ['ex', 'lbT'], ['ex'])
    tt(lbv[:, 2, :], exv[:, 2, :], ssum[:], ALU.mult, ['ex', 'ssum'], ['lbT'])
    tt(ex[:, 12:16], ex[:, 8:12], ex[:, 12:16], ALU.add, ['ex', 'lbT'], ['ex'])
    tt(lbv[:, 3, :], exv[:, 3, :], ssum[:], ALU.mult, ['ex', 'ssum'], ['lbT'])
    ts(omlb[:], lbT[:], -1.0, 1.0, ALU.mult, ALU.add, ['lbT'], ['omlb'])
    ts(nomlb[:], omlb[:], -1.0, None, ALU.mult, None, ['omlb'], ['nomlb'])

    def adaln_steps(l):
        pk = 7; p = l % 2
        modT = modTs[p]
        wsrc = w_ada[l].rearrange("(kc p) n -> p kc n", p=128)
        nxt = load_w(wsrc[:, :, 0:512], (8, 512), 512)
        yield
        for wt in range(12):
            Wv, wk = nxt
            if wt + 1 < 12:
                nxt = load_w(wsrc[:, :, (wt + 1) * 512:(wt + 2) * 512], (8, 512), 512)
            for jj in range(4):
                j = wt * 4 + jj
                for kc in range(KC):
                    mm(psb[pk][:, 2 * j:2 * j + 2], Wv[:, kc, jj * 128:(jj + 1) * 128],
                       cT[:].rearrange("p (w k) -> p k w", w=2)[:, kc, :], kc == 0, kc == KC - 1, [wk, 'cT'], [f"ps{pk}"])
            yield
        tt(modT[:], psb[pk][:, 0:96].rearrange("p (j w) -> p j w", w=2),
           bada[:, l * 48:(l + 1) * 48].unsqueeze(2).broadcast_to([128, 48, 2]), ALU.add, [f"ps{pk}", 'bada'], [f"modT{p}"])
        for (dst, dkey, scc, ncol) in ((a1s[p], f"a1_{p}", 8, C_N1), (a2s[p], f"a2_{p}", 32, C_N2)):
            for wch in range(2):
                stt(dst[:, :, wch], modT[:, scc:scc + 8, wch], 1.0, cst[:, ncol + l * 8:ncol + l * 8 + 8], ALU.add, ALU.mult,
                    [f"modT{p}", 'cst'], [dkey])
        yield

    def modcol(idx, kc, wch):
        return modTs[CUR['l'] % 2][:, idx * 8 + kc, wch:wch + 1]

    def modkey():
        return f"modT{CUR['l'] % 2}"

    def xload(i):
        src = x_d[i * 128:(i + 1) * 128, :] if i < 16 else ctx_d[(i - 16) * 128:(i - 15) * 128, :]
        S.dma('sp', xs[i % 2][:], src, w=[f"xs{i % 2}"])
    ada0 = adaln_steps(0)
    xload(0)
    for i in range(NT):
        b = i % 2
        next(ada0, None)
        for hh in range(2):
            pk = 2 * b + hh
            for q in range(4):
                kc = hh * 4 + q
                tr(psb[pk][:, q * 128:(q + 1) * 128], xs[b][:, kc * 128:(kc + 1) * 128], ident_f[:], [f"xs{b}", 'ident_f'], [f"ps{pk}"])
            cp(xs2[b][:, hh * 4:hh * 4 + 4, :], psb[pk][:].rearrange("p (q t) -> p q t", q=4), [f"ps{pk}"], [f"xt2_{b}"],
               eng='dve' if hh == 0 else 'dve')
        if i + 1 < NT: xload(i + 1)
        S.dma('sp', XT.rearrange("c p t -> p c t")[:, :, i * 128:(i + 1) * 128], xs2[b][:], r=[f"xt2_{b}"], w=[('XT', ('ld', i))])
    for _ in ada0: pass
    S.barrier()
    sc0.close(); sb = sb0

    XTv = XT.rearrange("c p t -> p c t")
    PSA = Rot([0, 1, 2, 3])

    def norm_phase(a_t, akey, sh_idx):
        sc = Scope(nc); sb = sc.sb
        xblk = [sb(f"xblk{i}", [128, KC, 512], F32) for i in range(2)]
        sqb = [sb(f"sqb{i}", [128, KC, 512], BF16) for i in range(2)]
        rstd_b = [sb(f"rstd{i}", [128, 512], F32) for i in range(2)]
        lnv_b = [sb(f"lnv{i}", [128, 512], F32) for i in range(2)]
        ntmp = [sb(f"ntmp{i}", [128, 512], F32) for i in range(8)]

        def stats(bi):
            t0, n = BLOCKS[bi]; b = bi % 2
            S.dma('sp', xblk[b][:, :, 0:n], XTv[:, :, t0:t0 + n], r=['XT'], w=[f"xblk{b}"])
            act(sqb[b][:, :, 0:n], xblk[b][:, :, 0:n], AF.Square, [f"xblk{b}"], [f"sqb{b}"])
            pk = PSA.get()
            for kc in range(KC):
                mm(psb[pk][:, 0:n], ones_b[:], sqb[b][:, kc, 0:n], kc == 0, kc == KC - 1, ['ones_b', f"sqb{b}"], [f"ps{pk}"])
            act(lnv_b[b][:, 0:n], psb[pk][:, 0:n], AF.Ln, [f"ps{pk}"], [f"lnv{b}"], scale=1.0 / D, bias=EPS)
            act(rstd_b[b][:, 0:n], lnv_b[b][:, 0:n], AF.Exp, [f"lnv{b}"], [f"rstd{b}"], scale=-0.5)

        def apply(bi):
            t0, n = BLOCKS[bi]; b = bi % 2
            wch = 0 if t0 < L else 1
            for kc in range(KC):
                stt(ntmp[kc][:, 0:n], xblk[b][:, kc, 0:n], a_t[:, kc, wch:wch + 1], rstd_b[b][:, 0:n], ALU.mult, ALU.mult,
                    [f"xblk{b}", f"rstd{b}", akey], [f"ntmp{kc}"])
            for kc in range(KC):
                if kc % 2 == 0:
                    act(hT[:, kc, t0:t0 + n], ntmp[kc][:, 0:n], AF.Identity, [f"ntmp{kc}", modkey()], [('hT', (kc, bi))],
                        bias=modcol(sh_idx, kc, wch), scale=1.0)
                else:
                    ts(hT[:, kc, t0:t0 + n], ntmp[kc][:, 0:n], modcol(sh_idx, kc, wch), None, ALU.add, None,
                       [f"ntmp{kc}", modkey()], [('hT', (kc, bi))])
        stats(0)
        for bi in range(len(BLOCKS)):
            if bi + 1 < len(BLOCKS): stats(bi + 1)
            apply(bi)
        S.barrier(); sc.close()

    def tap(name, src_ap, shape, dtp, r):
        if name not in taps: return
        t_ = dram("tap_" + name, shape, dtp, "ExternalOutput")
        tap_d[name] = t_
        S.dma('sp', t_, src_ap, r=r)

    qraw = [sb(f"qraw{i}", [128, 512], F32) for i in range(2)]
    qsq = [sb(f"qsq{i}", [128, 512], BF16) for i in range(2)]
    qn_b = [sb(f"qn{i}", [128, 512], BF16) for i in range(2)]
    qt1 = [sb(f"qt1_{i}", [128, 512], F32) for i in range(2)]
    qt2 = [sb(f"qt2_{i}", [128, 512], F32) for i in range(2)]
    QR = Rot([0, 1])
    A = {}

    def open_attn_scope():
        sc = Scope(nc); sb = sc.sb
        A['qT'] = sb("qT", [128, 4, T], BF16)
        A['kTz'] = sb("kTz", [128, 2, 2, T], BF16)
        A['aT'] = sb("aT", [128, 4, T], BF16)
        A['VE'] = sb("VE", [128, NT, 2, 192], BF16)
        A['pT'] = [sb(f"pT{i}", [128, 512], BF16) for i in range(4)]
        A['rden'] = sb("rden", [128, 512], F32); A['rbc'] = sb("rbc", [128, 512], F32)
        A['tmst'] = [sb(f"tmst{i}", [128, 4, 128], BF16) for i in range(2)]
        memset(A['VE'][:], 1.0, ['VE'])
        memset(A['kTz'][0:64, :, 1, :], 0.0, ['kTz'])
        return sc
    PT = Rot([0, 1, 2, 3]); TMST = Rot([0, 1])

    def qk_epilogue(pk, n, bi, t0, dest, dkey, wcol):
        latent = t0 < L
        i = QR.get()
        act(qn_b[i][:, 0:n], psb[pk][:, 0:n], AF.Copy, [f"ps{pk}", 'cst'], [f"qn{i}"], scale=wcol)
        act(qsq[i][:, 0:n], psb[pk][:, 0:n], AF.Square, [f"ps{pk}"], [f"qsq{i}"])

        def part2():
            p2 = PSQ.get()
            mm(psb[p2][:, 0:n], blk_b[:], qsq[i][:, 0:n], True, True, ['blk_b', f"qsq{i}"], [f"ps{p2}"])
            if latent:
                p3 = PSQ.get()
                mm(psb[p3][:, 0:n], Rt[:], qn_b[i][:, 0:n], True, True, ['Rt', f"qn{i}"], [f"ps{p3}"])
            act(qt1[i][:, 0:n], psb[p2][:, 0:n], AF.Ln, [f"ps{p2}"], [f"qt1_{i}"], scale=1.0 / 64, bias=EPS)
            act(qt1[i][:, 0:n], qt1[i][:, 0:n], AF.Exp, [f"qt1_{i}"], [f"qt1_{i}"], scale=-0.5)
            if latent:
                tt(qraw[i][:, 0:n], qn_b[i][:, 0:n], cosT[:, t0:t0 + n], ALU.mult, [f"qn{i}", 'cosT'], [f"qraw{i}"])
                tt(qt2[i][:, 0:n], psb[p3][:, 0:n], sinT[:, t0:t0 + n], ALU.mult, [f"ps{p3}", 'sinT'], [f"qt2_{i}"])
                tt(qraw[i][:, 0:n], qraw[i][:, 0:n], qt2[i][:, 0:n], ALU.add, [f"qraw{i}", f"qt2_{i}"], [f"qraw{i}"])
                tt(dest, qraw[i][:, 0:n], qt1[i][:, 0:n], ALU.mult, [f"qraw{i}", f"qt1_{i}"], [dkey])
            else:
                tt(dest, qn_b[i][:, 0:n], qt1[i][:, 0:n], ALU.mult, [f"qn{i}", f"qt1_{i}"], [dkey])
        return part2

    PSQ = Rot([4, 5, 6])

    def evac_dram(pk, n, dst_ap, dkey, func, dtp):
        if dtp == F32:
            i = STF.get(); buf = stf[i]; bk = f"stf{i}"
        else:
            i = STB.get(); buf = stb[i]; bk = f"stb{i}"
        if func is None:
            cp(buf[:, 0:n], psb[pk][:, 0:n], [f"ps{pk}"], [bk])
        else:
            act(buf[:, 0:n], psb[pk][:, 0:n], func, [f"ps{pk}"], [bk])
        S.dma('sp', dst_ap, buf[:, 0:n], r=[bk], w=[dkey])

    DEF = {'fn': None}

    def flush_deferred():
        if DEF['fn'] is not None:
            DEF['fn'](); DEF['fn'] = None

    def fm_chunk(Wv, wk, jj, epi):
        for bi, (t0, n) in enumerate(BLOCKS):
            pk = PSA.get()
            for kc in range(KC):
                mm(psb[pk][:, 0:n], Wv[:, kc, jj * 128:(jj + 1) * 128], hT[:, kc, t0:t0 + n], kc == 0, kc == KC - 1,
                   [wk, ('hT', (kc, bi))], [f"ps{pk}"])
            flush_deferred()
            r_ = epi(pk, n, bi, t0)
            DEF['fn'] = r_ if callable(r_) else None

    def tm_chunk(Wv, wk, jj, epi):
        for g4 in range(0, NT, 4):
            pk = PSA.get()
            nt4 = min(4, NT - g4)
            for q in range(nt4):
                i = g4 + q
                bi = min(i // 4, 4)
                for kc in range(KC):
                    mm(psb[pk][:, q * 128:(q + 1) * 128], hT[:, kc, i * 128:(i + 1) * 128], Wv[:, kc, jj * 128:(jj + 1) * 128],
                       kc == 0, kc == KC - 1, [wk, ('hT', (kc, bi))], [f"ps{pk}"])
            epi(pk, g4, nt4)

    def in_proj(l):
        wsrc = w_in[l].rearrange("(kc p) n -> p kc n", p=128)
        qcol = cst[:, C_QN + l:C_QN + l + 1]; kcol = cst[:, C_KN + l:C_KN + l + 1]

        def epi_for(j):
            if j in (0, 1):
                return lambda pk, n, bi, t0: evac_dram(pk, n, U[j, :, ucol(t0):ucol(t0) + n], ('U', (j, bi)), None, F32)
            if 2 <= j <= 5:
                return lambda pk, n, bi, t0: qk_epilogue(pk, n, bi, t0, A['qT'][:, j - 2, t0:t0 + n], ('qT', (j - 2, bi)), qcol)
            if j in (8, 9):
                return lambda pk, n, bi, t0: evac_dram(pk, n, HQ[j - 8, :, t0:t0 + n], ('HQ', (j - 8, bi)), AF.Copy, F32)
            if 12 <= j <= 15:
                return lambda pk, n, bi, t0: evac_dram(pk, n, Z[j - 12, :, t0:t0 + n], ('Z', (j - 12, bi)), None, F32)
            if j in (16, 17):
                return lambda pk, n, bi, t0: evac_dram(pk, n, HGATE[j - 16, :, t0:t0 + n], ('HGATE', (j - 16, bi)), AF.Silu, BF16)
            if j >= 18:
                return lambda pk, n, bi, t0: evac_dram(pk, n, GT[j - 18, :, t0:t0 + n], ('GT', (j - 18, bi)), AF.Sigmoid, BF16)
            return None

        def epi_v(pk, g4, nt4):
            for g in range(2):
                cp(A['VE'][:, g4:g4 + nt4, g, 64:128], psb[pk][:, 0:nt4 * 128].rearrange("p (q c) -> p q c", q=nt4)[:, :, g * 64:(g + 1) * 64],
                   [f"ps{pk}"], ['VE'])

        def epi_hi(hc):
            def f(pk, g4, nt4):
                i = TMST.get()
                act(A['tmst'][i][:, 0:nt4, :], psb[pk][:, 0:nt4 * 128].rearrange("p (q c) -> p q c", q=nt4), AF.Copy, [f"ps{pk}"], [f"tmst{i}"])
                S.dma('sp', VH.rearrange("(i p) c -> p i c", p=128)[:, g4:g4 + nt4, hc * 128:(hc + 1) * 128], A['tmst'][i][:, 0:nt4, :],
                      r=[f"tmst{i}"], w=[('VH', (hc, g4))])
            return f

        i = Wrot.get()
        Wk = Wb[i][:, 0:8 * 256].rearrange("p (a b) -> p a b", a=8)
        for g in range(2):
            for dup in range(2):
                S.dma('pool', Wk[:, :, g * 128 + dup * 64:g * 128 + dup * 64 + 64], wsrc[:, :, 768 + g * 64:768 + g * 64 + 64], w=[f"W{i}"])
        for g in range(2):
            def kepi(pk, n, bi, t0, g=g):
                p2 = qk_epilogue(pk, n, bi, t0, A['kTz'][:, g, 0, t0:t0 + n], ('kTz', (g, bi)), kcol)

                def fin():
                    p2()
                    cp(A['kTz'][64:128, g, 1, t0:t0 + n], A['kTz'][64:128, g, 0, t0:t0 + n], [('kTz', (g, bi))], [('kTz', (g, bi))])
                    memset(A['kTz'][64:128, g, 0, t0:t0 + n], 0.0, [('kTz', (g, bi))])
                return fin
            fm_chunk(Wk, f"W{i}", g, kepi)
        ntile = (IN_W + 511) // 512
        for wt in range(ntile):
            c0 = wt * 512; ncl = min(512, IN_W - c0)
            Wv, wk = load_w(wsrc[:, :, c0:c0 + ncl], (8, ncl), ncl)
            for jj in range(ncl // 128):
                j = wt * 4 + jj
                if j == 6: continue
                if j == 7: tm_chunk(Wv, wk, jj, epi_v)
                elif j in (10, 11): tm_chunk(Wv, wk, jj, epi_hi(j - 10))
                else: fm_chunk(Wv, wk, jj, epi_for(j))
        flush_deferred()

    def attention(l, ada_steps=None):
        LA = 2
        for bi, (t0, n) in enumerate(BLOCKS):
            latent = t0 < L
            if not latent and l == depth - 1 and depth == DEPTH:
                continue
            ktiles = list(range(NT)) if latent else [16, 17]
            items = [(h, ii, i) for h in range(8) for ii, i in enumerate(ktiles)]
            sc_bank = {}

            def emit_score(idx):
                h, ii, i = items[idx]
                g = h // 4; ch = h // 2; pb = (h % 2) * 64
                pk = PSA.get()
                mm(psb[pk][:, 0:n], A['kTz'][:, g, h % 2, i * 128:(i + 1) * 128], A['qT'][:, ch, t0:t0 + n], True, True,
                   [('kTz', (g, min(i // 4, 4))), ('qT', (ch, bi))], [f"ps{pk}"])
                sc_bank[idx] = pk

            def epi2(h):
                ch = h // 2; pb = (h % 2) * 64; po = 4 + (h % 2); r0 = 64 if h % 2 == 0 else 0
                mm(psb[6][:, 0:n], onesf[r0:r0 + 1, :], A['rden'][r0:r0 + 1, 0:n], True, True, ['onesf', ('rden', h % 2)], ['ps6'])
                cp(A['rbc'][pb:pb + 64, 0:n], psb[6][pb:pb + 64, 0:n], ['ps6'], [('rbc', h % 2)])
                tt(A['aT'][pb:pb + 64, ch, t0:t0 + n], psb[po][pb:pb + 64, 0:n], A['rbc'][pb:pb + 64, 0:n], ALU.mult,
                   [f"ps{po}", ('rbc', h % 2)], [('aT', (ch, bi, h % 2))])

            pending = []
            for idx in range(min(LA, len(items))): emit_score(idx)
            for idx, (h, ii, i) in enumerate(items):
                g = h // 4; po = 4 + (h % 2)
                voff = 64 if h % 2 == 0 else 0
                pk = sc_bank.pop(idx)
                pi = PT.get()
                act(A['pT'][pi][:, 0:n], psb[pk][:, 0:n], AF.Exp, [f"ps{pk}"], [f"pT{pi}"], scale=0.125)
                if idx + LA < len(items): emit_score(idx + LA)
                mm(psb[po][:, 0:n], A['VE'][:, i, g, voff:voff + 128], A['pT'][pi][:, 0:n], ii == 0, ii == len(ktiles) - 1,
                   ['VE', f"pT{pi}"], [f"ps{po}"])
                if ii == len(ktiles) - 1:
                    r0 = 64 if h % 2 == 0 else 0
                    S.op('dve', lambda e, r0=r0, po=po: e.reciprocal(out=A['rden'][r0:r0 + 1, 0:n], in_=psb[po][r0:r0 + 1, 0:n]),
                         [f"ps{po}"], [('rden', h % 2)])
                    pending.append((idx + min(8, 2 * len(ktiles) - 2), h))
                while pending and pending[0][0] <= idx:
                    epi2(pending.pop(0)[1])
                if ii == 0 and ada_steps is not None:
                    next(ada_steps, None)
            while pending:
                epi2(pending.pop(0)[1])
        for ch in range(4):
            S.dma('sp', BR[2 + ch, :, :], A['aT'][:, ch, :], r=['aT'], w=[('BR', 2 + ch)])

    def pool_phase(l):
        N = UW
        sc = Scope(nc); sb = sc.sb
        upad = sb("upad", [128, 2, UW], F32)
        s2 = sb("pl_s2", [128, UW], F32); s4 = sb("pl_s4", [128, UW], F32); s8 = sb("pl_s8", [128, UW], F32)
        ypool = sb("ypool", [128, 2, T], BF16)
        yedge = sb("yedge", [128, 16], F32)
        for ch in range(2):
            S.dma('sp', upad[:, ch, :], U[ch, :, :], r=['U'], w=[('upad', ch)])
            u = upad[:, ch, :]
            tt(s2[:, 1:N], u[:, 0:N - 1], u[:, 1:N], ALU.add, [('upad', ch)], ['pl_s2'])
            tt(s4[:, 2:N - 1], s2[:, 1:N - 2], s2[:, 3:N], ALU.add, ['pl_s2'], ['pl_s4'])
            if ch == 0:
                lv = ((s2, 'pl_s2'), (s4, 'pl_s4'))
            else:
                tt(s8[:, 4:N - 3], s4[:, 2:N - 5], s4[:, 6:N - 1], ALU.add, ['pl_s4'], ['pl_s8'])
                tt(s2[:, 8:N - 7], s8[:, 4:N - 11], s8[:, 12:N - 3], ALU.add, ['pl_s8', 'pl_s2'], ['pl_s2'])
                lv = ((s8, 'pl_s8'), (s2, 'pl_s2'))
            for half in range(2):
                w = [2, 4, 8, 16][2 * ch + half]
                rows = slice(64 * half, 64 * half + 64)
                src, skey = lv[half]
                for (ts0, tn, uc) in ((0, L, 8), (L, CT, L + 24)):
                    stt(ypool[rows, ch, ts0:ts0 + tn], src[rows, uc:uc + tn], 1.0 / w, u[rows, uc:uc + tn], ALU.mult, ALU.subtract,
                        [skey, ('upad', ch)], [('ypool', ch)])
                    for (e0, tb) in ((0, 0), (tn - 8, 8)):
                        tt(yedge[rows, 0:8], src[rows, uc + e0:uc + e0 + 8], rcE[rows, ch, tb:tb + 8], ALU.mult, [skey, 'rcE'], ['yedge'])
                        tt(ypool[rows, ch, ts0 + e0:ts0 + e0 + 8], yedge[rows, 0:8], u[rows, uc + e0:uc + e0 + 8], ALU.subtract,
                           ['yedge', ('upad', ch)], [('ypool', ch)])
        for g in range(4):
            ch, half = g // 2, g % 2
            S.dma('pool', pwbd[64 * half:64 * half + 64, ch, 64 * half:64 * half + 64], pool_w[l, g, :, :], w=['pwbd'])
        for ch in range(2):
            for bi, (t0, n) in enumerate(BLOCKS):
                pk = PSA.get()
                mm(psb[pk][:, 0:n], pwbd[:, ch, :], ypool[:, ch, t0:t0 + n], True, True, ['pwbd', ('ypool', ch)], [f"ps{pk}"])
                i = STB.get()
                ts(stb[i][:, 0:n], psb[pk][:, 0:n], cst[:, C_PS + l * 2 + ch:C_PS + l * 2 + ch + 1], None, ALU.mult, None, [f"ps{pk}", 'cst'], [f"stb{i}"])
                S.dma('sp', BR[ch, :, t0:t0 + n], stb[i][:, 0:n], r=[f"stb{i}"], w=[('BR', (ch, bi))])
        S.barrier(); sc.close()

    NCH = T // 16
    VBLK = Rot([0, 1]); ATM = Rot([0, 1, 2])
    Sbf = [hT[:, 4 * d:4 * d + 4, :].rearrange("p a b -> p (a b)").rearrange("p (v n) -> p v n", v=64) for d in range(2)]

    def nat(i, n, d=0):
        if d == 0:
            return (i - 16) * 8 + n if i >= 16 else 16 + i * 8 + n
        return 128 + (i - 16) * 8 + n if i >= 16 else i * 8 + n

    def hgrn_phase(l):
        sc = Scope(nc); sb = sc.sb
        qdec = [sb(f"qdec{d}", [128, T], BF16) for d in range(2)]
        ktil = [sb(f"ktil{d}", [128, T], BF16) for d in range(2)]
        kdecTM = [sb(f"kdecTM{d}", [128, NT, 128], BF16) for d in range(2)]
        abuf = [sb(f"abuf{d}", [128, NCH], F32) for d in range(2)]
        Vh = sb("Vh", [128, NT, 256], BF16)
        S.dma('sp', Vh[:], VH.rearrange("(i p) c -> p i c", p=128), r=['VH'], w=['Vh'])
        for hp in range(2):
            sc1 = Scope(nc); sb = sc1.sb
            hz = sb("hz", [128, T], F32); hlogf = sb("hlogf", [128, T], F32)
            hG = sb("hG", [128, T], F32); hD = sb("hD", [128, T], F32)
            hE = hlogf; hq_sb = sb("hq_sb", [128, T], F32)
            kdecT = sb("kdecT", [128, T], BF16)
            hsg = hz
            S.dma('sp', hq_sb[:], HQ[hp, :, :], r=['HQ'], w=['hq_sb'])
            HALF = T // 2
            for d in range(2):
                lcol = l * 4 + d * 2 + hp
                ocol = omlb[:, lcol:lcol + 1]
                H = [(hf, slice(hf * HALF, (hf + 1) * HALF)) for hf in range(2)]
                for hf, sl in H:
                    S.dma('sp', hz[:, sl], Z[d * 2 + hp, :, sl], r=['Z'], w=[('hz', hf)])
                for hf, sl in H:
                    act(hsg[:, sl], hz[:, sl], AF.Sigmoid, [('hz', hf)], [('hz', hf)])
                for hf, sl in H:
                    ts(hlogf[:, sl], hsg[:, sl], ocol, lbT[:, lcol:lcol + 1], ALU.mult, ALU.add, [('hz', hf), 'omlb', 'lbT'], [('hlogf', hf)])
                for hf, sl in H:
                    act(hlogf[:, sl], hlogf[:, sl], AF.Ln, [('hlogf', hf)], [('hlogf', hf)])
                for hf, sl in H:
                    ts(hz[:, sl], hsg[:, sl], -1.0, 1.0, ALU.mult, ALU.add, [('hz', hf)], [('hz', hf)])
                for hf, sl in H:
                    S.op('dve', lambda e: e.tensor_tensor_scan(out=hG[:, sl], data0=m01[:, sl], data1=hlogf[:, sl], initial=0.0,
                                                              op0=ALU.mult, op1=ALU.add), ['m01', ('hlogf', hf)], [('hG', hf)])
                loff = 16 if d == 0 else 0; coff = 0 if d == 0 else 128
                act(abuf[d][:, loff:loff + 72], hG[:, 15:HALF:16], AF.Exp, [('hG', 0)], [f"abuf{d}"])
                act(abuf[d][:, loff + 72:loff + 128], hG[:, HALF + 15:L:16], AF.Exp, [('hG', 1)], [f"abuf{d}"])
                act(abuf[d][:, coff:coff + 16], hG[:, L + 15:T:16], AF.Exp, [('hG', 1)], [f"abuf{d}"])
                for hf, sl in H:
                    G3 = hG[:, sl].rearrange("p (c s) -> p c s", s=16)
                    tt(hD[:, sl].rearrange("p (c s) -> p c s", s=16), G3[:, :, 15:16].broadcast_to([128, HALF // 16, 16]), G3, ALU.subtract,
                       [('hG', hf)], [('hD', hf)], eng='pool')
                if d == 0:
                    Gd, GLd, gk, glk = hG, hD, 'hG', 'hD'
                else:
                    for hf, sl in H:
                        tt(hD[:, sl], hD[:, sl], hlogf[:, sl], ALU.add, [('hD', hf), ('hlogf', hf)], [('hD', hf)])
                    for hf, sl in H:
                        tt(hG[:, sl], hG[:, sl], hlogf[:, sl], ALU.subtract, [('hG', hf), ('hlogf', hf)], [('hG', hf)])
                    Gd, GLd, gk, glk = hD, hG, 'hD', 'hG'
                for hf, sl in H:
                    act(hE[:, sl], Gd[:, sl], AF.Exp, [(gk, hf)], [('hlogf', hf)])
                for hf, sl in H:
                    tt(qdec[d][:, sl], hq_sb[:, sl], hE[:, sl], ALU.mult, ['hq_sb', ('hlogf', hf)], [(f"qdec{d}", hf)], eng='pool')
                for hf, sl in H:
                    act(hE[:, sl], Gd[:, sl], AF.Exp, [(gk, hf)], [('hlogf', hf)], scale=-1.0)
                for hf, sl in H:
                    stt(ktil[d][:, sl], hz[:, sl], ocol, hE[:, sl], ALU.mult, ALU.mult, [('hz', hf), ('hlogf', hf), 'omlb'], [(f"ktil{d}", hf)])
                for hf, sl in H:
                    act(hE[:, sl], GLd[:, sl], AF.Exp, [(glk, hf)], [('hlogf', hf)])
                for hf, sl in H:
                    stt(kdecT[:, sl], hz[:, sl], ocol, hE[:, sl], ALU.mult, ALU.mult, [('hz', hf), ('hlogf', hf), 'omlb'], [('kdecT', hf)])
                for g3 in range(6):
                    pk = PSA.get()
                    pv = psb[pk][:].bitcast(BF16)
                    for q in range(3):
                        i = g3 * 3 + q
                        tr(pv[:, q * 128:(q + 1) * 128], kdecT[:, i * 128:(i + 1) * 128], ident_b[:], [('kdecT', g3 // 3), 'ident_b'], [f"ps{pk}"])
                    act(kdecTM[d][:, g3 * 3:g3 * 3 + 3, :], pv[:, 0:3 * 128].rearrange("p (q c) -> p q c", q=3), AF.Copy, [f"ps{pk}"], [f"kdecTM{d}"])
            S.barrier(); sc1.close()
            sc2 = Scope(nc); sb = sc2.sb
            hgate_sb = sb("hgate_sb", [128, T], BF16)
            osum = sb("osum", [128, T], F32)
            Vblk = [sb(f"Vblk{i}", [128, 2, 8, 64], BF16) for i in range(2)]
            ATm = [sb(f"ATm{i}", [128, 128], BF16) for i in range(3)]
            kvbufs = [sb(f"kvbuf{d}", [128, 64, NCH], BF16) for d in range(2)]
            VR = 8
            a_rep = sb("a_rep", [128, VR, NCH], F32)
            S.dma('sp', hgate_sb[:], HGATE[hp, :, :], r=['HGATE'], w=['hgate_sb'])
            PSH = Rot([0, 1, 2])
            ACC = [3, 4, 5, 6, 7]
            started = set()

            def acc(i, h2):
                bank = ACC[i // 4]; col = (i % 4) * 128
                first = (bank, h2) not in started
                started.add((bank, h2))
                return bank, col, first
            for i in range(NT):
                vi = VBLK.get()
                tt(Vblk[vi][:], Vh[:, i, hp * 128:(hp + 1) * 128].rearrange("p (h v) -> p h v", h=2).unsqueeze(2).broadcast_to([128, 2, 8, 64]),
                   Emask[:].unsqueeze(1).unsqueeze(3).broadcast_to([128, 2, 8, 64]), ALU.mult, ['Vh', 'Emask'], [f"Vblk{vi}"])
                for d in range(2):
                    j0 = nat(i, 0, d)
                    pk = PSH.get()
                    for h2 in range(2):
                        mm(psb[pk][h2 * 64:(h2 + 1) * 64, :], kdecTM[d][:, i, h2 * 64:(h2 + 1) * 64],
                           Vblk[vi][:, h2, :, :].rearrange("p n v -> p (n v)"), True, True, [f"kdecTM{d}", f"Vblk{vi}"], [f"ps{pk}"])
                    act(kvbufs[d][:, :, j0:j0 + 8], psb[pk][:].rearrange("p (n v) -> p v n", n=8), AF.Copy, [f"ps{pk}"], [f"kvbuf{d}"])
            items = [(i, h2, d) for i in range(NT) for h2 in range(2) for d in range(2)]
            abank = {}

            def emit_A(idx):
                i, h2, d = items[idx]
                rows = slice(h2 * 64, h2 * 64 + 64)
                pk = PSH.get()
                mm(psb[pk][:, 0:128], ktil[d][rows, i * 128:(i + 1) * 128], qdec[d][rows, i * 128:(i + 1) * 128], True, True,
                   [f"ktil{d}", f"qdec{d}"], [f"ps{pk}"])
                abank[idx] = pk
            for idx in range(2): emit_A(idx)
            for idx, (i, h2, d) in enumerate(items):
                rows = slice(h2 * 64, h2 * 64 + 64)
                pk = abank.pop(idx)
                ai = ATM.get()
                tt(ATm[ai][:], psb[pk][:, 0:128], (maskF if d == 0 else maskB)[:], ALU.mult, [f"ps{pk}", 'maskF', 'maskB'], [f"ATm{ai}"])
                if idx + 2 < len(items): emit_A(idx + 2)
                bank, col, first = acc(i, h2)
                mm(psb[bank][rows, col:col + 128], Vh[:, i, hp * 128 + h2 * 64:hp * 128 + h2 * 64 + 64], ATm[ai][:], first, False,
                   ['Vh', f"ATm{ai}"], [(f"ps{bank}", h2)])
            for d in range(2):
                kvbuf = kvbufs[d]; kvk = f"kvbuf{d}"
                cp(a_rep[:], abuf[d][:].unsqueeze(1).broadcast_to([128, VR, NCH]), [f"abuf{d}"], ['a_rep'])
                rc = 0 if d == 0 else NCH - 1
                memset(a_rep[:, :, rc:rc + 1], 0.0, ['a_rep'])
                af = a_rep[:].rearrange("p v n -> p (v n)")
                for g4 in range(64 // VR):
                    kf = kvbuf[:, VR * g4:VR * g4 + VR, :].rearrange("p v n -> p (v n)")
                    of = Sbf[d][:, VR * g4:VR * g4 + VR, :].rearrange("p v n -> p (v n)")
                    if d == 0:
                        S.op('dve', lambda e: e.tensor_tensor_scan(out=of, data0=af, data1=kf, initial=0.0, op0=ALU.mult, op1=ALU.add),
                             [kvk, 'a_rep'], [f"Sbf{d}"])
                    else:
                        NF = VR * NCH
                        S.op('dve', lambda e: e.tensor_tensor_scan(out=of[:, NF - 1::-1], data0=af[:, NF - 1::-1], data1=kf[:, NF - 1::-1],
                                                                  initial=0.0, op0=ALU.mult, op1=ALU.add), [kvk, 'a_rep'], [f"Sbf{d}"])
                for i in range(NT):
                    for h2 in range(2):
                        rows = slice(h2 * 64, h2 * 64 + 64)
                        bank, col, first = acc(i, h2)
                        for n in range(8):
                            m = nat(i, n, d)
                            if d == 0:
                                if m == 0: continue
                                js = m - 1
                            else:
                                if m == NCH - 1: continue
                                js = m + 1
                            last = (d == 1) and (i == NT - 1 or i % 4 == 3) and n == 7
                            mm(psb[bank][rows, col + n * 16:col + (n + 1) * 16], Sbf[d][rows, :, js],
                               qdec[d][rows, i * 128 + n * 16:i * 128 + (n + 1) * 16], False, last, [f"Sbf{d}", f"qdec{d}"], [(f"ps{bank}", h2)])
            for i in range(NT):
                bank, col, _ = acc(i, 0)
                act(osum[:, i * 128:(i + 1) * 128], psb[bank][:, col:col + 128], AF.Copy, [f"ps{bank}"], [('osum', i)])
            hcol = cst[:, C_HN + l:C_HN + l + 1]
            for bi, (t0, n) in enumerate(BLOCKS):
                i = QR.get()
                act(qsq[i][:, 0:n], osum[:, t0:t0 + n], AF.Square, ['osum'], [f"qsq{i}"])
                p2 = PSQ.get()
                mm(psb[p2][:, 0:n], blk_b[:], qsq[i][:, 0:n], True, True, ['blk_b', f"qsq{i}"], [f"ps{p2}"])
                act(qt1[i][:, 0:n], psb[p2][:, 0:n], AF.Ln, [f"ps{p2}"], [f"qt1_{i}"], scale=1.0 / 64, bias=EPS)
                act(qt1[i][:, 0:n], qt1[i][:, 0:n], AF.Exp, [f"qt1_{i}"], [f"qt1_{i}"], scale=-0.5)
                stt(qt2[i][:, 0:n], osum[:, t0:t0 + n], hcol, qt1[i][:, 0:n], ALU.mult, ALU.mult, ['osum', f"qt1_{i}", 'cst'], [f"qt2_{i}"])
                si = STB.get()
                tt(stb[si][:, 0:n], qt2[i][:, 0:n], hgate_sb[:, t0:t0 + n], ALU.mult, [f"qt2_{i}", 'hgate_sb'], [f"stb{si}"])
                S.dma('sp', BR[6 + hp, :, t0:t0 + n], stb[si][:, 0:n], r=[f"stb{si}"], w=[('BR', (6 + hp, bi))])
            if 'osum' in taps and l == 0 and hp == 0: tap('osum', osum[:], [128, T], F32, ['osum'])
            S.barrier(); sc2.close()
        S.barrier(); sc.close()

    GTB = Rot([0, 1]); MT = Rot([0, 1, 2])

    def merge_phase(l):
        S.barrier(pool=True)
        sc = Scope(nc); sb = sc.sb
        wbr = sb("wbr", [128, 8, D], BF16)
        wo_sb = sb("wo_sb", [128, 8, D], BF16)
        brb = [sb(f"brb{i}", [128, 8, 512], BF16) for i in range(2)]
        gtb = [sb(f"gtb{i}", [128, 3, 512], BF16) for i in range(2)]
        yT = [sb(f"yT{i}", [128, 8, 512], BF16) for i in range(2)]
        mt = [sb(f"mt{i}", [128, 512], F32) for i in range(3)]
        xj = [sb(f"xj{i}", [128, 512], F32) for i in range(2)]; XJ = Rot([0, 1])
        c2 = [sb(f"c2_{i}", [128, 512], F32) for i in range(2)]; C2 = Rot([0, 1])
        for cbh in range(2):
            cs = slice(cbh * 512, cbh * 512 + 512)
            S.dma('pool', wbr[:, 0:2, cs], w_bp[l].rearrange("(kc p) n -> p kc n", p=128)[:, :, cs], w=[('wbr', (0, cbh))])
            S.dma('pool', wbr[:, 2:6, cs], w_ba[l].rearrange("(kc p) n -> p kc n", p=128)[:, :, cs], w=[('wbr', (1, cbh))])
            S.dma('pool', wbr[:, 6:8, cs], w_bh[l].rearrange("(kc p) n -> p kc n", p=128)[:, :, cs], w=[('wbr', (2, cbh))])
        for cbh in range(2):
            cs = slice(cbh * 512, cbh * 512 + 512)
            for h in range(2):
                S.dma('pool', wo_sb[:, h * 4:h * 4 + 4, cs], w_out[l].rearrange("(kc p) n -> p kc n", p=128)[:, h * 4:h * 4 + 4, cs],
                      w=[('wo_sb', (h, cbh))])
        groups = ((0, 2), (2, 6), (6, 8))
        mblocks = [(bi, t0, n) for bi, (t0, n) in enumerate(BLOCKS) if not (t0 >= L and l == depth - 1 and depth == DEPTH)]

        def load_brb(k):
            bi_, t0_, n_ = mblocks[k]
            S.dma('sp', brb[bi_ % 2][:, :, 0:n_], BR.rearrange("c p t -> p c t")[:, :, t0_:t0_ + n_], r=['BR'], w=[f"brb{bi_ % 2}"])
        def branch_load(k, j):
            bi, t0, n = mblocks[k]
            gi = GTB.get()
            S.dma('sp', gtb[gi][:, :, 0:n], GT.rearrange("(g j) p t -> j p g t", g=3)[j, :, :, t0:t0 + n], r=['GT'], w=[f"gtb{gi}"])
            return gi

        def wout_load(k, j):
            bi, t0, n = mblocks[k]
            xi = XJ.get()
            S.dma('sp', xj[xi][:, 0:n], XT[j, :, t0:t0 + n], r=[('XT', (j, bi))], w=[f"xj{xi}"])
            return xi

        def branch_j(k, j, gi):
            bi, t0, n = mblocks[k]; b = bi % 2
            pks = []
            for gidx, (k0, k1) in enumerate(groups):
                pk = PSA.get() if gidx < 2 else PSQ.get()
                for kc in range(k0, k1):
                    mm(psb[pk][:, 0:n], wbr[:, kc, j * 128:(j + 1) * 128], brb[b][:, kc, 0:n], kc == k0, kc == k1 - 1,
                       [('wbr', (gidx, j // 4)), f"brb{b}"], [f"ps{pk}"])
                pks.append(pk)
            m0 = MT.get(); m1 = MT.get(); ci = C2.get()
            tt(mt[m0][:, 0:n], psb[pks[0]][:, 0:n], gtb[gi][:, 0, 0:n], ALU.mult, [f"ps{pks[0]}", f"gtb{gi}"], [f"mt{m0}"])
            tt(mt[m1][:, 0:n], psb[pks[1]][:, 0:n], gtb[gi][:, 1, 0:n], ALU.mult, [f"ps{pks[1]}", f"gtb{gi}"], [f"mt{m1}"])
            act(c2[ci][:, 0:n], psb[pks[2]][:, 0:n], AF.Copy, [f"ps{pks[2]}"], [f"c2_{ci}"])
            tt(c2[ci][:, 0:n], c2[ci][:, 0:n], gtb[gi][:, 2, 0:n], ALU.mult, [f"c2_{ci}", f"gtb{gi}"], [f"c2_{ci}"], eng='pool')
            tt(mt[m0][:, 0:n], mt[m0][:, 0:n], mt[m1][:, 0:n], ALU.add, [f"mt{m0}", f"mt{m1}"], [f"mt{m0}"])
            tt(yT[b][:, j, 0:n], mt[m0][:, 0:n], c2[ci][:, 0:n], ALU.add, [f"mt{m0}", f"c2_{ci}"], [(f"yT{b}", j)], eng='pool')

        def wout_j(k, j, xi):
            bi, t0, n = mblocks[k]; b = bi % 2
            wch = 0 if t0 < L else 1
            pk = PSQ.get()
            for kc in range(KC):
                mm(psb[pk][:, 0:n], wo_sb[:, kc, j * 128:(j + 1) * 128], yT[b][:, kc, 0:n], kc == 0, kc == KC - 1,
                   [('wo_sb', (kc // 4, j // 4)), (f"yT{b}", kc)], [f"ps{pk}"])
            stt(xj[xi][:, 0:n], psb[pk][:, 0:n], modcol(2, j, wch), xj[xi][:, 0:n], ALU.mult, ALU.add,
                [f"ps{pk}", modkey(), f"xj{xi}"], [f"xj{xi}"])
            S.dma('sp', XT[j, :, t0:t0 + n], xj[xi][:, 0:n], r=[f"xj{xi}"], w=[('XT', (j, bi))])

        load_brb(0)
        if len(mblocks) > 1: load_brb(1)
        steps = [('b', 0, j) for j in range(8)]
        for k in range(len(mblocks)):
            for j in range(8):
                if k + 1 < len(mblocks): steps.append(('b', k + 1, j))
                steps.append(('w', k, j))
        loaded = {}

        def issue_load(idx):
            kind, k, j = steps[idx]
            loaded[idx] = branch_load(k, j) if kind == 'b' else wout_load(k, j)
        nb = {'b': 0, 'w': 0}
        pend = []
        for idx in range(len(steps)):
            while pend and False: pass
            la = idx
            while la < len(steps) and la <= idx + 3:
                if la not in loaded:
                    kind = steps[la][0]
                    inflight = sum(1 for q in loaded if q >= idx and steps[q][0] == kind)
                    if inflight < 2: issue_load(la)
                    else: break
                la += 1
            kind, k, j = steps[idx]
            if kind == 'b' and j == 0 and k + 1 < len(mblocks) and k >= 1: load_brb(k + 1)
            if kind == 'b': branch_j(k, j, loaded[idx])
            else: wout_j(k, j, loaded[idx])
        S.barrier(); sc.close()

    SIL = Rot([0, 1])

    def ffn_phase(l, last):
        S.barrier(pool=True)
        sc = Scope(nc); sb = sc.sb
        w2_sb = sb("w2_sb", [128, FC, D], BF16)
        acb = [sb(f"acb{i}", [128, FC, 512], BF16) for i in range(2)]
        sil = [sb(f"sil{i}", [128, 512], F32) for i in range(2)]
        xj = [sb(f"xj{i}", [128, 512], F32) for i in range(2)]; XJ = Rot([0, 1])
        wsrc = w_f1[l].rearrange("(kc p) n -> p kc n", p=128)
        nblk = BLOCKS[:4] if last else BLOCKS
        w2_loads = [(h, cbh) for h in range(0, FC, 2) for cbh in range(2)]

        def w2_load(h, cbh):
            cs = slice(cbh * 512, cbh * 512 + 512)
            S.dma('pool', w2_sb[:, h:h + 2, cs], w_f2[l].rearrange("(kc p) n -> p kc n", p=128)[:, h:h + 2, cs], w=[('w2_sb', (h, cbh))])
        for wt in range(FC // 2):
            if wt >= 2:
                for _ in range(3):
                    if w2_loads: w2_load(*w2_loads.pop(0))
            i = Wrot.get()
            Wv = Wb[i][:, 0:8 * 512].rearrange("p (a b) -> p a b", a=8)
            S.dma('pool', Wv[:, :, 0:256], wsrc[:, :, wt * 256:wt * 256 + 256], w=[f"W{i}"])
            S.dma('pool', Wv[:, :, 256:512], wsrc[:, :, DFF + wt * 256:DFF + wt * 256 + 256], w=[f"W{i}"])
            for jj in range(2):
                fcx = wt * 2 + jj
                for bi, (t0, n) in enumerate(nblk):
                    pa = PSA.get(); pb_ = PSQ.get()
                    for kc in range(KC):
                        mm(psb[pa][:, 0:n], Wv[:, kc, jj * 128:(jj + 1) * 128], hT[:, kc, t0:t0 + n], kc == 0, kc == KC - 1,
                           [f"W{i}", ('hT', (kc, bi))], [f"ps{pa}"])
                    for kc in range(KC):
                        mm(psb[pb_][:, 0:n], Wv[:, kc, 256 + jj * 128:256 + (jj + 1) * 128], hT[:, kc, t0:t0 + n], kc == 0, kc == KC - 1,
                           [f"W{i}", ('hT', (kc, bi))], [f"ps{pb_}"])
                    si = SIL.get()
                    act(sil[si][:, 0:n], psb[pa][:, 0:n], AF.Silu, [f"ps{pa}"], [f"sil{si}"])
                    bi2 = STB.get()
                    tt(stb[bi2][:, 0:n], psb[pb_][:, 0:n], sil[si][:, 0:n], ALU.mult, [f"ps{pb_}", f"sil{si}"], [f"stb{bi2}"])
                    S.dma('sp', ACTS[fcx, :, t0:t0 + n], stb[bi2][:, 0:n], r=[f"stb{bi2}"], w=[('ACTS', (fcx, bi))])
        while w2_loads: w2_load(*w2_loads.pop(0))

        def load_acb(k):
            t0_, n_ = nblk[k]
            for hh in range(2):
                S.dma('sp', acb[k % 2][:, hh * 11:hh * 11 + 11, 0:n_], ACTS.rearrange("c p t -> p c t")[:, hh * 11:hh * 11 + 11, t0_:t0_ + n_],
                      r=['ACTS'], w=[(f"acb{k % 2}", hh)])
        load_acb(0)
        for bi, (t0, n) in enumerate(nblk):
            wch = 0 if t0 < L else 1
            b = bi % 2
            if bi + 1 < len(nblk): load_acb(bi + 1)
            def xload(j_):
                xi_ = XJ.get()
                S.dma('sp', xj[xi_][:, 0:n], XT[j_, :, t0:t0 + n], r=[('XT', (j_, bi))], w=[f"xj{xi_}"])
                return xi_
            xnext = xload(0)
            for j in range(8):
                xi = xnext
                if j + 1 < 8: xnext = xload(j + 1)
                pk = PSA.get()
                for kc in range(FC):
                    mm(psb[pk][:, 0:n], w2_sb[:, kc, j * 128:(j + 1) * 128], acb[b][:, kc, 0:n], kc == 0, kc == FC - 1,
                       ['w2_sb', (f"acb{b}", kc // 11)], [f"ps{pk}"])
                stt(xj[xi][:, 0:n], psb[pk][:, 0:n], modcol(5, j, wch), xj[xi][:, 0:n], ALU.mult, ALU.add,
                    [f"ps{pk}", modkey(), f"xj{xi}"], [f"xj{xi}"])
                S.dma('sp', XT[j, :, t0:t0 + n], xj[xi][:, 0:n], r=[f"xj{xi}"], w=[('XT', (j, bi))])
        S.barrier(); sc.close()

    def run_layers():
        for l in range(depth):
            last = (l == depth - 1) and depth == DEPTH
            CUR['l'] = l
            ada_next = adaln_steps(l + 1) if l + 1 < depth else None
            norm_phase(a1s[l % 2], f"a1_{l % 2}", 0)
            if stop_after == ('norm1', l): return
            asc = open_attn_scope()
            in_proj(l)
            S.barrier()
            if l == 0:
                if 'qT' in taps: tap('qT', A['qT'][:], [128, 4, T], BF16, ['qT'])
                if 'VE' in taps: tap('VE', A['VE'][:], [128, NT, 2, 192], BF16, ['VE'])
                if 'hT' in taps: tap('hT', hT[:], [128, KC, T], BF16, ['hT'])
                S.barrier()
            if stop_after == ('inproj', l):
                asc.close(); return
            attention(l, ada_next)
            if ada_next is not None:
                for _ in ada_next: pass
            S.barrier(); asc.close()
            if stop_after == ('attn', l): return
            pool_phase(l)
            if stop_after == ('pool', l): return
            hgrn_phase(l)
            if stop_after == ('hgrn', l): return
            merge_phase(l)
            if stop_after == ('merge', l): return
            norm_phase(a2s[l % 2], f"a2_{l % 2}", 3)
            ffn_phase(l, last)

    run_layers()
    if 'modT' in taps: tap('modT', modTs[0][:], [128, 48, 2], F32, ['modT0'])
    if 'cosT' in taps: tap('cosT', cosT[:], [128, L], F32, ['cosT'])
    if 'sinT' in taps: tap('sinT', sinT[:], [128, L], F32, ['sinT'])
    for nm, src in (('XT', XT), ('U', U), ('HQ', HQ), ('Z', Z), ('HGATE', HGATE), ('VH', VH), ('GT', GT), ('BR', BR), ('ACTS', ACTS)):
        if nm in taps:
            t_ = dram("tap_" + nm, list(src.shape), src.dtype, "ExternalOutput")
            tap_d[nm] = t_
            S.dma('sp', t_, src, r=[nm])

    scf = Scope(nc)
    xs = [scf.sb(f"xs{i}", [128, D], F32) for i in range(2)]
    xs2 = [scf.sb(f"xt2_{i}", [128, KC, 128], F32) for i in range(2)]
    def oload(i):
        S.dma('sp', xs2[i % 2][:], XTv[:, :, i * 128:(i + 1) * 128], r=['XT'], w=[f"xt2_{i % 2}"])
    oload(0)
    for i in range(16):
        b = i % 2
        for hh in range(2):
            pk = 2 * b + hh
            for q in range(4):
                kc = hh * 4 + q
                tr(psb[pk][:, q * 128:(q + 1) * 128], xs2[b][:, kc, :], ident_f[:], [f"xt2_{b}", 'ident_f'], [f"ps{pk}"])
            cp(xs[b][:, hh * 512:(hh + 1) * 512], psb[pk][:], [f"ps{pk}"], [f"xs{b}"])
        if i + 1 < 16: oload(i + 1)
        S.dma('sp', out_d[i * 128:(i + 1) * 128, :], xs[b][:], r=[f"xs{b}"], w=[('out', i)])
    S.final('sp')
    return nc, list(tap_d.keys())


_IN_NAMES = ["w_ada", "b_ada", "norm1_w", "w_in", "pool_w", "pool_scale", "q_norm_w", "k_norm_w", "hg_lb_logits", "hg_norm_w",
             "w_branch_pool", "w_branch_attn", "w_branch_hg", "w_out", "norm2_w", "w_ffn_in", "w_ffn_out"]


def make_in_maps(inputs):
    f = lambda a: np.ascontiguousarray(np.asarray(a, dtype=np.float32))
    shared = {k: f(inputs[k]) for k in _IN_NAMES}
    x = f(inputs["x"]); c = f(inputs["c"]); ctx = f(inputs["ctx"]); cc = f(inputs["c_ctx"]).reshape(8, 128)
    maps = []
    for b in range(8):
        m = dict(shared)
        m["x"] = x[b]; m["ctx"] = ctx[b]; m["c"] = c[b].reshape(8, 128); m["c_ctx"] = cc
        maps.append(m)
    return maps


def kernel(**inputs):
    nc, _ = build()
    res = run_bass_kernel_spmd(nc, make_in_maps(inputs), core_ids=list(range(8)))
    return np.stack([np.asarray(r["out"], dtype=np.float32) for r in res.results], axis=0)
```

```python
import math
import numpy as np
import concourse.bass as bass
import concourse.mybir as mybir
from concourse.bass_utils import run_bass_kernel_spmd

F32 = mybir.dt.float32; BF16 = mybir.dt.bfloat16; I32 = mybir.dt.int32
AF = mybir.ActivationFunctionType; ALU = mybir.AluOpType
AX = mybir.AxisListType

D = 1024; KC = 8; L = 2048; CT = 256; T = L + CT; DEPTH = 4
NT = T // 128
BLOCKS = [(0, 512), (512, 512), (1024, 512), (1536, 512), (2048, 256)]
IN_W = 5376; DFF = 2816; FC = DFF // 128
EPS = 1e-6
UW = 2336


def ucol(t):
    return t + 8 if t < L else t + 24


class Sched:
    EPOCH = 30000

    def __init__(self, nc, same_sync=True, ndma=12):
        self.nc = nc
        self.E = {'pe': nc.tensor, 'act': nc.scalar, 'dve': nc.vector, 'pool': nc.gpsimd, 'sp': nc.sync}
        self.cnt = {e: 0 for e in self.E}
        self.esem = {e: [] for e in self.E}
        self.waited = {e: {} for e in self.E}
        self.res = {}
        self.dsems = []; self.dval = []
        self.dpool = {}; self.drr = {}
        for q in ('sp', 'pool'):
            self.dpool[q] = []
            for i in range(ndma):
                self.dpool[q].append(len(self.dsems))
                self.dsems.append(nc.alloc_semaphore(f"d_{q}_{i}")); self.dval.append(0)
            self.drr[q] = 0
        self.same_sync = same_sync

    def _wait(self, e, tok, force=False):
        if tok is None: return
        if tok[0] == 'e':
            _, x, ep, v = tok
            if x == e and (e == 'pe' or not (self.same_sync or force)): return
            cur = self.waited[e].get(('e', x), (-1, 0))
            if (ep, v) <= cur: return
            self.waited[e][('e', x)] = (ep, v)
            self.E[e].wait_ge(self.esem[x][ep], v)
        else:
            _, i, v = tok
            if v == 0 or self.waited[e].get(('d', i), 0) >= v: return
            self.waited[e][('d', i)] = v
            self.E[e].wait_ge(self.dsems[i], v)

    @staticmethod
    def _split(key):
        return key if isinstance(key, tuple) else (key, None)

    def _ents(self, key):
        name, sub = self._split(key)
        d = self.res.get(name, {})
        if sub is None: return list(d.values())
        return [d[k] for k in (sub, None) if k in d]

    def _deps(self, r, w):
        deps = []
        for k in r:
            for ent in self._ents(k):
                if ent['w'] is not None: deps.append(ent['w'])
        for k in w:
            for ent in self._ents(k):
                if ent['w'] is not None: deps.append(ent['w'])
                deps.extend(ent['r'].values())
        return deps

    def _record(self, tok, r, w):
        rk = (tok[0], tok[1])
        for k in r:
            name, sub = self._split(k)
            ent = self.res.setdefault(name, {}).setdefault(sub, {'w': None, 'r': {}})
            ent['r'][rk] = tok
        for k in w:
            name, sub = self._split(k)
            if sub is None: self.res[name] = {None: {'w': tok, 'r': {}}}
            else: self.res.setdefault(name, {})[sub] = {'w': tok, 'r': {}}

    def op(self, e, fn, r=(), w=()):
        for tok in self._deps(r, w): self._wait(e, tok)
        inst = fn(self.E[e])
        k = self.cnt[e]; self.cnt[e] += 1
        ep, v = divmod(k, self.EPOCH); v += 1
        while len(self.esem[e]) <= ep:
            self.esem[e].append(self.nc.alloc_semaphore(f"s_{e}_{len(self.esem[e])}"))
        inst.then_inc(self.esem[e][ep], 1)
        tok = ('e', e, ep, v)
        self._record(tok, r, w)
        return tok

    def dma(self, q, out, in_, r=(), w=(), **kw):
        pool = self.dpool[q]; i = pool[self.drr[q] % len(pool)]; self.drr[q] += 1
        self._wait(q, ('d', i, self.dval[i]))
        for tok in self._deps(r, w): self._wait(q, tok)
        inst = self.E[q].dma_start(out=out, in_=in_, **kw)
        self.dval[i] += 16
        inst.then_inc(self.dsems[i], 16)
        tok = ('d', i, self.dval[i])
        self._record(tok, r, w)
        return tok

    def last_tok(self, e):
        k = self.cnt[e] - 1
        if k < 0: return None
        ep, v = divmod(k, self.EPOCH)
        return ('e', e, ep, v + 1)

    def barrier(self, pool=False):
        toks = [self.last_tok(e) for e in ('pe', 'act', 'dve', 'pool')]
        for i in self.dpool['sp']: toks.append(('d', i, self.dval[i]))
        for e in ('pe', 'act', 'dve', 'sp') + (('pool',) if pool else ()):
            for tok in toks: self._wait(e, tok, force=True)

    def final(self, e='sp'):
        toks = [self.last_tok(x) for x in ('pe', 'act', 'dve', 'pool')]
        toks += [('d', i, self.dval[i]) for i in range(len(self.dsems))]
        for tok in toks: self._wait(e, tok, force=True)


class Scope:
    uid = 0

    def __init__(self, nc):
        from contextlib import ExitStack
        self.nc = nc; self.es = ExitStack()

    def sb(self, name, shape, dtp):
        Scope.uid += 1
        return self.es.enter_context(self.nc.sbuf_tensor(f"{name}_{Scope.uid}", list(shape), dtp))

    def close(self):
        self.es.close()


class Rot:
    def __init__(self, items):
        self.items = items; self.i = 0

    def get(self):
        it = self.items[self.i % len(self.items)]; self.i += 1
        return it


def build(depth=DEPTH, taps=(), stop_after=None):
    nc = bass.Bass("TRN2", target_bir_lowering=False)
    S = Sched(nc)
    dram = lambda name, shape, dtp, kind="Internal": nc.dram_tensor(name, list(shape), dtp, kind=kind).ap()
    x_d = dram("x", [L, D], F32, "ExternalInput")
    ctx_d = dram("ctx", [CT, D], F32, "ExternalInput")
    c_d = dram("c", [8, 128], F32, "ExternalInput")
    cc_d = dram("c_ctx", [8, 128], F32, "ExternalInput")
    w_ada = dram("w_ada", [DEPTH, D, 6 * D], F32, "ExternalInput")
    b_ada = dram("b_ada", [DEPTH, 6 * D], F32, "ExternalInput")
    norm1_w = dram("norm1_w", [DEPTH, D], F32, "ExternalInput")
    w_in = dram("w_in", [DEPTH, D, IN_W], F32, "ExternalInput")
    pool_w = dram("pool_w", [DEPTH, 4, 64, 64], F32, "ExternalInput")
    pool_scale = dram("pool_scale", [DEPTH, 256], F32, "ExternalInput")
    q_norm_w = dram("q_norm_w", [DEPTH, 64], F32, "ExternalInput")
    k_norm_w = dram("k_norm_w", [DEPTH, 64], F32, "ExternalInput")
    hg_lb = dram("hg_lb_logits", [DEPTH, 2, 256], F32, "ExternalInput")
    hg_norm_w = dram("hg_norm_w", [DEPTH, 64], F32, "ExternalInput")
    w_bp = dram("w_branch_pool", [DEPTH, 256, D], F32, "ExternalInput")
    w_ba = dram("w_branch_attn", [DEPTH, 512, D], F32, "ExternalInput")
    w_bh = dram("w_branch_hg", [DEPTH, 256, D], F32, "ExternalInput")
    w_out = dram("w_out", [DEPTH, D, D], F32, "ExternalInput")
    norm2_w = dram("norm2_w", [DEPTH, D], F32, "ExternalInput")
    w_f1 = dram("w_ffn_in", [DEPTH, D, 2 * DFF], F32, "ExternalInput")
    w_f2 = dram("w_ffn_out", [DEPTH, DFF, D], F32, "ExternalInput")
    out_d = dram("out", [L, D], F32, "ExternalOutput")
    tap_d = {}
    XT = dram("XT", [8, 128, T], F32)
    U = dram("U", [2, 128, UW], F32)
    HQ = dram("HQ", [2, 128, T], F32)
    Z = dram("Z", [4, 128, T], F32)
    HGATE = dram("HGATE", [2, 128, T], BF16)
    VH = dram("VH", [T, 256], BF16)
    GT = dram("GT", [24, 128, T], BF16)
    BR = dram("BR", [8, 128, T], BF16)
    ACTS = dram("ACTS", [FC, 128, T], BF16)

    def sb(name, shape, dtp):
        return nc.alloc_sbuf_tensor(name, list(shape), dtp)

    Wb = [sb(f"W{i}", [128, 4096], BF16) for i in range(3)]
    Wrot = Rot([0, 1, 2])
    hT = sb("hT", [128, KC, T], BF16)
    ident_f = sb("ident_f", [128, 128], F32); ident_b = sb("ident_b", [128, 128], BF16)
    ones_b = sb("ones_b", [128, 128], BF16); blk_b = sb("blk_b", [128, 128], BF16)
    onesf = sb("onesf", [128, 128], F32); Rt = sb("Rt", [128, 128], BF16)
    cosT = sb("cosT", [128, L], F32); sinT = sb("sinT", [128, L], F32)
    maskF = sb("maskF", [128, 128], BF16); maskB = sb("maskB", [128, 128], BF16)
    Emask = sb("Emask", [128, 8], BF16); Erev = sb("Erev", [128, 8], BF16)
    m01 = sb("m01", [128, T], BF16)
    cst = sb("cst", [128, 128], F32); bada = sb("bada", [128, 192], F32)
    lbT = sb("lbT", [128, 16], F32); omlb = sb("omlb", [128, 16], F32); nomlb = sb("nomlb", [128, 16], F32)
    cT = sb("cT", [128, 16], BF16)
    modTs = [sb(f"modT{i}", [128, 48, 2], F32) for i in range(2)]
    a1s = [sb(f"a1_{i}", [128, 8, 2], F32) for i in range(2)]; a2s = [sb(f"a2_{i}", [128, 8, 2], F32) for i in range(2)]
    CUR = {'l': 0}
    rcE = sb("rcE", [128, 2, 16], F32)
    pwbd = sb("pwbd", [128, 2, 128], BF16)
    psb = [nc.alloc_psum_tensor(f"ps{i}", [128, 512], F32) for i in range(8)]
    stf = [sb(f"stf{i}", [128, 512], F32) for i in range(2)]; STF = Rot([0, 1])
    stb = [sb(f"stb{i}", [128, 512], BF16) for i in range(3)]; STB = Rot([0, 1, 2])

    V = lambda e: e

    def act(out, in_, func, r, w, **kw):
        return S.op('act', lambda e: e.activation(out=out, in_=in_, func=func, **kw), r, w)

    def tt(out, in0, in1, op, r, w, eng='dve'):
        return S.op(eng, lambda e: e.tensor_tensor(out=out, in0=in0, in1=in1, op=op), r, w)

    def ts(out, in0, s1, s2, op0, op1, r, w, eng='dve'):
        if op1 is None:
            return S.op(eng, lambda e: e.tensor_scalar(out=out, in0=in0, scalar1=s1, scalar2=None, op0=op0), r, w)
        return S.op(eng, lambda e: e.tensor_scalar(out=out, in0=in0, scalar1=s1, scalar2=s2, op0=op0, op1=op1), r, w)

    def stt(out, in0, scalar, in1, op0, op1, r, w):
        return S.op('dve', lambda e: e.scalar_tensor_tensor(out=out, in0=in0, scalar=scalar, in1=in1, op0=op0, op1=op1), r, w)

    def cp(out, in_, r, w, eng='dve'):
        return S.op(eng, lambda e: e.tensor_copy(out=out, in_=in_), r, w)

    def mm(out, lhsT, rhs, start, stop, r, w):
        return S.op('pe', lambda e: e.matmul(out, lhsT=lhsT, rhs=rhs, start=start, stop=stop), r, w)

    def tr(out, in_, ident, r, w):
        return S.op('pe', lambda e: e.transpose(out, in_, ident), r, w)

    def memset(ap, val, w, eng='dve'):
        return S.op(eng, lambda e: e.memset(ap, val), (), w)

    def load_w(src, shape_free, ncols_total):
        i = Wrot.get()
        a, b = shape_free
        dst = Wb[i][:, 0:a * b].rearrange("p (a b) -> p a b", a=a)
        S.dma('pool', dst, src, w=[f"W{i}"])
        return dst, f"W{i}"

    sc0 = Scope(nc); sb0 = sb; sb = sc0.sb
    xs = [sb(f"xs{i}", [128, D], F32) for i in range(2)]
    xs2 = [sb(f"xt2_{i}", [128, KC, 128], F32) for i in range(2)]
    iot = sb("iot", [128, 128], I32); pI = sb("pI", [128, 1], I32); cI = sb("cI", [128, 128], I32)
    pc = sb("pc", [128, 1], I32); ccI = sb("ccI", [128, 128], I32)
    tmpA = sb("tmpA", [128, 128], F32); tmpB = sb("tmpB", [128, 128], F32)
    S.op('pool', lambda e: e.iota(iot[:], pattern=[[1, 128]], base=0, channel_multiplier=-1), w=['iot'])
    S.op('pool', lambda e: e.iota(pI[:], pattern=[[0, 1]], base=0, channel_multiplier=1), w=['pI'])
    S.op('pool', lambda e: e.iota(cI[:], pattern=[[1, 128]], base=0, channel_multiplier=0), w=['cI'])
    ts(ident_f[:], iot[:], 0.0, None, ALU.is_equal, None, ['iot'], ['ident_f'])
    cp(ident_b[:], ident_f[:], ['ident_f'], ['ident_b'])
    memset(ones_b[:], 1.0, ['ones_b']); memset(onesf[:], 1.0, ['onesf'])
    memset(blk_b[:], 0.0, ['blk_b'])
    memset(blk_b[0:64, 0:64], 1.0, ['blk_b']); memset(blk_b[64:128, 64:128], 1.0, ['blk_b'])
    ts(tmpA[:], iot[:], 16.0, None, ALU.is_equal, None, ['iot'], ['tmpA'])
    ts(tmpB[:], iot[:], -16.0, None, ALU.is_equal, None, ['iot'], ['tmpB'])
    for b in range(4):
        memset(tmpA[:, 32 * b:32 * b + 16], 0.0, ['tmpA'])
        memset(tmpB[:, 32 * b + 16:32 * b + 32], 0.0, ['tmpB'])
    tt(Rt[:], tmpA[:], tmpB[:], ALU.subtract, ['tmpA', 'tmpB'], ['Rt'])
    ts(pc[:], pI[:], 4, None, ALU.arith_shift_right, None, ['pI'], ['pc'])
    ts(ccI[:], cI[:], 4, None, ALU.arith_shift_right, None, ['cI'], ['ccI'])
    tt(tmpA[:], ccI[:], pc[:, 0:1].broadcast_to([128, 128]), ALU.is_equal, ['ccI', 'pc'], ['tmpA'])
    ts(tmpB[:], iot[:], 0.0, None, ALU.is_ge, None, ['iot'], ['tmpB'])
    tt(maskF[:], tmpA[:], tmpB[:], ALU.mult, ['tmpA', 'tmpB'], ['maskF'])
    ts(tmpB[:], iot[:], 0.0, None, ALU.is_le, None, ['iot', 'maskF'], ['tmpB'])
    tt(maskB[:], tmpA[:], tmpB[:], ALU.mult, ['tmpA', 'tmpB'], ['maskB'])
    tt(Emask[:], cI[:, 0:8], pc[:, 0:1].broadcast_to([128, 8]), ALU.is_equal, ['cI', 'pc'], ['Emask'])
    ts(tmpB[:, 0:8], cI[:, 0:8], -1.0, 7.0, ALU.mult, ALU.add, ['cI', 'maskB'], ['tmpB'])
    tt(Erev[:], tmpB[:, 0:8], pc[:, 0:1].broadcast_to([128, 8]), ALU.is_equal, ['tmpB', 'pc'], ['Erev'])
    big_i = sb("big_i", [128, T], I32); big_f = sb("big_f", [128, T], F32)
    big_g = sb("big_g", [128, T], F32); big_h = sb("big_h", [128, T], F32)
    S.op('pool', lambda e: e.iota(big_i[:], pattern=[[1, T]], base=0, channel_multiplier=0), w=['big_i'])
    ts(big_i[:], big_i[:], 15, None, ALU.bitwise_and, None, ['big_i'], ['big_i'])
    ts(m01[:], big_i[:], 0.0, None, ALU.is_gt, None, ['big_i'], ['m01'])
    freq = sb("freq", [128, 1], F32); jI = sb("jI", [128, 1], I32)
    ts(jI[:], pI[:], 15, None, ALU.bitwise_and, None, ['pI'], ['jI'])
    cp(freq[:], jI[:], ['jI'], ['freq'])
    act(freq[:], freq[:], AF.Exp, ['freq'], ['freq'], scale=-math.log(10000.0) / 16.0)
    for q in range(4):
        pat = [[1, 32], [0, 64]] if q % 2 == 0 else [[0, 32], [1, 64]]
        S.op('pool', lambda e, q=q, pat=pat: e.iota(big_i[32 * q:32 * q + 32, 0:L], pattern=pat, base=0, channel_multiplier=0),
             r=['m01'], w=['big_i'])
    ts(big_f[:, 0:L], big_i[:, 0:L], freq[:, 0:1], None, ALU.mult, None, ['big_i', 'freq'], ['big_f'])
    TWO_PI = 2.0 * math.pi

    def sin_of(dst, shift, dkey):
        ts(big_g[:, 0:L], big_f[:, 0:L], shift, None, ALU.add, None, ['big_f'], ['big_g'])
        ki = big_i[:, 0:L]
        ts(ki, big_g[:, 0:L], 1.0 / TWO_PI, None, ALU.mult, None, ['big_g'], ['big_i'])
        cp(big_h[:, 0:L], ki, ['big_i'], ['big_h'])
        stt(big_g[:, 0:L], big_h[:, 0:L], -TWO_PI, big_g[:, 0:L], ALU.mult, ALU.add, ['big_h', 'big_g'], ['big_g'])
        ts(big_h[:, 0:L], big_g[:, 0:L], math.pi, -TWO_PI, ALU.is_gt, ALU.mult, ['big_g'], ['big_h'])
        tt(big_g[:, 0:L], big_g[:, 0:L], big_h[:, 0:L], ALU.add, ['big_g', 'big_h'], ['big_g'])
        ts(big_h[:, 0:L], big_g[:, 0:L], -math.pi, TWO_PI, ALU.is_lt, ALU.mult, ['big_g'], ['big_h'])
        tt(big_g[:, 0:L], big_g[:, 0:L], big_h[:, 0:L], ALU.add, ['big_g', 'big_h'], ['big_g'])
        ts(big_g[:, 0:L], big_g[:, 0:L], -3.141592, 3.141592, ALU.max, ALU.min, ['big_g'], ['big_g'])
        act(dst, big_g[:, 0:L], AF.Sin, ['big_g'], [dkey])

    sin_of(sinT[:], 0.0, 'sinT')
    sin_of(cosT[:], math.pi / 2.0, 'cosT')
    for ch in range(2):
        for half in range(2):
            w = [2, 4, 8, 16][2 * ch + half]
            rows = slice(64 * half, 64 * half + 64)
            memset(rcE[rows, ch, :], 1.0 / w, ['rcE'], eng='pool')
            for t in range(w // 2):
                memset(rcE[rows, ch, t:t + 1], 1.0 / (t + w // 2), ['rcE'], eng='pool')
            for i in range(8):
                if (8 - i) < w // 2:
                    memset(rcE[rows, ch, 8 + i:9 + i], 1.0 / ((8 - i) + w // 2), ['rcE'], eng='pool')
    memset(pwbd[:], 0.0, ['pwbd'])
    memset(big_h[:], 0.0, ['big_h'])
    if True:
        for ch in range(2):
            S.dma('sp', U[ch, :, 0:T], big_h[:, 0:T], r=['big_h'], w=[('U', ch)])
            S.dma('sp', U[ch, :, T:UW], big_h[:, 0:UW - T], r=['big_h'], w=[('U', ch)])

    stg = sb("stg", [128, 128], F32)
    for half in range(2):
        memset(stg[:], 0.0, ['stg'])
        S.dma('sp', stg[0:96, :], b_ada.rearrange("l (j p) -> (l j) p", p=128)[96 * half:96 * half + 96, :], w=['stg'])
        tr(psb[0][:, 0:128], stg[:], ident_f[:], ['stg', 'ident_f'], ['ps0'])
        cp(bada[:, 96 * half:96 * half + 96], psb[0][:, 0:96], ['ps0'], ['bada'])
    memset(stg[:], 0.0, ['stg'])
    S.dma('sp', stg[0:32, :], norm1_w.rearrange("l (k p) -> (l k) p", p=128), w=['stg'])
    S.dma('sp', stg[32:64, :], norm2_w.rearrange("l (k p) -> (l k) p", p=128), w=['stg'])
    S.dma('sp', stg[64:72, :], pool_scale.rearrange("l (k p) -> (l k) p", p=128), w=['stg'])
    S.dma('sp', stg[72:88, :], hg_lb.rearrange("l d (k p) -> (l d k) p", p=128), w=['stg'])
    for (r0, src) in ((88, q_norm_w), (92, k_norm_w), (96, hg_norm_w)):
        S.dma('sp', stg[r0:r0 + 4, 0:64], src[:, :], w=['stg'])
        S.dma('sp', stg[r0:r0 + 4, 64:128], src[:, :], w=['stg'])
    S.dma('sp', stg[100:108, :], c_d[:, :], w=['stg'])
    S.dma('sp', stg[108:116, :], cc_d[:, :], w=['stg'])
    tr(psb[0][:, 0:128], stg[:], ident_f[:], ['stg', 'ident_f'], ['ps0'])
    cp(cst[:], psb[0][:, 0:128], ['ps0'], ['cst'])
    C_N1, C_N2, C_PS, C_LB, C_QN, C_KN, C_HN, C_C = 0, 32, 64, 72, 88, 92, 96, 100
    act(cT[:], cst[:, C_C:C_C + 16], AF.Silu, ['cst'], ['cT'])
    ex = sb("ex", [128, 16], F32); ssum = sb("ssum", [128, 4], F32)
    act(ex[:], cst[:, C_LB:C_LB + 16], AF.Exp, ['cst'], ['ex'])
    exv = ex[:].rearrange("p (l m) -> p l m", l=4)
    tt(ssum[:], exv[:, 0, :], exv[:, 1, :], ALU.add, ['ex'], ['ssum'])
    tt(ssum[:], ssum[:], exv[:, 2, :], ALU.add, ['ex', 'ssum'], ['ssum'])
    tt(ssum[:], ssum[:], exv[:, 3, :], ALU.add, ['ex', 'ssum'], ['ssum'])
    S.op('dve', lambda e: e.reciprocal(out=ssum[:], in_=ssum[:]), ['ssum'], ['ssum'])
    lbv = lbT[:].rearrange("p (l m) -> p l m", l=4)
    memset(lbT[:], 0.0, ['lbT'])
    tt(lbv[:, 1, :], exv[:, 1, :], ssum[:], ALU.mult, ['ex', 'ssum'], ['lbT'])
    tt(ex[:, 8:12], ex[:, 4:8], ex[:, 8:12], ALU.add, ['ex', 'lbT'], ['ex'])
    tt(lbv[:, 2, :], exv[:, 2, :], ssum[:], ALU.mult, ['ex', 'ssum'], ['lbT'])
    tt(ex[:, 12:16], ex[:, 8:12], ex[:, 12:16], ALU.add, ['ex', 'lbT'], ['ex'])
    tt(lbv[:, 3, :], exv[:, 3, :], ssum[:], ALU.mult, ['ex', 'ssum'], ['lbT'])
    ts(omlb[:], lbT[:], -1.0, 1.0, ALU.mult, ALU.add, ['lbT'], ['omlb'])
    ts(nomlb[:], omlb[:], -1.0, None, ALU.mult, None, ['omlb'], ['nomlb'])

    def adaln_steps(l):
        pk = 7; p = l % 2
        modT = modTs[p]
        wsrc = w_ada[l].rearrange("(kc p) n -> p kc n", p=128)
        nxt = load_w(wsrc[:, :, 0:512], (8, 512), 512)
        yield
        for wt in range(12):
            Wv, wk = nxt
            if wt + 1 < 12:
                nxt = load_w(wsrc[:, :, (wt + 1) * 512:(wt + 2) * 512], (8, 512), 512)
            for jj in range(4):
                j = wt * 4 + jj
                for kc in range(KC):
                    mm(psb[pk][:, 2 * j:2 * j + 2], Wv[:, kc, jj * 128:(jj + 1) * 128],
                       cT[:].rearrange("p (w k) -> p k w", w=2)[:, kc, :], kc == 0, kc == KC - 1, [wk, 'cT'], [f"ps{pk}"])
            yield
        tt(modT[:], psb[pk][:, 0:96].rearrange("p (j w) -> p j w", w=2),
           bada[:, l * 48:(l + 1) * 48].unsqueeze(2).broadcast_to([128, 48, 2]), ALU.add, [f"ps{pk}", 'bada'], [f"modT{p}"])
        for (dst, dkey, scc, ncol) in ((a1s[p], f"a1_{p}", 8, C_N1), (a2s[p], f"a2_{p}", 32, C_N2)):
            for wch in range(2):
                stt(dst[:, :, wch], modT[:, scc:scc + 8, wch], 1.0, cst[:, ncol + l * 8:ncol + l * 8 + 8], ALU.add, ALU.mult,
                    [f"modT{p}", 'cst'], [dkey])
        yield

    def modcol(idx, kc, wch):
        return modTs[CUR['l'] % 2][:, idx * 8 + kc, wch:wch + 1]

    def modkey():
        return f"modT{CUR['l'] % 2}"

    def xload(i):
        src = x_d[i * 128:(i + 1) * 128, :] if i < 16 else ctx_d[(i - 16) * 128:(i - 15) * 128, :]
        S.dma('sp', xs[i % 2][:], src, w=[f"xs{i % 2}"])
    ada0 = adaln_steps(0)
    xload(0)
    for i in range(NT):
        b = i % 2
        next(ada0, None)
        for hh in range(2):
            pk = 2 * b + hh
            for q in range(4):
                kc = hh * 4 + q
                tr(psb[pk][:, q * 128:(q + 1) * 128], xs[b][:, kc * 128:(kc + 1) * 128], ident_f[:], [f"xs{b}", 'ident_f'], [f"ps{pk}"])
            cp(xs2[b][:, hh * 4:hh * 4 + 4, :], psb[pk][:].rearrange("p (q t) -> p q t", q=4), [f"ps{pk}"], [f"xt2_{b}"],
               eng='dve' if hh == 0 else 'dve')
        if i + 1 < NT: xload(i + 1)
        S.dma('sp', XT.rearrange("c p t -> p c t")[:, :, i * 128:(i + 1) * 128], xs2[b][:], r=[f"xt2_{b}"], w=[('XT', ('ld', i))])
    for _ in ada0: pass
    S.barrier()
    sc0.close(); sb = sb0

    XTv = XT.rearrange("c p t -> p c t")
    PSA = Rot([0, 1, 2, 3])

    def norm_phase(a_t, akey, sh_idx):
        sc = Scope(nc); sb = sc.sb
        xblk = [sb(f"xblk{i}", [128, KC, 512], F32) for i in range(2)]
        sqb = [sb(f"sqb{i}", [128, KC, 512], BF16) for i in range(2)]
        rstd_b = [sb(f"rstd{i}", [128, 512], F32) for i in range(2)]
        lnv_b = [sb(f"lnv{i}", [128, 512], F32) for i in range(2)]
        ntmp = [sb(f"ntmp{i}", [128, 512], F32) for i in range(8)]

        def stats(bi):
            t0, n = BLOCKS[bi]; b = bi % 2
            S.dma('sp', xblk[b][:, :, 0:n], XTv[:, :, t0:t0 + n], r=['XT'], w=[f"xblk{b}"])
            act(sqb[b][:, :, 0:n], xblk[b][:, :, 0:n], AF.Square, [f"xblk{b}"], [f"sqb{b}"])
            pk = PSA.get()
            for kc in range(KC):
                mm(psb[pk][:, 0:n], ones_b[:], sqb[b][:, kc, 0:n], kc == 0, kc == KC - 1, ['ones_b', f"sqb{b}"], [f"ps{pk}"])
            act(lnv_b[b][:, 0:n], psb[pk][:, 0:n], AF.Ln, [f"ps{pk}"], [f"lnv{b}"], scale=1.0 / D, bias=EPS)
            act(rstd_b[b][:, 0:n], lnv_b[b][:, 0:n], AF.Exp, [f"lnv{b}"], [f"rstd{b}"], scale=-0.5)

        def apply(bi):
            t0, n = BLOCKS[bi]; b = bi % 2
            wch = 0 if t0 < L else 1
            for kc in range(KC):
                stt(ntmp[kc][:, 0:n], xblk[b][:, kc, 0:n], a_t[:, kc, wch:wch + 1], rstd_b[b][:, 0:n], ALU.mult, ALU.mult,
                    [f"xblk{b}", f"rstd{b}", akey], [f"ntmp{kc}"])
            for kc in range(KC):
                if kc % 2 == 0:
                    act(hT[:, kc, t0:t0 + n], ntmp[kc][:, 0:n], AF.Identity, [f"ntmp{kc}", modkey()], [('hT', (kc, bi))],
                        bias=modcol(sh_idx, kc, wch), scale=1.0)
                else:
                    ts(hT[:, kc, t0:t0 + n], ntmp[kc][:, 0:n], modcol(sh_idx, kc, wch), None, ALU.add, None,
                       [f"ntmp{kc}", modkey()], [('hT', (kc, bi))])
        stats(0)
        for bi in range(len(BLOCKS)):
            if bi + 1 < len(BLOCKS): stats(bi + 1)
            apply(bi)
        S.barrier(); sc.close()

    def tap(name, src_ap, shape, dtp, r):
        if name not in taps: return
        t_ = dram("tap_" + name, shape, dtp, "ExternalOutput")
        tap_d[name] = t_
        S.dma('sp', t_, src_ap, r=r)

    qraw = [sb(f"qraw{i}", [128, 512], F32) for i in range(2)]
    qsq = [sb(f"qsq{i}", [128, 512], BF16) for i in range(2)]
    qn_b = [sb(f"qn{i}", [128, 512], BF16) for i in range(2)]
    qt1 = [sb(f"qt1_{i}", [128, 512], F32) for i in range(2)]
    qt2 = [sb(f"qt2_{i}", [128, 512], F32) for i in range(2)]
    QR = Rot([0, 1])
    A = {}

    def open_attn_scope():
        sc = Scope(nc); sb = sc.sb
        A['qT'] = sb("qT", [128, 4, T], BF16)
        A['kTz'] = sb("kTz", [128, 2, 2, T], BF16)
        A['aT'] = sb("aT", [128, 4, T], BF16)
        A['VE'] = sb("VE", [128, NT, 2, 192], BF16)
        A['pT'] = [sb(f"pT{i}", [128, 512], BF16) for i in range(4)]
        A['rden'] = sb("rden", [128, 512], F32); A['rbc'] = sb("rbc", [128, 512], F32)
        A['tmst'] = [sb(f"tmst{i}", [128, 4, 128], BF16) for i in range(2)]
        memset(A['VE'][:], 1.0, ['VE'])
        memset(A['kTz'][0:64, :, 1, :], 0.0, ['kTz'])
        return sc
    PT = Rot([0, 1, 2, 3]); TMST = Rot([0, 1])

    def qk_epilogue(pk, n, bi, t0, dest, dkey, wcol):
        latent = t0 < L
        i = QR.get()
        act(qn_b[i][:, 0:n], psb[pk][:, 0:n], AF.Copy, [f"ps{pk}", 'cst'], [f"qn{i}"], scale=wcol)
        act(qsq[i][:, 0:n], psb[pk][:, 0:n], AF.Square, [f"ps{pk}"], [f"qsq{i}"])

        def part2():
            p2 = PSQ.get()
            mm(psb[p2][:, 0:n], blk_b[:], qsq[i][:, 0:n], True, True, ['blk_b', f"qsq{i}"], [f"ps{p2}"])
            if latent:
                p3 = PSQ.get()
                mm(psb[p3][:, 0:n], Rt[:], qn_b[i][:, 0:n], True, True, ['Rt', f"qn{i}"], [f"ps{p3}"])
            act(qt1[i][:, 0:n], psb[p2][:, 0:n], AF.Ln, [f"ps{p2}"], [f"qt1_{i}"], scale=1.0 / 64, bias=EPS)
            act(qt1[i][:, 0:n], qt1[i][:, 0:n], AF.Exp, [f"qt1_{i}"], [f"qt1_{i}"], scale=-0.5)
            if latent:
                tt(qraw[i][:, 0:n], qn_b[i][:, 0:n], cosT[:, t0:t0 + n], ALU.mult, [f"qn{i}", 'cosT'], [f"qraw{i}"])
                tt(qt2[i][:, 0:n], psb[p3][:, 0:n], sinT[:, t0:t0 + n], ALU.mult, [f"ps{p3}", 'sinT'], [f"qt2_{i}"])
                tt(qraw[i][:, 0:n], qraw[i][:, 0:n], qt2[i][:, 0:n], ALU.add, [f"qraw{i}", f"qt2_{i}"], [f"qraw{i}"])
                tt(dest, qraw[i][:, 0:n], qt1[i][:, 0:n], ALU.mult, [f"qraw{i}", f"qt1_{i}"], [dkey])
            else:
                tt(dest, qn_b[i][:, 0:n], qt1[i][:, 0:n], ALU.mult, [f"qn{i}", f"qt1_{i}"], [dkey])
        return part2

    PSQ = Rot([4, 5, 6])

    def evac_dram(pk, n, dst_ap, dkey, func, dtp):
        if dtp == F32:
            i = STF.get(); buf = stf[i]; bk = f"stf{i}"
        else:
            i = STB.get(); buf = stb[i]; bk = f"stb{i}"
        if func is None:
            cp(buf[:, 0:n], psb[pk][:, 0:n], [f"ps{pk}"], [bk])
        else:
            act(buf[:, 0:n], psb[pk][:, 0:n], func, [f"ps{pk}"], [bk])
        S.dma('sp', dst_ap, buf[:, 0:n], r=[bk], w=[dkey])

    DEF = {'fn': None}

    def flush_deferred():
        if DEF['fn'] is not None:
            DEF['fn'](); DEF['fn'] = None

    def fm_chunk(Wv, wk, jj, epi):
        for bi, (t0, n) in enumerate(BLOCKS):
            pk = PSA.get()
            for kc in range(KC):
                mm(psb[pk][:, 0:n], Wv[:, kc, jj * 128:(jj + 1) * 128], hT[:, kc, t0:t0 + n], kc == 0, kc == KC - 1,
                   [wk, ('hT', (kc, bi))], [f"ps{pk}"])
            flush_deferred()
            r_ = epi(pk, n, bi, t0)
            DEF['fn'] = r_ if callable(r_) else None

    def tm_chunk(Wv, wk, jj, epi):
        for g4 in range(0, NT, 4):
            pk = PSA.get()
            nt4 = min(4, NT - g4)
            for q in range(nt4):
                i = g4 + q
                bi = min(i // 4, 4)
                for kc in range(KC):
                    mm(psb[pk][:, q * 128:(q + 1) * 128], hT[:, kc, i * 128:(i + 1) * 128], Wv[:, kc, jj * 128:(jj + 1) * 128],
                       kc == 0, kc == KC - 1, [wk, ('hT', (kc, bi))], [f"ps{pk}"])
            epi(pk, g4, nt4)

    def in_proj(l):
        wsrc = w_in[l].rearrange("(kc p) n -> p kc n", p=128)
        qcol = cst[:, C_QN + l:C_QN + l + 1]; kcol = cst[:, C_KN + l:C_KN + l + 1]

        def epi_for(j):
            if j in (0, 1):
                return lambda pk, n, bi, t0: evac_dram(pk, n, U[j, :, ucol(t0):ucol(t0) + n], ('U', (j, bi)), None, F32)
            if 2 <= j <= 5:
                return lambda pk, n, bi, t0: qk_epilogue(pk, n, bi, t0, A['qT'][:, j - 2, t0:t0 + n], ('qT', (j - 2, bi)), qcol)
            if j in (8, 9):
                return lambda pk, n, bi, t0: evac_dram(pk, n, HQ[j - 8, :, t0:t0 + n], ('HQ', (j - 8, bi)), AF.Copy, F32)
            if 12 <= j <= 15:
                return lambda pk, n, bi, t0: evac_dram(pk, n, Z[j - 12, :, t0:t0 + n], ('Z', (j - 12, bi)), None, F32)
            if j in (16, 17):
                return lambda pk, n, bi, t0: evac_dram(pk, n, HGATE[j - 16, :, t0:t0 + n], ('HGATE', (j - 16, bi)), AF.Silu, BF16)
            if j >= 18:
                return lambda pk, n, bi, t0: evac_dram(pk, n, GT[j - 18, :, t0:t0 + n], ('GT', (j - 18, bi)), AF.Sigmoid, BF16)
            return None

        def epi_v(pk, g4, nt4):
            for g in range(2):
                cp(A['VE'][:, g4:g4 + nt4, g, 64:128], psb[pk][:, 0:nt4 * 128].rearrange("p (q c) -> p q c", q=nt4)[:, :, g * 64:(g + 1) * 64],
                   [f"ps{pk}"], ['VE'])

        def epi_hi(hc):
            def f(pk, g4, nt4):
                i = TMST.get()
                act(A['tmst'][i][:, 0:nt4, :], psb[pk][:, 0:nt4 * 128].rearrange("p (q c) -> p q c", q=nt4), AF.Copy, [f"ps{pk}"], [f"tmst{i}"])
                S.dma('sp', VH.rearrange("(i p) c -> p i c", p=128)[:, g4:g4 + nt4, hc * 128:(hc + 1) * 128], A['tmst'][i][:, 0:nt4, :],
                      r=[f"tmst{i}"], w=[('VH', (hc, g4))])
            return f

        i = Wrot.get()
        Wk = Wb[i][:, 0:8 * 256].rearrange("p (a b) -> p a b", a=8)
        for g in range(2):
            for dup in range(2):
                S.dma('pool', Wk[:, :, g * 128 + dup * 64:g * 128 + dup * 64 + 64], wsrc[:, :, 768 + g * 64:768 + g * 64 + 64], w=[f"W{i}"])
        for g in range(2):
            def kepi(pk, n, bi, t0, g=g):
                p2 = qk_epilogue(pk, n, bi, t0, A['kTz'][:, g, 0, t0:t0 + n], ('kTz', (g, bi)), kcol)

                def fin():
                    p2()
                    cp(A['kTz'][64:128, g, 1, t0:t0 + n], A['kTz'][64:128, g, 0, t0:t0 + n], [('kTz', (g, bi))], [('kTz', (g, bi))])
                    memset(A['kTz'][64:128, g, 0, t0:t0 + n], 0.0, [('kTz', (g, bi))])
                return fin
            fm_chunk(Wk, f"W{i}", g, kepi)
        ntile = (IN_W + 511) // 512
        for wt in range(ntile):
            c0 = wt * 512; ncl = min(512, IN_W - c0)
            Wv, wk = load_w(wsrc[:, :, c0:c0 + ncl], (8, ncl), ncl)
            for jj in range(ncl // 128):
                j = wt * 4 + jj
                if j == 6: continue
                if j == 7: tm_chunk(Wv, wk, jj, epi_v)
                elif j in (10, 11): tm_chunk(Wv, wk, jj, epi_hi(j - 10))
                else: fm_chunk(Wv, wk, jj, epi_for(j))
        flush_deferred()

    def attention(l, ada_steps=None):
        LA = 2
        for bi, (t0, n) in enumerate(BLOCKS):
            latent = t0 < L
            if not latent and l == depth - 1 and depth == DEPTH:
                continue
            ktiles = list(range(NT)) if latent else [16, 17]
            items = [(h, ii, i) for h in range(8) for ii, i in enumerate(ktiles)]
            sc_bank = {}

            def emit_score(idx):
                h, ii, i = items[idx]
                g = h // 4; ch = h // 2; pb = (h % 2) * 64
                pk = PSA.get()
                mm(psb[pk][:, 0:n], A['kTz'][:, g, h % 2, i * 128:(i + 1) * 128], A['qT'][:, ch, t0:t0 + n], True, True,
                   [('kTz', (g, min(i // 4, 4))), ('qT', (ch, bi))], [f"ps{pk}"])
                sc_bank[idx] = pk

            def epi2(h):
                ch = h // 2; pb = (h % 2) * 64; po = 4 + (h % 2); r0 = 64 if h % 2 == 0 else 0
                mm(psb[6][:, 0:n], onesf[r0:r0 + 1, :], A['rden'][r0:r0 + 1, 0:n], True, True, ['onesf', ('rden', h % 2)], ['ps6'])
                cp(A['rbc'][pb:pb + 64, 0:n], psb[6][pb:pb + 64, 0:n], ['ps6'], [('rbc', h % 2)])
                tt(A['aT'][pb:pb + 64, ch, t0:t0 + n], psb[po][pb:pb + 64, 0:n], A['rbc'][pb:pb + 64, 0:n], ALU.mult,
                   [f"ps{po}", ('rbc', h % 2)], [('aT', (ch, bi, h % 2))])

            pending = []
            for idx in range(min(LA, len(items))): emit_score(idx)
            for idx, (h, ii, i) in enumerate(items):
                g = h // 4; po = 4 + (h % 2)
                voff = 64 if h % 2 == 0 else 0
                pk = sc_bank.pop(idx)
                pi = PT.get()
                act(A['pT'][pi][:, 0:n], psb[pk][:, 0:n], AF.Exp, [f"ps{pk}"], [f"pT{pi}"], scale=0.125)
                if idx + LA < len(items): emit_score(idx + LA)
                mm(psb[po][:, 0:n], A['VE'][:, i, g, voff:voff + 128], A['pT'][pi][:, 0:n], ii == 0, ii == len(ktiles) - 1,
                   ['VE', f"pT{pi}"], [f"ps{po}"])
                if ii == len(ktiles) - 1:
                    r0 = 64 if h % 2 == 0 else 0
                    S.op('dve', lambda e, r0=r0, po=po: e.reciprocal(out=A['rden'][r0:r0 + 1, 0:n], in_=psb[po][r0:r0 + 1, 0:n]),
                         [f"ps{po}"], [('rden', h % 2)])
                    pending.append((idx + min(8, 2 * len(ktiles) - 2), h))
                while pending and pending[0][0] <= idx:
                    epi2(pending.pop(0)[1])
                if ii == 0 and ada_steps is not None:
                    next(ada_steps, None)
            while pending:
                epi2(pending.pop(0)[1])
        for ch in range(4):
            S.dma('sp', BR[2 + ch, :, :], A['aT'][:, ch, :], r=['aT'], w=[('BR', 2 + ch)])

    def pool_phase(l):
        N = UW
        sc = Scope(nc); sb = sc.sb
        upad = sb("upad", [128, 2, UW], F32)
        s2 = sb("pl_s2", [128, UW], F32); s4 = sb("pl_s4", [128, UW], F32); s8 = sb("pl_s8", [128, UW], F32)
        ypool = sb("ypool", [128, 2, T], BF16)
        yedge = sb("yedge", [128, 16], F32)
        for ch in range(2):
            S.dma('sp', upad[:, ch, :], U[ch, :, :], r=['U'], w=[('upad', ch)])
            u = upad[:, ch, :]
            tt(s2[:, 1:N], u[:, 0:N - 1], u[:, 1:N], ALU.add, [('upad', ch)], ['pl_s2'])
            tt(s4[:, 2:N - 1], s2[:, 1:N - 2], s2[:, 3:N], ALU.add, ['pl_s2'], ['pl_s4'])
            if ch == 0:
                lv = ((s2, 'pl_s2'), (s4, 'pl_s4'))
            else:
                tt(s8[:, 4:N - 3], s4[:, 2:N - 5], s4[:, 6:N - 1], ALU.add, ['pl_s4'], ['pl_s8'])
                tt(s2[:, 8:N - 7], s8[:, 4:N - 11], s8[:, 12:N - 3], ALU.add, ['pl_s8', 'pl_s2'], ['pl_s2'])
                lv = ((s8, 'pl_s8'), (s2, 'pl_s2'))
            for half in range(2):
                w = [2, 4, 8, 16][2 * ch + half]
                rows = slice(64 * half, 64 * half + 64)
                src, skey = lv[half]
                for (ts0, tn, uc) in ((0, L, 8), (L, CT, L + 24)):
                    stt(ypool[rows, ch, ts0:ts0 + tn], src[rows, uc:uc + tn], 1.0 / w, u[rows, uc:uc + tn], ALU.mult, ALU.subtract,
                        [skey, ('upad', ch)], [('ypool', ch)])
                    for (e0, tb) in ((0, 0), (tn - 8, 8)):
                        tt(yedge[rows, 0:8], src[rows, uc + e0:uc + e0 + 8], rcE[rows, ch, tb:tb + 8], ALU.mult, [skey, 'rcE'], ['yedge'])
                        tt(ypool[rows, ch, ts0 + e0:ts0 + e0 + 8], yedge[rows, 0:8], u[rows, uc + e0:uc + e0 + 8], ALU.subtract,
                           ['yedge', ('upad', ch)], [('ypool', ch)])
        for g in range(4):
            ch, half = g // 2, g % 2
            S.dma('pool', pwbd[64 * half:64 * half + 64, ch, 64 * half:64 * half + 64], pool_w[l, g, :, :], w=['pwbd'])
        for ch in range(2):
            for bi, (t0, n) in enumerate(BLOCKS):
                pk = PSA.get()
                mm(psb[pk][:, 0:n], pwbd[:, ch, :], ypool[:, ch, t0:t0 + n], True, True, ['pwbd', ('ypool', ch)], [f"ps{pk}"])
                i = STB.get()
                ts(stb[i][:, 0:n], psb[pk][:, 0:n], cst[:, C_PS + l * 2 + ch:C_PS + l * 2 + ch + 1], None, ALU.mult, None, [f"ps{pk}", 'cst'], [f"stb{i}"])
                S.dma('sp', BR[ch, :, t0:t0 + n], stb[i][:, 0:n], r=[f"stb{i}"], w=[('BR', (ch, bi))])
        S.barrier(); sc.close()

    NCH = T // 16
    VBLK = Rot([0, 1]); ATM = Rot([0, 1, 2])
    Sbf = [hT[:, 4 * d:4 * d + 4, :].rearrange("p a b -> p (a b)").rearrange("p (v n) -> p v n", v=64) for d in range(2)]

    def nat(i, n, d=0):
        if d == 0:
            return (i - 16) * 8 + n if i >= 16 else 16 + i * 8 + n
        return 128 + (i - 16) * 8 + n if i >= 16 else i * 8 + n

    def hgrn_phase(l):
        sc = Scope(nc); sb = sc.sb
        qdec = [sb(f"qdec{d}", [128, T], BF16) for d in range(2)]
        ktil = [sb(f"ktil{d}", [128, T], BF16) for d in range(2)]
        kdecTM = [sb(f"kdecTM{d}", [128, NT, 128], BF16) for d in range(2)]
        abuf = [sb(f"abuf{d}", [128, NCH], F32) for d in range(2)]
        Vh = sb("Vh", [128, NT, 256], BF16)
        S.dma('sp', Vh[:], VH.rearrange("(i p) c -> p i c", p=128), r=['VH'], w=['Vh'])
        for hp in range(2):
            sc1 = Scope(nc); sb = sc1.sb
            hz = sb("hz", [128, T], F32); hlogf = sb("hlogf", [128, T], F32)
            hG = sb("hG", [128, T], F32); hD = sb("hD", [128, T], F32)
            hE = hlogf; hq_sb = sb("hq_sb", [128, T], F32)
            kdecT = sb("kdecT", [128, T], BF16)
            hsg = hz
            S.dma('sp', hq_sb[:], HQ[hp, :, :], r=['HQ'], w=['hq_sb'])
            HALF = T // 2
            for d in range(2):
                lcol = l * 4 + d * 2 + hp
                ocol = omlb[:, lcol:lcol + 1]
                H = [(hf, slice(hf * HALF, (hf + 1) * HALF)) for hf in range(2)]
                for hf, sl in H:
                    S.dma('sp', hz[:, sl], Z[d * 2 + hp, :, sl], r=['Z'], w=[('hz', hf)])
                for hf, sl in H:
                    act(hsg[:, sl], hz[:, sl], AF.Sigmoid, [('hz', hf)], [('hz', hf)])
                for hf, sl in H:
                    ts(hlogf[:, sl], hsg[:, sl], ocol, lbT[:, lcol:lcol + 1], ALU.mult, ALU.add, [('hz', hf), 'omlb', 'lbT'], [('hlogf', hf)])
                for hf, sl in H:
                    act(hlogf[:, sl], hlogf[:, sl], AF.Ln, [('hlogf', hf)], [('hlogf', hf)])
                for hf, sl in H:
                    ts(hz[:, sl], hsg[:, sl], -1.0, 1.0, ALU.mult, ALU.add, [('hz', hf)], [('hz', hf)])
                for hf, sl in H:
                    S.op('dve', lambda e: e.tensor_tensor_scan(out=hG[:, sl], data0=m01[:, sl], data1=hlogf[:, sl], initial=0.0,
                                                              op0=ALU.mult, op1=ALU.add), ['m01', ('hlogf', hf)], [('hG', hf)])
                loff = 16 if d == 0 else 0; coff = 0 if d == 0 else 128
                act(abuf[d][:, loff:loff + 72], hG[:, 15:HALF:16], AF.Exp, [('hG', 0)], [f"abuf{d}"])
                act(abuf[d][:, loff + 72:loff + 128], hG[:, HALF + 15:L:16], AF.Exp, [('hG', 1)], [f"abuf{d}"])
                act(abuf[d][:, coff:coff + 16], hG[:, L + 15:T:16], AF.Exp, [('hG', 1)], [f"abuf{d}"])
                for hf, sl in H:
                    G3 = hG[:, sl].rearrange("p (c s) -> p c s", s=16)
                    tt(hD[:, sl].rearrange("p (c s) -> p c s", s=16), G3[:, :, 15:16].broadcast_to([128, HALF // 16, 16]), G3, ALU.subtract,
                       [('hG', hf)], [('hD', hf)], eng='pool')
                if d == 0:
                    Gd, GLd, gk, glk = hG, hD, 'hG', 'hD'
                else:
                    for hf, sl in H:
                        tt(hD[:, sl], hD[:, sl], hlogf[:, sl], ALU.add, [('hD', hf), ('hlogf', hf)], [('hD', hf)])
                    for hf, sl in H:
                        tt(hG[:, sl], hG[:, sl], hlogf[:, sl], ALU.subtract, [('hG', hf), ('hlogf', hf)], [('hG', hf)])
                    Gd, GLd, gk, glk = hD, hG, 'hD', 'hG'
                for hf, sl in H:
                    act(hE[:, sl], Gd[:, sl], AF.Exp, [(gk, hf)], [('hlogf', hf)])
                for hf, sl in H:
                    tt(qdec[d][:, sl], hq_sb[:, sl], hE[:, sl], ALU.mult, ['hq_sb', ('hlogf', hf)], [(f"qdec{d}", hf)], eng='pool')
                for hf, sl in H:
                    act(hE[:, sl], Gd[:, sl], AF.Exp, [(gk, hf)], [('hlogf', hf)], scale=-1.0)
                for hf, sl in H:
                    stt(ktil[d][:, sl], hz[:, sl], ocol, hE[:, sl], ALU.mult, ALU.mult, [('hz', hf), ('hlogf', hf), 'omlb'], [(f"ktil{d}", hf)])
                for hf, sl in H:
                    act(hE[:, sl], GLd[:, sl], AF.Exp, [(glk, hf)], [('hlogf', hf)])
                for hf, sl in H:
                    stt(kdecT[:, sl], hz[:, sl], ocol, hE[:, sl], ALU.mult, ALU.mult, [('hz', hf), ('hlogf', hf), 'omlb'], [('kdecT', hf)])
                for g3 in range(6):
                    pk = PSA.get()
                    pv = psb[pk][:].bitcast(BF16)
                    for q in range(3):
                        i = g3 * 3 + q
                        tr(pv[:, q * 128:(q + 1) * 128], kdecT[:, i * 128:(i + 1) * 128], ident_b[:], [('kdecT', g3 // 3), 'ident_b'], [f"ps{pk}"])
                    act(kdecTM[d][:, g3 * 3:g3 * 3 + 3, :], pv[:, 0:3 * 128].rearrange("p (q c) -> p q c", q=3), AF.Copy, [f"ps{pk}"], [f"kdecTM{d}"])
            S.barrier(); sc1.close()
            sc2 = Scope(nc); sb = sc2.sb
            hgate_sb = sb("hgate_sb", [128, T], BF16)
            osum = sb("osum", [128, T], F32)
            Vblk = [sb(f"Vblk{i}", [128, 2, 8, 64], BF16) for i in range(2)]
            ATm = [sb(f"ATm{i}", [128, 128], BF16) for i in range(3)]
            kvbufs = [sb(f"kvbuf{d}", [128, 64, NCH], BF16) for d in range(2)]
            VR = 8
            a_rep = sb("a_rep", [128, VR, NCH], F32)
            S.dma('sp', hgate_sb[:], HGATE[hp, :, :], r=['HGATE'], w=['hgate_sb'])
            PSH = Rot([0, 1, 2])
            ACC = [3, 4, 5, 6, 7]
            started = set()

            def acc(i, h2):
                bank = ACC[i // 4]; col = (i % 4) * 128
                first = (bank, h2) not in started
                started.add((bank, h2))
                return bank, col, first
            for i in range(NT):
                vi = VBLK.get()
                tt(Vblk[vi][:], Vh[:, i, hp * 128:(hp + 1) * 128].rearrange("p (h v) -> p h v", h=2).unsqueeze(2).broadcast_to([128, 2, 8, 64]),
                   Emask[:].unsqueeze(1).unsqueeze(3).broadcast_to([128, 2, 8, 64]), ALU.mult, ['Vh', 'Emask'], [f"Vblk{vi}"])
                for d in range(2):
                    j0 = nat(i, 0, d)
                    pk = PSH.get()
                    for h2 in range(2):
                        mm(psb[pk][h2 * 64:(h2 + 1) * 64, :], kdecTM[d][:, i, h2 * 64:(h2 + 1) * 64],
                           Vblk[vi][:, h2, :, :].rearrange("p n v -> p (n v)"), True, True, [f"kdecTM{d}", f"Vblk{vi}"], [f"ps{pk}"])
                    act(kvbufs[d][:, :, j0:j0 + 8], psb[pk][:].rearrange("p (n v) -> p v n", n=8), AF.Copy, [f"ps{pk}"], [f"kvbuf{d}"])
            items = [(i, h2, d) for i in range(NT) for h2 in range(2) for d in range(2)]
            abank = {}

            def emit_A(idx):
                i, h2, d = items[idx]
                rows = slice(h2 * 64, h2 * 64 + 64)
                pk = PSH.get()
                mm(psb[pk][:, 0:128], ktil[d][rows, i * 128:(i + 1) * 128], qdec[d][rows, i * 128:(i + 1) * 128], True, True,
                   [f"ktil{d}", f"qdec{d}"], [f"ps{pk}"])
                abank[idx] = pk
            def scan_setup(d):
                cp(a_rep[:], abuf[d][:].unsqueeze(1).broadcast_to([128, VR, NCH]), [f"abuf{d}"], ['a_rep'])
                rc = 0 if d == 0 else NCH - 1
                memset(a_rep[:, :, rc:rc + 1], 0.0, ['a_rep'])

            def scan_instr(d, g4):
                af = a_rep[:].rearrange("p v n -> p (v n)")
                kf = kvbufs[d][:, VR * g4:VR * g4 + VR, :].rearrange("p v n -> p (v n)")
                of = Sbf[d][:, VR * g4:VR * g4 + VR, :].rearrange("p v n -> p (v n)")
                if d == 0:
                    S.op('dve', lambda e: e.tensor_tensor_scan(out=of, data0=af, data1=kf, initial=0.0, op0=ALU.mult, op1=ALU.add),
                         [f"kvbuf{d}", 'a_rep'], [f"Sbf{d}"])
                else:
                    NF = VR * NCH
                    S.op('dve', lambda e: e.tensor_tensor_scan(out=of[:, NF - 1::-1], data0=af[:, NF - 1::-1], data1=kf[:, NF - 1::-1],
                                                              initial=0.0, op0=ALU.mult, op1=ALU.add), [f"kvbuf{d}", 'a_rep'], [f"Sbf{d}"])
            scan_setup(0)
            g_next = 0
            for idx in range(2): emit_A(idx)
            for idx, (i, h2, d) in enumerate(items):
                rows = slice(h2 * 64, h2 * 64 + 64)
                pk = abank.pop(idx)
                ai = ATM.get()
                tt(ATm[ai][:], psb[pk][:, 0:128], (maskF if d == 0 else maskB)[:], ALU.mult, [f"ps{pk}", 'maskF', 'maskB'], [f"ATm{ai}"])
                if idx + 2 < len(items): emit_A(idx + 2)
                bank, col, first = acc(i, h2)
                mm(psb[bank][rows, col:col + 128], Vh[:, i, hp * 128 + h2 * 64:hp * 128 + h2 * 64 + 64], ATm[ai][:], first, False,
                   ['Vh', f"ATm{ai}"], [(f"ps{bank}", h2)])
                if idx % 9 == 8 and g_next < 64 // VR:
                    scan_instr(0, g_next); g_next += 1
            while g_next < 64 // VR:
                scan_instr(0, g_next); g_next += 1
            for d in range(2):
                if d == 1:
                    scan_setup(1)
                    for g4 in range(64 // VR): scan_instr(1, g4)
                for i in range(NT):
                    for h2 in range(2):
                        rows = slice(h2 * 64, h2 * 64 + 64)
                        bank, col, first = acc(i, h2)
                        for n in range(8):
                            m = nat(i, n, d)
                            if d == 0:
                                if m == 0: continue
                                js = m - 1
                            else:
                                if m == NCH - 1: continue
                                js = m + 1
                            last = (d == 1) and (i == NT - 1 or i % 4 == 3) and n == 7
                            mm(psb[bank][rows, col + n * 16:col + (n + 1) * 16], Sbf[d][rows, :, js],
                               qdec[d][rows, i * 128 + n * 16:i * 128 + (n + 1) * 16], False, last, [f"Sbf{d}", f"qdec{d}"], [(f"ps{bank}", h2)])
            for i in range(NT):
                bank, col, _ = acc(i, 0)
                act(osum[:, i * 128:(i + 1) * 128], psb[bank][:, col:col + 128], AF.Copy, [f"ps{bank}"], [('osum', i)])
            hcol = cst[:, C_HN + l:C_HN + l + 1]
            for bi, (t0, n) in enumerate(BLOCKS):
                i = QR.get()
                act(qsq[i][:, 0:n], osum[:, t0:t0 + n], AF.Square, ['osum'], [f"qsq{i}"])
                p2 = PSQ.get()
                mm(psb[p2][:, 0:n], blk_b[:], qsq[i][:, 0:n], True, True, ['blk_b', f"qsq{i}"], [f"ps{p2}"])
                act(qt1[i][:, 0:n], psb[p2][:, 0:n], AF.Ln, [f"ps{p2}"], [f"qt1_{i}"], scale=1.0 / 64, bias=EPS)
                act(qt1[i][:, 0:n], qt1[i][:, 0:n], AF.Exp, [f"qt1_{i}"], [f"qt1_{i}"], scale=-0.5)
                stt(qt2[i][:, 0:n], osum[:, t0:t0 + n], hcol, qt1[i][:, 0:n], ALU.mult, ALU.mult, ['osum', f"qt1_{i}", 'cst'], [f"qt2_{i}"])
                si = STB.get()
                tt(stb[si][:, 0:n], qt2[i][:, 0:n], hgate_sb[:, t0:t0 + n], ALU.mult, [f"qt2_{i}", 'hgate_sb'], [f"stb{si}"])
                S.dma('sp', BR[6 + hp, :, t0:t0 + n], stb[si][:, 0:n], r=[f"stb{si}"], w=[('BR', (6 + hp, bi))])
            if 'osum' in taps and l == 0 and hp == 0: tap('osum', osum[:], [128, T], F32, ['osum'])
            S.barrier(); sc2.close()
        S.barrier(); sc.close()

    GTB = Rot([0, 1]); MT = Rot([0, 1, 2])

    def merge_phase(l):
        S.barrier(pool=True)
        sc = Scope(nc); sb = sc.sb
        wbr = sb("wbr", [128, 8, D], BF16)
        wo_sb = sb("wo_sb", [128, 8, D], BF16)
        brb = [sb(f"brb{i}", [128, 8, 512], BF16) for i in range(2)]
        gtb = [sb(f"gtb{i}", [128, 3, 512], BF16) for i in range(2)]
        yT = [sb(f"yT{i}", [128, 8, 512], BF16) for i in range(2)]
        mt = [sb(f"mt{i}", [128, 512], F32) for i in range(3)]
        xj = [sb(f"xj{i}", [128, 512], F32) for i in range(2)]; XJ = Rot([0, 1])
        c2 = [sb(f"c2_{i}", [128, 512], F32) for i in range(2)]; C2 = Rot([0, 1])
        for cbh in range(2):
            cs = slice(cbh * 512, cbh * 512 + 512)
            S.dma('pool', wbr[:, 0:2, cs], w_bp[l].rearrange("(kc p) n -> p kc n", p=128)[:, :, cs], w=[('wbr', (0, cbh))])
            S.dma('pool', wbr[:, 2:6, cs], w_ba[l].rearrange("(kc p) n -> p kc n", p=128)[:, :, cs], w=[('wbr', (1, cbh))])
            S.dma('pool', wbr[:, 6:8, cs], w_bh[l].rearrange("(kc p) n -> p kc n", p=128)[:, :, cs], w=[('wbr', (2, cbh))])
        for cbh in range(2):
            cs = slice(cbh * 512, cbh * 512 + 512)
            for h in range(2):
                S.dma('pool', wo_sb[:, h * 4:h * 4 + 4, cs], w_out[l].rearrange("(kc p) n -> p kc n", p=128)[:, h * 4:h * 4 + 4, cs],
                      w=[('wo_sb', (h, cbh))])
        groups = ((0, 2), (2, 6), (6, 8))
        mblocks = [(bi, t0, n) for bi, (t0, n) in enumerate(BLOCKS) if not (t0 >= L and l == depth - 1 and depth == DEPTH)]

        def load_brb(k):
            bi_, t0_, n_ = mblocks[k]
            S.dma('sp', brb[bi_ % 2][:, :, 0:n_], BR.rearrange("c p t -> p c t")[:, :, t0_:t0_ + n_], r=['BR'], w=[f"brb{bi_ % 2}"])
        def branch_load(k, j):
            bi, t0, n = mblocks[k]
            gi = GTB.get()
            S.dma('sp', gtb[gi][:, :, 0:n], GT.rearrange("(g j) p t -> j p g t", g=3)[j, :, :, t0:t0 + n], r=['GT'], w=[f"gtb{gi}"])
            return gi

        def wout_load(k, j):
            bi, t0, n = mblocks[k]
            xi = XJ.get()
            S.dma('sp', xj[xi][:, 0:n], XT[j, :, t0:t0 + n], r=[('XT', (j, bi))], w=[f"xj{xi}"])
            return xi

        def branch_j(k, j, gi):
            bi, t0, n = mblocks[k]; b = bi % 2
            pks = []
            for gidx, (k0, k1) in enumerate(groups):
                pk = PSA.get() if gidx < 2 else PSQ.get()
                for kc in range(k0, k1):
                    mm(psb[pk][:, 0:n], wbr[:, kc, j * 128:(j + 1) * 128], brb[b][:, kc, 0:n], kc == k0, kc == k1 - 1,
                       [('wbr', (gidx, j // 4)), f"brb{b}"], [f"ps{pk}"])
                pks.append(pk)
            m0 = MT.get(); m1 = MT.get(); ci = C2.get()
            tt(mt[m0][:, 0:n], psb[pks[0]][:, 0:n], gtb[gi][:, 0, 0:n], ALU.mult, [f"ps{pks[0]}", f"gtb{gi}"], [f"mt{m0}"])
            tt(mt[m1][:, 0:n], psb[pks[1]][:, 0:n], gtb[gi][:, 1, 0:n], ALU.mult, [f"ps{pks[1]}", f"gtb{gi}"], [f"mt{m1}"])
            act(c2[ci][:, 0:n], psb[pks[2]][:, 0:n], AF.Copy, [f"ps{pks[2]}"], [f"c2_{ci}"])
            tt(c2[ci][:, 0:n], c2[ci][:, 0:n], gtb[gi][:, 2, 0:n], ALU.mult, [f"c2_{ci}", f"gtb{gi}"], [f"c2_{ci}"], eng='pool')
            tt(mt[m0][:, 0:n], mt[m0][:, 0:n], mt[m1][:, 0:n], ALU.add, [f"mt{m0}", f"mt{m1}"], [f"mt{m0}"])
            tt(yT[b][:, j, 0:n], mt[m0][:, 0:n], c2[ci][:, 0:n], ALU.add, [f"mt{m0}", f"c2_{ci}"], [(f"yT{b}", j)], eng='pool')

        def wout_j(k, j, xi):
            bi, t0, n = mblocks[k]; b = bi % 2
            wch = 0 if t0 < L else 1
            pk = PSQ.get()
            for kc in range(KC):
                mm(psb[pk][:, 0:n], wo_sb[:, kc, j * 128:(j + 1) * 128], yT[b][:, kc, 0:n], kc == 0, kc == KC - 1,
                   [('wo_sb', (kc // 4, j // 4)), (f"yT{b}", kc)], [f"ps{pk}"])
            stt(xj[xi][:, 0:n], psb[pk][:, 0:n], modcol(2, j, wch), xj[xi][:, 0:n], ALU.mult, ALU.add,
                [f"ps{pk}", modkey(), f"xj{xi}"], [f"xj{xi}"])
            S.dma('sp', XT[j, :, t0:t0 + n], xj[xi][:, 0:n], r=[f"xj{xi}"], w=[('XT', (j, bi))])

        load_brb(0)
        if len(mblocks) > 1: load_brb(1)
        steps = [('b', 0, j) for j in range(8)]
        for k in range(len(mblocks)):
            for j in range(8):
                if k + 1 < len(mblocks): steps.append(('b', k + 1, j))
                steps.append(('w', k, j))
        loaded = {}

        def issue_load(idx):
            kind, k, j = steps[idx]
            loaded[idx] = branch_load(k, j) if kind == 'b' else wout_load(k, j)
        nb = {'b': 0, 'w': 0}
        pend = []
        for idx in range(len(steps)):
            while pend and False: pass
            la = idx
            while la < len(steps) and la <= idx + 3:
                if la not in loaded:
                    kind = steps[la][0]
                    inflight = sum(1 for q in loaded if q >= idx and steps[q][0] == kind)
                    if inflight < 2: issue_load(la)
                    else: break
                la += 1
            kind, k, j = steps[idx]
            if kind == 'b' and j == 0 and k + 1 < len(mblocks) and k >= 1: load_brb(k + 1)
            if kind == 'b': branch_j(k, j, loaded[idx])
            else: wout_j(k, j, loaded[idx])
        S.barrier(); sc.close()

    SIL = Rot([0, 1])

    def ffn_phase(l, last):
        S.barrier(pool=True)
        sc = Scope(nc); sb = sc.sb
        w2_sb = sb("w2_sb", [128, FC, D], BF16)
        acb = [sb(f"acb{i}", [128, FC, 512], BF16) for i in range(2)]
        sil = [sb(f"sil{i}", [128, 512], F32) for i in range(2)]
        xj = [sb(f"xj{i}", [128, 512], F32) for i in range(2)]; XJ = Rot([0, 1])
        wsrc = w_f1[l].rearrange("(kc p) n -> p kc n", p=128)
        nblk = BLOCKS[:4] if last else BLOCKS
        w2_loads = [(h, cbh) for h in range(0, FC, 2) for cbh in range(2)]

        def w2_load(h, cbh):
            cs = slice(cbh * 512, cbh * 512 + 512)
            S.dma('pool', w2_sb[:, h:h + 2, cs], w_f2[l].rearrange("(kc p) n -> p kc n", p=128)[:, h:h + 2, cs], w=[('w2_sb', (h, cbh))])
        for wt in range(FC // 2):
            if wt >= 2:
                for _ in range(3):
                    if w2_loads: w2_load(*w2_loads.pop(0))
            i = Wrot.get()
            Wv = Wb[i][:, 0:8 * 512].rearrange("p (a b) -> p a b", a=8)
            S.dma('pool', Wv[:, :, 0:256], wsrc[:, :, wt * 256:wt * 256 + 256], w=[f"W{i}"])
            S.dma('pool', Wv[:, :, 256:512], wsrc[:, :, DFF + wt * 256:DFF + wt * 256 + 256], w=[f"W{i}"])
            for jj in range(2):
                fcx = wt * 2 + jj
                for bi, (t0, n) in enumerate(nblk):
                    pa = PSA.get(); pb_ = PSQ.get()
                    for kc in range(KC):
                        mm(psb[pa][:, 0:n], Wv[:, kc, jj * 128:(jj + 1) * 128], hT[:, kc, t0:t0 + n], kc == 0, kc == KC - 1,
                           [f"W{i}", ('hT', (kc, bi))], [f"ps{pa}"])
                    for kc in range(KC):
                        mm(psb[pb_][:, 0:n], Wv[:, kc, 256 + jj * 128:256 + (jj + 1) * 128], hT[:, kc, t0:t0 + n], kc == 0, kc == KC - 1,
                           [f"W{i}", ('hT', (kc, bi))], [f"ps{pb_}"])
                    si = SIL.get()
                    act(sil[si][:, 0:n], psb[pa][:, 0:n], AF.Silu, [f"ps{pa}"], [f"sil{si}"])
                    bi2 = STB.get()
                    tt(stb[bi2][:, 0:n], psb[pb_][:, 0:n], sil[si][:, 0:n], ALU.mult, [f"ps{pb_}", f"sil{si}"], [f"stb{bi2}"])
                    S.dma('sp', ACTS[fcx, :, t0:t0 + n], stb[bi2][:, 0:n], r=[f"stb{bi2}"], w=[('ACTS', (fcx, bi))])
        while w2_loads: w2_load(*w2_loads.pop(0))

        def load_acb(k):
            t0_, n_ = nblk[k]
            for hh in range(2):
                S.dma('sp', acb[k % 2][:, hh * 11:hh * 11 + 11, 0:n_], ACTS.rearrange("c p t -> p c t")[:, hh * 11:hh * 11 + 11, t0_:t0_ + n_],
                      r=['ACTS'], w=[(f"acb{k % 2}", hh)])
        load_acb(0)
        for bi, (t0, n) in enumerate(nblk):
            wch = 0 if t0 < L else 1
            b = bi % 2
            if bi + 1 < len(nblk): load_acb(bi + 1)
            def xload(j_):
                xi_ = XJ.get()
                S.dma('sp', xj[xi_][:, 0:n], XT[j_, :, t0:t0 + n], r=[('XT', (j_, bi))], w=[f"xj{xi_}"])
                return xi_
            xnext = xload(0)
            for j in range(8):
                xi = xnext
                if j + 1 < 8: xnext = xload(j + 1)
                pk = PSA.get()
                for kc in range(FC):
                    mm(psb[pk][:, 0:n], w2_sb[:, kc, j * 128:(j + 1) * 128], acb[b][:, kc, 0:n], kc == 0, kc == FC - 1,
                       ['w2_sb', (f"acb{b}", kc // 11)], [f"ps{pk}"])
                stt(xj[xi][:, 0:n], psb[pk][:, 0:n], modcol(5, j, wch), xj[xi][:, 0:n], ALU.mult, ALU.add,
                    [f"ps{pk}", modkey(), f"xj{xi}"], [f"xj{xi}"])
                S.dma('sp', XT[j, :, t0:t0 + n], xj[xi][:, 0:n], r=[f"xj{xi}"], w=[('XT', (j, bi))])
        S.barrier(); sc.close()

    def run_layers():
        for l in range(depth):
            last = (l == depth - 1) and depth == DEPTH
            CUR['l'] = l
            ada_next = adaln_steps(l + 1) if l + 1 < depth else None
            norm_phase(a1s[l % 2], f"a1_{l % 2}", 0)
            if stop_after == ('norm1', l): return
            asc = open_attn_scope()
            in_proj(l)
            S.barrier()
            if l == 0:
                if 'qT' in taps: tap('qT', A['qT'][:], [128, 4, T], BF16, ['qT'])
                if 'VE' in taps: tap('VE', A['VE'][:], [128, NT, 2, 192], BF16, ['VE'])
                if 'hT' in taps: tap('hT', hT[:], [128, KC, T], BF16, ['hT'])
                S.barrier()
            if stop_after == ('inproj', l):
                asc.close(); return
            attention(l, ada_next)
            if ada_next is not None:
                for _ in ada_next: pass
            S.barrier(); asc.close()
            if stop_after == ('attn', l): return
            pool_phase(l)
            if stop_after == ('pool', l): return
            hgrn_phase(l)
            if stop_after == ('hgrn', l): return
            merge_phase(l)
            if stop_after == ('merge', l): return
            norm_phase(a2s[l % 2], f"a2_{l % 2}", 3)
            ffn_phase(l, last)

    run_layers()
    if 'modT' in taps: tap('modT', modTs[0][:], [128, 48, 2], F32, ['modT0'])
    if 'cosT' in taps: tap('cosT', cosT[:], [128, L], F32, ['cosT'])
    if 'sinT' in taps: tap('sinT', sinT[:], [128, L], F32, ['sinT'])
    for nm, src in (('XT', XT), ('U', U), ('HQ', HQ), ('Z', Z), ('HGATE', HGATE), ('VH', VH), ('GT', GT), ('BR', BR), ('ACTS', ACTS)):
        if nm in taps:
            t_ = dram("tap_" + nm, list(src.shape), src.dtype, "ExternalOutput")
            tap_d[nm] = t_
            S.dma('sp', t_, src, r=[nm])

    scf = Scope(nc)
    xs = [scf.sb(f"xs{i}", [128, D], F32) for i in range(2)]
    xs2 = [scf.sb(f"xt2_{i}", [128, KC, 128], F32) for i in range(2)]
    def oload(i):
        S.dma('sp', xs2[i % 2][:], XTv[:, :, i * 128:(i + 1) * 128], r=['XT'], w=[f"xt2_{i % 2}"])
    oload(0)
    for i in range(16):
        b = i % 2
        for hh in range(2):
            pk = 2 * b + hh
            for q in range(4):
                kc = hh * 4 + q
                tr(psb[pk][:, q * 128:(q + 1) * 128], xs2[b][:, kc, :], ident_f[:], [f"xt2_{b}", 'ident_f'], [f"ps{pk}"])
            cp(xs[b][:, hh * 512:(hh + 1) * 512], psb[pk][:], [f"ps{pk}"], [f"xs{b}"])
        if i + 1 < 16: oload(i + 1)
        S.dma('sp', out_d[i * 128:(i + 1) * 128, :], xs[b][:], r=[f"xs{b}"], w=[('out', i)])
    S.final('sp')
    return nc, list(tap_d.keys())


_IN_NAMES = ["w_ada", "b_ada", "norm1_w", "w_in", "pool_w", "pool_scale", "q_norm_w", "k_norm_w", "hg_lb_logits", "hg_norm_w",
             "w_branch_pool", "w_branch_attn", "w_branch_hg", "w_out", "norm2_w", "w_ffn_in", "w_ffn_out"]


def make_in_maps(inputs):
    f = lambda a: np.ascontiguousarray(np.asarray(a, dtype=np.float32))
    shared = {k: f(inputs[k]) for k in _IN_NAMES}
    x = f(inputs["x"]); c = f(inputs["c"]); ctx = f(inputs["ctx"]); cc = f(inputs["c_ctx"]).reshape(8, 128)
    maps = []
    for b in range(8):
        m = dict(shared)
        m["x"] = x[b]; m["ctx"] = ctx[b]; m["c"] = c[b].reshape(8, 128); m["c_ctx"] = cc
        maps.append(m)
    return maps


def kernel(**inputs):
    nc, _ = build()
    res = run_bass_kernel_spmd(nc, make_in_maps(inputs), core_ids=list(range(8)))
    return np.stack([np.asarray(r["out"], dtype=np.float32) for r in res.results], axis=0)
```
